# Optimizing a Trainium2 kernel written in Bass

```python
import jax, jax.numpy as jnp
from jax import lax
import numpy as np

D_MODEL = 1024
BATCH = 2
SEQ = 8192
DEPTH = 1

HEAD_DIM = 64
N_Q_HEADS = 8
N_KV_HEADS = 2
Q_PER_KV = N_Q_HEADS // N_KV_HEADS
ATTN_WIDTH = N_Q_HEADS * HEAD_DIM
KV_WIDTH = N_KV_HEADS * HEAD_DIM
WINDOW = 128
ATTN_BLOCK = WINDOW
ROPE_THETA = 10000.0
CONV_WIDTH = D_MODEL - ATTN_WIDTH
CONV_K = 3
IN_PROJ_WIDTH = ATTN_WIDTH + 2 * KV_WIDTH + 3 * CONV_WIDTH
SPLITS = (ATTN_WIDTH, ATTN_WIDTH + KV_WIDTH, ATTN_WIDTH + 2 * KV_WIDTH,
          ATTN_WIDTH + 2 * KV_WIDTH + CONV_WIDTH, ATTN_WIDTH + 2 * KV_WIDTH + 2 * CONV_WIDTH)
PEER_HEADS = 8
PEER_KEYS = 128
PEER_EXPERTS = PEER_KEYS * PEER_KEYS
PEER_TOPK = 16
PEER_QDIM = 256
PEER_HALF = PEER_QDIM // 2
PEER_CHUNK = 128
N_MOD = 6
EPS = 1e-6
NEG = -1e30

kernel_name = "hymba_swa_sink_shortconv_peer_adaln"


def rmsnorm(x, g):
    x32 = x.astype(jnp.float32)
    y = x32 * lax.rsqrt(jnp.mean(x32 * x32, axis=-1, keepdims=True) + EPS)
    return (y * g.astype(jnp.float32)).astype(x.dtype)


def modulate(h, shift, scale):
    return h * (1.0 + scale[:, None, :]) + shift[:, None, :]


def rope(t, cos, sin):
    t1, t2 = jnp.split(t.astype(jnp.float32), 2, axis=-1)
    out = jnp.concatenate([t1 * cos - t2 * sin, t2 * cos + t1 * sin], axis=-1)
    return out.astype(t.dtype)


def sliding_window_attention(q, k, v, sinks):
    B, S = q.shape[0], q.shape[1]
    nb = S // ATTN_BLOCK
    qb = q.reshape(B, nb, ATTN_BLOCK, N_KV_HEADS, Q_PER_KV, HEAD_DIM)

    def band(t):
        tb = t.reshape(B, nb, ATTN_BLOCK, N_KV_HEADS, HEAD_DIM)
        prev = jnp.pad(tb, ((0, 0), (1, 0), (0, 0), (0, 0), (0, 0)))[:, :-1]
        return jnp.concatenate([prev, tb], axis=2)

    kb, vb = band(k), band(v)
    s = jnp.einsum('bnqhgd,bnkhd->bnhgqk', qb, kb).astype(jnp.float32) * (HEAD_DIM ** -0.5)
    qi = jnp.arange(ATTN_BLOCK)[:, None]
    kj = jnp.arange(2 * ATTN_BLOCK)[None, :]
    dist = qi + ATTN_BLOCK - kj
    blk = jnp.arange(nb)[:, None, None]
    valid = (dist >= 0) & (dist < WINDOW) & (blk * ATTN_BLOCK - ATTN_BLOCK + kj >= 0)
    s = jnp.where(valid[None, :, None, None], s, NEG)
    sink = sinks.astype(jnp.float32).reshape(1, 1, N_KV_HEADS, Q_PER_KV, 1, 1)
    m = jnp.maximum(jnp.max(s, axis=-1, keepdims=True), sink)
    e = jnp.exp(s - m)
    p = e / (jnp.sum(e, axis=-1, keepdims=True) + jnp.exp(sink - m))
    o = jnp.einsum('bnhgqk,bnkhd->bnqhgd', p.astype(v.dtype), vb)
    return o.reshape(B, S, ATTN_WIDTH)


def short_conv(h, w):
    S = h.shape[1]
    hp = jnp.pad(h, ((0, 0), (CONV_K - 1, 0), (0, 0)))
    y = w[0] * hp[:, 0:S]
    for k in range(1, CONV_K):
        y = y + w[k] * hp[:, k:k + S]
    return y


def peer(h, wq, subkeys, u_tab, v_tab):
    B, S, D = h.shape
    T = B * S
    xt = h.reshape(T, D)
    q = (xt @ wq).reshape(T, PEER_HEADS, 2, PEER_HALF)
    sc = jnp.einsum('thpd,hpkd->thpk', q, subkeys).astype(jnp.float32)
    vals, ids = lax.top_k(sc, PEER_TOPK)
    cand = (vals[:, :, 0, :, None] + vals[:, :, 1, None, :]).reshape(T, PEER_HEADS, PEER_TOPK * PEER_TOPK)
    cid = (ids[:, :, 0, :, None] * PEER_KEYS + ids[:, :, 1, None, :]).reshape(T, PEER_HEADS, PEER_TOPK * PEER_TOPK)
    top, pos = lax.top_k(cand, PEER_TOPK)
    idx = jnp.take_along_axis(cid, pos, axis=-1)
    g = jax.nn.softmax(top, axis=-1).astype(h.dtype)
    nc = T // PEER_CHUNK

    def expert_chunk(args):
        xc, ic, gc = args
        u = jnp.take(u_tab, ic, axis=0)
        a = jnp.einsum('chkd,cd->chk', u, xc)
        wgt = gc * jax.nn.gelu(a)
        vv = jnp.take(v_tab, ic, axis=0)
        return jnp.einsum('chk,chkd->cd', wgt, vv)

    out = lax.map(expert_chunk, (xt.reshape(nc, PEER_CHUNK, D),
                                 idx.reshape(nc, PEER_CHUNK, PEER_HEADS, PEER_TOPK),
                                 g.reshape(nc, PEER_CHUNK, PEER_HEADS, PEER_TOPK)))
    return out.reshape(B, S, D)


def setup_inputs(seed: int = 0) -> dict:
    key = jax.random.key(seed)
    ks = jax.random.split(key, 20)
    f32 = jnp.float32
    D = D_MODEL

    def nrm(k, shape, scale):
        return jax.random.normal(k, shape, f32) * scale

    return {
        "x": nrm(ks[0], (BATCH, SEQ, D), 1.0),
        "c": nrm(ks[1], (BATCH, D), 1.0),
        "positions": jnp.broadcast_to(jnp.arange(SEQ, dtype=jnp.int32)[None, :], (BATCH, SEQ)),
        "ada_w": nrm(ks[2], (DEPTH, D, N_MOD * D), 0.5 * D ** -0.5),
        "ada_b": nrm(ks[3], (DEPTH, N_MOD * D), 0.02),
        "norm1_g": 1.0 + nrm(ks[4], (DEPTH, D), 0.02),
        "w_in": nrm(ks[5], (DEPTH, D, IN_PROJ_WIDTH), D ** -0.5),
        "q_norm_g": 1.0 + nrm(ks[6], (DEPTH, HEAD_DIM), 0.02),
        "k_norm_g": 1.0 + nrm(ks[7], (DEPTH, HEAD_DIM), 0.02),
        "sinks": nrm(ks[8], (DEPTH, N_Q_HEADS), 0.5),
        "conv_w": nrm(ks[9], (DEPTH, CONV_K, CONV_WIDTH), CONV_K ** -0.5),
        "attn_out_g": 1.0 + nrm(ks[10], (DEPTH, ATTN_WIDTH), 0.02),
        "conv_out_g": 1.0 + nrm(ks[11], (DEPTH, CONV_WIDTH), 0.02),
        "w_out": nrm(ks[12], (DEPTH, D, D), D ** -0.5),
        "norm2_g": 1.0 + nrm(ks[13], (DEPTH, D), 0.02),
        "peer_wq": nrm(ks[14], (DEPTH, D, PEER_HEADS * PEER_QDIM), D ** -0.5),
        "peer_subkeys": nrm(ks[15], (DEPTH, PEER_HEADS, 2, PEER_KEYS, PEER_HALF), PEER_HALF ** -0.5),
        "peer_u": nrm(ks[16], (DEPTH, PEER_EXPERTS, D), D ** -0.5),
        "peer_v": nrm(ks[17], (DEPTH, PEER_EXPERTS, D), (PEER_HEADS * PEER_TOPK) ** -0.5),
    }


def reference(x, c, positions, ada_w, ada_b, norm1_g, w_in, q_norm_g, k_norm_g, sinks, conv_w,
              attn_out_g, conv_out_g, w_out, norm2_g, peer_wq, peer_subkeys, peer_u, peer_v):
    B, S, _ = x.shape
    inv_freq = ROPE_THETA ** (-jnp.arange(0, HEAD_DIM, 2, dtype=jnp.float32) / HEAD_DIM)
    ang = positions.astype(jnp.float32)[..., None] * inv_freq
    cos = jnp.cos(ang)[:, :, None, :]
    sin = jnp.sin(ang)[:, :, None, :]
    c_act = jax.nn.silu(c)

    for l in range(DEPTH):
        mod = c_act @ ada_w[l] + ada_b[l]
        shift1, scale1, gate1, shift2, scale2, gate2 = jnp.split(mod, N_MOD, axis=-1)

        h = modulate(rmsnorm(x, norm1_g[l]), shift1, scale1)
        proj = h @ w_in[l]
        q, k, v, cb, cc, cu = jnp.split(proj, SPLITS, axis=-1)

        q = rope(rmsnorm(q.reshape(B, S, N_Q_HEADS, HEAD_DIM), q_norm_g[l]), cos, sin)
        k = rope(rmsnorm(k.reshape(B, S, N_KV_HEADS, HEAD_DIM), k_norm_g[l]), cos, sin)
        v = v.reshape(B, S, N_KV_HEADS, HEAD_DIM)
        attn = sliding_window_attention(q, k, v, sinks[l])

        conv = cb * short_conv(cc * cu, conv_w[l])

        merged = jnp.concatenate([rmsnorm(attn, attn_out_g[l]), rmsnorm(conv, conv_out_g[l])], axis=-1)
        x = x + gate1[:, None, :] * (merged @ w_out[l])

        h2 = modulate(rmsnorm(x, norm2_g[l]), shift2, scale2)
        x = x + gate2[:, None, :] * peer(h2, peer_wq[l], peer_subkeys[l], peer_u[l], peer_v[l])
    return x
```

```python
import math
import os
from contextlib import ExitStack

import numpy as np
import concourse.bass as bass
import concourse.mybir as mybir
from concourse.bass_utils import run_bass_kernel_spmd

F32 = mybir.dt.float32
BF16 = mybir.dt.bfloat16
I32 = mybir.dt.int32
U32 = mybir.dt.uint32
ALU = mybir.AluOpType
AF = mybir.ActivationFunctionType
AX = mybir.AxisListType

N_CORES = 8
D = 1024
NT = 16
TOK = NT * 128
NTE = NT + 1
EPS = 1e-6
NR = 11
ENGS = ["sync", "scalar", "vector", "gpsimd", "tensor"]
DEBUG = False
N_TILES_RUN = NT
STAGE = 3
NEXP = 16384
BF16_TABLE = True


class Region:
    _all = {}

    def __init__(self, phys, lo, hi):
        self.phys, self.lo, self.hi = phys, lo, hi
        self.w = None
        self.r = []
        Region._all.setdefault(phys, []).append(self)
        self._ov = None

    def ov(self):
        return [o for o in Region._all[self.phys] if o.lo < self.hi and self.lo < o.hi]


class Op:
    __slots__ = ("eng", "fn", "deps", "is_dma", "dsem", "dval", "needs_inc", "seq", "waits")

    def __init__(self, eng, fn, deps, is_dma=False, dsem=None, dval=0):
        self.eng, self.fn, self.deps = eng, fn, deps
        self.is_dma, self.dsem, self.dval = is_dma, dsem, dval
        self.needs_inc = False
        self.seq = 0
        self.waits = []


class Prog:
    def __init__(self, sems):
        self.sems = sems
        self.ops = []
        self.dcnt = {}

    def _deps(self, reads, writes, extra):
        deps = {}
        for r in reads:
            for o in r.ov():
                if o.w is not None:
                    deps[o.w] = "raw"
        for w in writes:
            for o in w.ov():
                if o.w is not None and o.w not in deps:
                    deps[o.w] = "war"
                for rd in o.r:
                    if rd not in deps:
                        deps[rd] = "war"
        for e in extra:
            if e is not None:
                deps[e] = "raw"
        return deps

    def _commit(self, op, reads, writes):
        for r in reads:
            r.r.append(op)
        for w in writes:
            w.w = op
            w.r = []

    def op(self, eng, fn, reads=(), writes=(), extra=()):
        op = Op(eng, fn, self._deps(reads, writes, extra))
        self._commit(op, reads, writes)
        self.ops.append(op)
        return op

    def dma(self, eng, fn, sem, reads=(), writes=(), extra=()):
        self.dcnt[sem] = self.dcnt.get(sem, 0) + 16
        op = Op(eng, fn, self._deps(reads, writes, extra), True, sem, self.dcnt[sem])
        self._commit(op, reads, writes)
        self.ops.append(op)
        return op

    def finalize(self):
        for c in self.ops:
            for d, kind in c.deps.items():
                if d.is_dma:
                    c.waits.append(d)
                    continue
                if d.eng == c.eng and not c.is_dma:
                    if c.eng == "tensor":
                        continue
                    if kind != "raw":
                        continue
                d.needs_inc = True
                c.waits.append(d)
        cnt = {e: 0 for e in ENGS}
        for o in self.ops:
            if o.needs_inc and not o.is_dma:
                cnt[o.eng] += 1
                o.seq = cnt[o.eng]
        self.final_cnt = cnt

    def emit(self, block, tail_waits):
        self.finalize()
        sems = self.sems
        per = {e: [o for o in self.ops if o.eng == e] for e in ENGS}

        def run_engine(e, name):
            waited = {}
            for o in per[name]:
                for d in o.waits:
                    if d.is_dma:
                        s, v = d.dsem, d.dval
                    else:
                        s, v = sems[d.eng], d.seq
                    k = id(s)
                    if waited.get(k, 0) >= v:
                        continue
                    waited[k] = v
                    e.wait_ge(s, v)
                ins = o.fn(e)
                if o.is_dma:
                    ins.then_inc(o.dsem, 16)
                elif o.needs_inc:
                    ins.then_inc(sems[o.eng], 1)
            if name == "sync":
                for (s, v) in tail_waits():
                    e.wait_ge(s, v)

        @block.sync
        def _(e):
            run_engine(e, "sync")

        @block.scalar
        def _(e):
            run_engine(e, "scalar")

        @block.vector
        def _(e):
            run_engine(e, "vector")

        @block.gpsimd
        def _(e):
            run_engine(e, "gpsimd")

        @block.tensor
        def _(e):
            run_engine(e, "tensor")


def _split_2pi():
    two_pi = 2.0 * math.pi
    c1 = 6.28125
    r = two_pi - c1
    c2 = float(np.float32(r))
    m, e = math.frexp(c2)
    c2 = math.ldexp(round(m * 2048) / 2048, e)
    c3 = float(np.float32(two_pi - c1 - c2))
    return c1, c2, c3


def build_nc():
    Region._all = {}
    nc = bass.Bass("TRN2", target_bir_lowering=False)

    def din(name, shape, dt=F32):
        return nc.dram_tensor(name, list(shape), dt, kind="ExternalInput").ap()

    xh = din("xh", [NTE * 128, D])
    posi_d = din("posi", [128, NTE], I32)
    ccol_d = din("ccol", [128, 8])
    ada_w = din("ada_w", [D, 6 * D])
    ada_b = din("ada_b", [1, 6 * D])
    crow = din("crow", [1, 3584])
    convw_d = din("convw", [128, 4, 3])
    convg_d = din("convg", [128, 4])
    hflag_d = din("hflag", [128, 1])
    w_in = din("w_in", [D, 2304])
    w_out = din("w_out", [D, D])
    wq = din("wq", [D, 2048])
    skT_d = din("skT", [128, 16, 128])
    peer_uv = din("peer_uv", [NEXP, 2 * D])
    uvb = nc.dram_tensor("uvb", [NEXP, 2 * D], BF16, kind="Internal").ap() if BF16_TABLE else None
    wqb = nc.dram_tensor("wqb", [D, 2048], BF16, kind="Internal").ap()
    ident_d = din("ident", [128, 128])
    masks_d = din("masks", [128, 3, 128])
    out = nc.dram_tensor("out", [TOK, D], F32, kind="ExternalOutput").ap()
    dbg = {}
    if DEBUG:
        dbg["x1"] = nc.dram_tensor("dbg_x1", [TOK, D], F32, kind="ExternalOutput").ap()
        dbg["h2"] = nc.dram_tensor("dbg_h2", [TOK, D], F32, kind="ExternalOutput").ap()
        dbg["idx"] = nc.dram_tensor("dbg_idx", [TOK, 128], I32, kind="ExternalOutput").ap()
        dbg["g"] = nc.dram_tensor("dbg_g", [TOK, 128], F32, kind="ExternalOutput").ap()
        dbg["attn"] = nc.dram_tensor("dbg_attn", [TOK, 512], F32, kind="ExternalOutput").ap()
        dbg["mod"] = nc.dram_tensor("dbg_mod", [128, 6, D], F32, kind="ExternalOutput").ap()
        dbg["cs"] = nc.dram_tensor("dbg_cs", [128, 2, NTE, 32], F32, kind="ExternalOutput").ap()

    with ExitStack() as es:
        E = es.enter_context

        def sb(name, shape, dt=F32):
            return E(nc.sbuf_tensor("sb_" + name, list(shape), dt))

        ident_f = sb("ident_f", [128, 128])
        ident_b = sb("ident_b", [128, 128], BF16)
        ones_f = sb("ones_f", [128, 128])
        masks_b = sb("masks_b", [128, 3, 128], BF16)
        ccol = sb("ccol", [128, 8])
        cact = sb("cact", [128, 8])
        posi = sb("posi", [128, NTE], I32)
        posf = sb("posf", [128, NTE])
        invf = sb("invf", [128, 32])
        cos_t = sb("cos_t", [128, NTE, 32])
        sin_t = sb("sin_t", [128, NTE, 32])
        qkg = sb("qkg", [128, 640])
        attn_g = sb("attn_g", [128, 512])
        sinkb = sb("sinkb", [128, 8])
        esink = sb("esink", [128, 8])
        convw = sb("convw", [128, 4, 3])
        convg = sb("convg", [128, 4])
        hflag = sb("hflag", [128, 1])
        cst = sb("cst", [128, 4])
        iota16 = sb("iota16", [128, 16])
        mod6 = sb("mod6", [128, 6, D])
        w_in_sb = sb("w_in_sb", [128, 8, 2304], BF16)
        w_out_sb = sb("w_out_sb", [128, 8, D], BF16)
        accv = sb("accv", [128, D])
        wqc = sb("wqc", [128, 2, 8, 512], BF16)
        skT_sb = sb("skT_sb", [128, 16, 128], BF16)
        ring = sb("ring", [128, NR, D])
        BX2 = sb("BX2", [128, 2, D])
        BH2 = sb("BH2", [128, 2, D], BF16)
        BHT = sb("BHT", [128, 8, 128], BF16)
        S8A = sb("S8A", [128, 2048])
        S8B = sb("S8B", [128, 2048])
        S4A = sb("S4A", [128, 2048], BF16)
        S4B = sb("S4B", [128, 2048], BF16)
        BY = sb("BY", [128, D])
        dg = sb("dg", [128, 4, 128], BF16)
        attnT = sb("attnT", [128, 4, 128], BF16)
        kT = sb("kT", [64, 2, 2, 128], BF16)
        vaug = sb("vaug", [128, 2, 2, 65], BF16)
        pbuf = sb("pbuf", [128, 4, 130])
        st = sb("st", [128, 64])
        top = sb("top", [128, 128])
        pos = sb("pos", [128, 128], U32)
        idxi2 = sb("idxi2", [128, 2, 128], I32)
        gsm2 = sb("gsm2", [128, 2, 128])
        gav = sb("gav", [128, 128])
        esm = sb("esm", [128, 128])
        av = sb("av", [128, 128])
        PS = E(nc.psum_tensor("PS", [128, 4096], F32))

        sems = {k: E(nc.semaphore("s_" + k)) for k in ENGS}
        sem_c = E(nc.semaphore("d_c"))
        sem_w = [E(nc.semaphore("d_w%d" % i)) for i in range(4)]
        sem_ada = [E(nc.semaphore("d_ada%d" % i)) for i in range(2)]
        sem_x = E(nc.semaphore("d_x"))
        sem_o = E(nc.semaphore("d_o"))
        sem_dbg = E(nc.semaphore("d_dbg"))
        sem_ring = [E(nc.semaphore("d_r%d" % i)) for i in range(NR)]
        sem_cvl = [E(nc.semaphore("d_cvl%d" % i)) for i in range(2)]
        sem_wqc = [E(nc.semaphore("d_wqc%d" % i)) for i in range(2)]
        sem_cvs = [E(nc.semaphore("d_cvs%d" % i)) for i in range(2)]
        block = E(nc.Block())
        P = Prog(sems)

        def R(phys, lo=0, hi=1 << 30):
            return Region(phys, lo, hi)

        rg = {}
        for nm in ["ident_f", "ident_b", "ones_f", "masks_b", "ccol", "cact", "posi", "posf", "invf",
                   "cos_t", "sin_t", "qkg", "attn_g", "sinkb", "esink", "convw", "convg", "hflag", "cst", "iota16",
                   "w_in", "w_out", "skT", "BHT", "BY", "attnT", "pbuf",
                   "top", "pos",
                   "esm"]:
            rg[nm] = R(nm)
        rg_mod = [R("mod6", 4096 * k, 4096 * (k + 1)) for k in range(6)]
        rg_ring = [R("ring", 4096 * k, 4096 * (k + 1)) for k in range(NR)]
        rg_adab = [rg["BY"], rg["BY"]]
        adab = BY[0:1, 0:512].rearrange("p (s n) -> p s n", n=256)
        rg_BX = [R("BX2", 4096 * k, 4096 * (k + 1)) for k in range(2)]
        rg_idxi = [R("idxi2", 512 * k, 512 * (k + 1)) for k in range(2)]
        rg_gsm = [R("gsm2", 512 * k, 512 * (k + 1)) for k in range(2)]
        rg_BH = [R("BH2", 2048 * k, 2048 * (k + 1)) for k in range(2)]
        rg_dg = [R("dg", 256 * k, 256 * (k + 1)) for k in range(4)]
        rg_uvb = R("uvb")
        rg_uvb2 = R("uvb2")
        rg_wqb, rg_wqb2 = R("wqb"), R("wqb2")
        rg_accv = R("accv")
        rg_wqc = [R("wqc", 8192 * k, 8192 * (k + 1)) for k in range(2)]
        rg_av = [R("av", 4 * k, 4 * k + 4) for k in range(128)]
        rg_gav = [R("gav", 4 * k, 4 * k + 4) for k in range(128)]
        wv = ident_f
        rg_wv = [R("ident_f", 4 * k, 4 * k + 4) for k in range(128)]
        ring_bf = ring[:].bitcast(BF16)
        rg_kT = [R("kT", 512 * k, 512 * (k + 1)) for k in range(2)]
        rg_v = [R("vaug", 260 * k, 260 * (k + 1)) for k in range(2)]
        rg_st = {}

        def ST(name, lo, hi):
            rg_st[name] = (R("st", 4 * lo, 4 * hi), st[:, lo:hi])
            return rg_st[name]

        masks_f = S8A[:, 1600:1984].rearrange("p (m t) -> p m t", t=128)
        rg["masks_f"] = R("S8A", 6400, 7936)
        qk_v, sq_v, tmp_v = S8A[:, 0:640], S8A[:, 640:1280], S8A[:, 1280:1920]
        rg_qk, rg_sq, rg_tmp = R("S8A", 0, 2560), R("S8A", 2560, 5120), R("S8A", 5120, 7680)
        sc_v, rg_sc = S8A[:, 0:2048], R("S8A", 0, 8192)
        k1u, rg_k1u = S8A[:, 0:128].bitcast(U32), R("S8A", 0, 512)
        k2u, rg_k2u = S8A[:, 128:256].bitcast(U32), R("S8A", 512, 1024)
        k1f, rg_k1f = S8A[:, 256:384], R("S8A", 1024, 1536)
        k2f, rg_k2f = S8A[:, 384:512], R("S8A", 1536, 2048)
        isel, rg_isel = S8A[:, 512:768].rearrange("p (s n) -> p s n", n=128), R("S8A", 2048, 3072)
        idxf, rg_idxf = S8A[:, 768:896], R("S8A", 3072, 3584)
        attn_v, cu_v, c1_v, cv_v = S8B[:, 0:512], S8B[:, 512:1024], S8B[:, 1024:1536], S8B[:, 1536:2048]
        rg_attn, rg_cu, rg_c1, rg_cv = R("S8B", 0, 2048), R("S8B", 2048, 4096), R("S8B", 4096, 6144), R("S8B", 6144, 8192)
        cand_v, rg_cand = S8B[:, 0:2048], R("S8B", 0, 8192)
        rg_fin = R("S8B", 0, 4096)
        qr_v, attn_n_v, cvT_v = S4A[:, 0:640], S4A[:, 640:1152], S4A[:, 1152:1664]
        rg_qr, rg_attn_n, rg_cvT = R("S4A", 0, 1280), R("S4A", 1280, 2304), R("S4A", 2304, 3328)
        vals = S4A[:, 0:512].bitcast(F32)
        ids = S4A[:, 512:1024].bitcast(U32)
        idsf = S4A[:, 1024:1536].bitcast(F32)
        mr = S4A[:, 1536:2048].bitcast(F32)
        rg["vals"], rg["ids"], rg["idsf"], rg["mr"] = R("S4A", 0, 1024), R("S4A", 1024, 2048), R("S4A", 2048, 3072), R("S4A", 3072, 4096)
        qT_v = S4B[0:64, 0:1024].rearrange("p (h t) -> p h t", t=128)
        pT_v = [S4B[:, 1024:1536], S4B[:, 1536:2048]]
        rg_qT, rg_pT = R("S4B", 0, 2048), [R("S4B", 2048, 3072), R("S4B", 3072, 4096)]
        qpT_v, rg_qpT = S4B[:, 0:2048].rearrange("p (b t) -> p b t", t=128), R("S4B", 0, 4096)
        oh_v, rg_oh = S4B[:, 0:2048], R("S4B", 0, 4096)
        junk_v, rg_junk = S4B[:, 0:1024], R("S4B", 0, 2048)
        rg_ps = [R("PS", 2048 * b, 2048 * (b + 1)) for b in range(8)]

        def bank(b, n=512, off=0):
            return PS[:, 512 * b + off:512 * b + off + n]

        def bank_bf(b):
            return PS[:, 512 * b:512 * (b + 1)].bitcast(BF16)

        grp_c, grp_w = [], [[], [], [], []]

        def ld(dst, src, regs, sem=sem_c, eng="sync"):
            o = P.dma(eng, lambda e, dst=dst, src=src: e.dma_start(out=dst, in_=src), sem, writes=regs)
            grp_c.append(o)
            return o

        ld(ident_f[:], ident_d, [rg["ident_f"]])
        ld(masks_f, masks_d, [rg["masks_f"]])
        ld(ccol[:], ccol_d, [rg["ccol"]])
        ld(posi[:], posi_d, [rg["posi"]])
        ld(convw[:], convw_d, [rg["convw"]])
        ld(convg[:], convg_d, [rg["convg"]])
        ld(hflag[:], hflag_d, [rg["hflag"]])
        if 'b' not in os.environ.get('KSKIP', ''):
          ld(mod6[:, 4, :], crow[:, 0:1024].partition_broadcast(128), [rg_mod[4]])
          ld(mod6[:, 5, :], crow[:, 1024:2048].partition_broadcast(128), [rg_mod[5]])
          ld(attn_g[:], crow[:, 2048:2560].partition_broadcast(128), [rg["attn_g"]])
          ld(qkg[:], crow[:, 2560:3200].partition_broadcast(128), [rg["qkg"]])
          ld(sinkb[:], crow[:, 3200:3208].partition_broadcast(128), [rg["sinkb"]])
          ld(invf[:], crow[:, 3208:3240].partition_broadcast(128), [rg["invf"]])

        SKIP = os.environ.get('KSKIP', '')
        for kt in range(8 if 'w' not in SKIP else 0):
            for c in range(2):
                grp_w[0].append(P.dma("gpsimd", lambda e, kt=kt, c=c: e.dma_start(out=w_in_sb[:, kt, 1152 * c:1152 * (c + 1)],
                                                                   in_=w_in[128 * kt:128 * (kt + 1), 1152 * c:1152 * (c + 1)]),
                      sem_w[0], writes=[]))
        for kt in range(8 if ('w' not in SKIP and '1' not in SKIP) else 0):
            grp_w[1].append(P.dma("gpsimd", lambda e, kt=kt: e.dma_start(out=w_out_sb[:, kt, :], in_=w_out[128 * kt:128 * (kt + 1), :]),
                  sem_w[1], writes=[]))
        for b4 in range(4 if ('w' not in SKIP and '3' not in SKIP) else 0):
            grp_w[3].append(P.dma("gpsimd", lambda e, b4=b4: e.dma_start(out=skT_sb[:, 4 * b4:4 * b4 + 4, :], in_=skT_d[:, 4 * b4:4 * b4 + 4, :]),
                  sem_w[3], writes=[]))
        for o in grp_c:
            o.dval = P.dcnt[sem_c]
        for k, nm in enumerate(["w_in", "w_out", None, "skT"]):
            for o in grp_w[k]:
                o.dval = P.dcnt.get(sem_w[k], 0)
            if grp_w[k] and nm:
                rg[nm].w = grp_w[k][-1]

        V = lambda fn, reads=(), writes=(), extra=(): P.op("vector", fn, reads, writes, extra)
        A = lambda fn, reads=(), writes=(), extra=(): P.op("scalar", fn, reads, writes, extra)
        T = lambda fn, reads=(), writes=(), extra=(): P.op("tensor", fn, reads, writes, extra)
        G = lambda fn, reads=(), writes=(), extra=(): P.op("gpsimd", fn, reads, writes, extra)

        V(lambda e: e.tensor_copy(out=ident_b[:], in_=ident_f[:]), [rg["ident_f"]], [rg["ident_b"]])
        V(lambda e: e.tensor_copy(out=masks_b[:], in_=masks_f), [rg["masks_f"]], [rg["masks_b"]])
        V(lambda e: e.memset(ones_f[:], 1.0), [], [rg["ones_f"]])
        V(lambda e: e.memset(vaug[:], 1.0), [], rg_v)
        V(lambda e: e.memset(pbuf[:], 0.0), [], [rg["pbuf"]])
        rg_cst = R("cst")
        V(lambda e: e.memset(cst[:, 0:1], EPS), [], [rg_cst])
        V(lambda e: e.memset(cst[:, 1:2], math.pi / 2), [], [rg_cst])
        if 'i' not in SKIP:
          G(lambda e: e.iota(iota16[:], pattern=[[1, 16]], base=0, channel_multiplier=0, allow_small_or_imprecise_dtypes=True),
            [], [rg["iota16"]])

        if BF16_TABLE:
            stg = [ring_bf[:, 4 + 2 * k:6 + 2 * k, :] for k in range(2)]
            rg_stg = [R("ring", 16384 + 8192 * k, 16384 + 8192 * (k + 1)) for k in range(2)]
            cv_stores = []
            NCH = NEXP // 256

            NWQ = D // 256
            def cv_src_dst(c):
                if c < NWQ:
                    return wq[256 * c:256 * (c + 1), :], wqb[256 * c:256 * (c + 1), :]
                c2 = c - NWQ
                return peer_uv[256 * c2:256 * (c2 + 1), :], uvb[256 * c2:256 * (c2 + 1), :]

            def cv_load(c):
                src = cv_src_dst(c)[0].rearrange("(p j) n -> p j n", j=2)
                P.dma("gpsimd", lambda e, c=c, src=src: e.dma_start(out=stg[c % 2], in_=src), sem_cvl[c % 2], writes=[rg_stg[c % 2]])

            def cv_store(c):
                dst = cv_src_dst(c)[1].rearrange("(p j) n -> p j n", j=2)
                cv_stores.append(P.dma("gpsimd", lambda e, c=c, dst=dst: e.dma_start(out=dst, in_=stg[c % 2]), sem_cvs[c % 2], reads=[rg_stg[c % 2]]))

            NCH = NCH + NWQ
            cv_load(0)
            for c in range(1, NCH):
                cv_load(c)
                cv_store(c - 1)
            cv_store(NCH - 1)
            rg_wqb.w = cv_stores[NWQ - 1]
            rg_wqb2.w = cv_stores[NWQ - 2]
            rg_uvb.w = cv_stores[-1]
            rg_uvb2.w = cv_stores[-2]
        V(lambda e: e.tensor_copy(out=posf[:], in_=posi[:]), [rg["posi"]], [rg["posf"]])
        A(lambda e: e.activation(out=cact[:], in_=ccol[:], func=AF.Silu), [rg["ccol"]], [rg["cact"]])
        A(lambda e: e.activation(out=esink[:], in_=sinkb[:], func=AF.Exp), [rg["sinkb"]], [rg["esink"]])
        V(lambda e: e.tensor_scalar(out=qkg[:, 0:512], in0=qkg[:, 0:512], scalar1=0.125, scalar2=None, op0=ALU.mult),
          [rg["qkg"]], [rg["qkg"]])
        cact_rep = BX2[:, 0, :].rearrange("p (k m) -> p k m", m=128)
        V(lambda e: e.tensor_copy(out=cact_rep, in_=cact[:].unsqueeze(2).to_broadcast([128, 8, 128])),
          [rg["cact"]], [rg_BX[0]])

        C1, C2, C3 = _split_2pi()
        MAGIC = 12582912.0
        PI_LO = 3.1415925
        ang = S8A[:, 0:NTE * 32].rearrange("p (i j) -> p i j", j=32)
        nn = S8A[:, 1024:1024 + NTE * 32].rearrange("p (i j) -> p i j", j=32)
        rg_ang, rg_nn = R("S8A", 0, 4096), R("S8A", 4096, 8192)
        V(lambda e: e.tensor_tensor(out=ang, in0=posf[:].unsqueeze(2).to_broadcast([128, NTE, 32]),
                                    in1=invf[:].unsqueeze(1).to_broadcast([128, NTE, 32]), op=ALU.mult),
          [rg["posf"], rg["invf"]], [rg_ang])
        V(lambda e: e.tensor_scalar(out=nn, in0=ang, scalar1=1.0 / (2 * math.pi), scalar2=MAGIC, op0=ALU.mult, op1=ALU.add),
          [rg_ang], [rg_nn])
        V(lambda e: e.tensor_scalar(out=nn, in0=nn, scalar1=MAGIC, scalar2=None, op0=ALU.subtract), [rg_nn], [rg_nn])
        for cc in (C1, C2, C3):
            V(lambda e, cc=cc: e.scalar_tensor_tensor(out=ang, in0=nn, scalar=-cc, in1=ang, op0=ALU.mult, op1=ALU.add),
              [rg_nn, rg_ang], [rg_ang])
        V(lambda e: e.tensor_scalar(out=ang, in0=ang, scalar1=PI_LO, scalar2=-PI_LO, op0=ALU.min, op1=ALU.max), [rg_ang], [rg_ang])
        A(lambda e: e.activation(out=sin_t[:], in_=ang, func=AF.Sin), [rg_ang], [rg["sin_t"]])
        A(lambda e: e.activation(out=nn, in_=ang, func=AF.Abs), [rg_ang], [rg_nn])
        A(lambda e: e.activation(out=cos_t[:], in_=nn, func=AF.Sin, scale=-1.0, bias=cst[:, 1:2]), [rg_nn, rg_cst], [rg["cos_t"]])

        ada_r = ada_w.rearrange("(k p) n -> p k n", p=128)
        chunkbuf = [ring[:, 0:2, :].rearrange("p s (k n) -> p (s k) n", n=256) if False else None, None]
        chunkbuf = [ring[:, 2 * s:2 * s + 2, :].rearrange("p s (k n) -> p (s k) n", n=256) for s in range(2)]
        rg_chunk = [R("ring", 8192 * s, 8192 * (s + 1)) for s in range(2)]
        for c in range(24 if 'a' not in SKIP else 0):
            s = c % 2
            o1 = P.dma("sync", lambda e, c=c, s=s: e.dma_start(out=chunkbuf[s], in_=ada_r[:, :, 256 * c:256 * (c + 1)]),
                       sem_ada[s], writes=[rg_chunk[s]])
            o2 = P.dma("sync", lambda e, c=c, s=s: e.dma_start(out=adab[0:1, s, :], in_=ada_b[:, 256 * c:256 * (c + 1)]),
                       sem_ada[s], writes=[rg_adab[s]])
            o1.dval = o2.dval
            pb = bank(s, 256)
            for kt in range(8):
                T(lambda e, kt=kt, s=s, pb=pb: e.matmul(pb, lhsT=cact_rep[:, kt, :], rhs=chunkbuf[s][:, kt, :], start=(kt == 0), stop=False),
                  [rg_BX[0], rg_chunk[s]], [rg_ps[s]])
            T(lambda e, s=s, pb=pb: e.matmul(pb, lhsT=ones_f[0:1, :], rhs=adab[0:1, s, :], start=False, stop=True),
              [rg["ones_f"], rg_adab[s]], [rg_ps[s]])
            sec, off = c // 4, (c % 4) * 256
            if sec in (1, 4):
                k = 4 if sec == 1 else 5
                V(lambda e, k=k, off=off, pb=pb: e.scalar_tensor_tensor(out=mod6[:, k, off:off + 256], in0=pb, scalar=1.0,
                                                                         in1=mod6[:, k, off:off + 256], op0=ALU.add, op1=ALU.mult),
                  [rg_ps[s], rg_mod[k]], [rg_mod[k]])
            else:
                k = {0: 0, 2: 1, 3: 2, 5: 3}[sec]
                A(lambda e, k=k, off=off, pb=pb: e.copy(out=mod6[:, k, off:off + 256], in_=pb), [rg_ps[s]], [rg_mod[k]])
        shift1, gate1, shift2, gate2, A1, A2 = [mod6[:, k, :] for k in range(6)]
        rg_shift1, rg_gate1, rg_shift2, rg_gate2, rg_A1, rg_A2 = rg_mod
        setup_done = V(lambda e: e.memset(st[:, 60:61], 0.0), [rg_mod[k] for k in range(6)], [R("st", 240, 244)])


        Region._all["ring"] = list(rg_ring)
        if DEBUG:
            P.dma("sync", lambda e: e.dma_start(out=dbg["mod"], in_=mod6[:]), sem_dbg, reads=rg_mod)
            P.dma("sync", lambda e: e.dma_start(out=dbg["cs"][:, 0], in_=cos_t[:]), sem_dbg, reads=[rg["cos_t"]])
            P.dma("sync", lambda e: e.dma_start(out=dbg["cs"][:, 1], in_=sin_t[:]), sem_dbg, reads=[rg["sin_t"]])

        s_ss, s_rs, s_rr = ST("ss", 0, 1), ST("rs", 1, 2), ST("rr", 2, 3)
        s_ss10, s_rs10, s_rr10 = ST("ss10", 4, 14), ST("rs10", 14, 24), ST("rr10", 24, 34)
        s_den, s_rden = ST("den", 34, 42), ST("rden", 42, 50)
        s_sa, s_rsa, s_ra = ST("sa", 50, 51), ST("rsa", 51, 52), ST("ra", 52, 53)
        s_rsc, s_rc = ST("rsc", 53, 54), ST("rc", 54, 55)
        s_sm, s_rsm = ST("sm", 40, 48), ST("rsm", 48, 56)

        eps_ap = cst[:, 0:1]

        def rms_stats(src_ap, src_rg, scratch_ap, scratch_rg, n, s_sum, s_sq, s_r):
            A(lambda e: e.activation(out=scratch_ap, in_=src_ap, func=AF.Square, accum_out=s_sum[1]),
              [src_rg], [scratch_rg, s_sum[0]])
            A(lambda e: e.activation(out=s_sq[1], in_=s_sum[1], func=AF.Sqrt, scale=1.0 / n, bias=eps_ap),
              [s_sum[0], rg_cst], [s_sq[0]])
            V(lambda e: e.reciprocal(out=s_r[1], in_=s_sq[1]), [s_sq[0]], [s_r[0]])

        gcount = [0]
        vcount = [0]

        table = uvb if BF16_TABLE else peer_uv

        def gather_uv(slot, ib):
            r = gcount[0] % NR
            gcount[0] += 1
            P.dma("gpsimd", lambda e, r=r, slot=slot, ib=ib: e.indirect_dma_start(
                out=ring_bf[:, r, :], out_offset=None, in_=table,
                in_offset=bass.IndirectOffsetOnAxis(ap=idxi2[:, ib, slot:slot + 1], axis=0)),
                sem_ring[r], reads=[rg_idxi[ib], rg_uvb, rg_uvb2], writes=[rg_ring[r]], extra=[setup_done])
            return r

        def stage_A(i):
            cur, prv = i % 2, (i - 1) % 2
            xb = i % 2
            BX = BX2[:, xb, :]
            rBX = rg_BX[xb]
            P.dma("sync", lambda e, i=i: e.dma_start(out=BX, in_=xh[128 * i:128 * (i + 1), :]), sem_x, writes=[rBX])
            rms_stats(BX, rBX, BY[:], rg["BY"], D, s_ss, s_rs, s_rr)
            V(lambda e: e.scalar_tensor_tensor(out=BY[:], in0=BX, scalar=s_rr[1], in1=A1, op0=ALU.mult, op1=ALU.mult),
              [rBX, s_rr[0], rg_A1], [rg["BY"]])
            BH, rBH = BH2[:, i % 2, :], rg_BH[i % 2]
            V(lambda e: e.tensor_tensor(out=BH, in0=BY[:], in1=shift1, op=ALU.add), [rg["BY"], rg_shift1], [rBH])
            yield
            p0 = bank_bf(0).rearrange("p (k t) -> p k t", t=128)
            for kt in range(8):
                T(lambda e, kt=kt: e.transpose(out=p0[:, kt, :], in_=BH[:, 128 * kt:128 * (kt + 1)], identity=ident_b[:]),
                  [rBH, rg["ident_b"]], [rg_ps[0]])
            A(lambda e: e.copy(out=BHT[:], in_=p0[:, 0:8, :]), [rg_ps[0]], [rg["BHT"]])
            yield
            for c in range(2):
                for kt in range(8):
                    T(lambda e, c=c, kt=kt: e.matmul(bank(1 + c, 384), lhsT=BHT[:, kt, :], rhs=w_in_sb[:, kt, 384 * c:384 * (c + 1)],
                                                     start=(kt == 0), stop=(kt == 7)),
                      [rg["BHT"], rg["w_in"]], [rg_ps[1 + c]])
            yield
            for nt_ in range(12):
                for kt in range(8):
                    T(lambda e, nt_=nt_, kt=kt: e.matmul(PS[:, 1536 + 128 * nt_:1536 + 128 * (nt_ + 1)],
                                                         lhsT=w_in_sb[:, kt, 768 + 128 * nt_:768 + 128 * (nt_ + 1)], rhs=BHT[:, kt, :],
                                                         start=(kt == 0), stop=(kt == 7)),
                      [rg["BHT"], rg["w_in"]], [rg_ps[3 + nt_ // 4]])
                if nt_ % 3 == 2:
                    yield
            A(lambda e: e.copy(out=qk_v[:, 0:384], in_=bank(1, 384)), [rg_ps[1]], [rg_qk])
            A(lambda e: e.copy(out=qk_v[:, 384:640], in_=bank(2, 256)), [rg_ps[2]], [rg_qk])
            A(lambda e, cur=cur: e.copy(out=vaug[:, cur, :, 0:64], in_=bank(2, 128, 256).rearrange("p (j d) -> p j d", d=64)),
              [rg_ps[2]], [rg_v[cur]])
            V(lambda e: e.tensor_tensor(out=sq_v, in0=qk_v, in1=qk_v, op=ALU.mult), [rg_qk], [rg_sq])
            V(lambda e: e.tensor_reduce(out=s_ss10[1], in_=sq_v.rearrange("p (h d) -> p h d", d=64), axis=AX.X, op=ALU.add),
              [rg_sq], [s_ss10[0]])
            A(lambda e: e.activation(out=s_rs10[1], in_=s_ss10[1], func=AF.Sqrt, scale=1.0 / 64, bias=eps_ap),
              [s_ss10[0], rg_cst], [s_rs10[0]])
            V(lambda e: e.reciprocal(out=s_rr10[1], in_=s_rs10[1]), [s_rs10[0]], [s_rr10[0]])
            V(lambda e: e.tensor_tensor(out=tmp_v.rearrange("p (h d) -> p h d", d=64), in0=qk_v.rearrange("p (h d) -> p h d", d=64),
                                        in1=s_rr10[1].unsqueeze(2).to_broadcast([128, 10, 64]), op=ALU.mult),
              [rg_qk, s_rr10[0]], [rg_tmp])
            V(lambda e: e.tensor_tensor(out=qk_v, in0=tmp_v, in1=qkg[:], op=ALU.mult), [rg_tmp, rg["qkg"]], [rg_qk])
            yield
            q4 = lambda ap: ap.rearrange("p (h two d) -> p h two d", two=2, d=32)
            V(lambda e, i=i: e.tensor_tensor(out=q4(sq_v), in0=q4(qk_v),
                                             in1=cos_t[:, i, :].unsqueeze(1).unsqueeze(1).to_broadcast([128, 10, 2, 32]), op=ALU.mult),
              [rg_qk, rg["cos_t"]], [rg_sq])
            V(lambda e, i=i: e.tensor_tensor(out=q4(tmp_v), in0=q4(qk_v),
                                             in1=sin_t[:, i, :].unsqueeze(1).unsqueeze(1).to_broadcast([128, 10, 2, 32]), op=ALU.mult),
              [rg_qk, rg["sin_t"]], [rg_tmp])
            V(lambda e: e.tensor_tensor(out=q4(qr_v)[:, :, 0, :], in0=q4(sq_v)[:, :, 0, :], in1=q4(tmp_v)[:, :, 1, :], op=ALU.subtract),
              [rg_sq, rg_tmp], [rg_qr])
            V(lambda e: e.tensor_tensor(out=q4(qr_v)[:, :, 1, :], in0=q4(sq_v)[:, :, 1, :], in1=q4(tmp_v)[:, :, 0, :], op=ALU.add),
              [rg_sq, rg_tmp], [rg_qr])
            yield
            cb_ps = PS[:, 1536:2048].rearrange("p (c t) -> p c t", t=128)
            cc_ps = PS[:, 2048:2560].rearrange("p (c t) -> p c t", t=128)
            cu_ps = PS[:, 2560:3072].rearrange("p (c t) -> p c t", t=128)
            cu3 = cu_v.rearrange("p (c t) -> p c t", t=128)
            c13 = c1_v.rearrange("p (c t) -> p c t", t=128)
            cv3 = cv_v.rearrange("p (c t) -> p c t", t=128)
            A(lambda e: e.copy(out=cu3, in_=cu_ps), [rg_ps[5]], [rg_cu])
            V(lambda e: e.tensor_copy(out=pbuf[:, :, 0:2], in_=pbuf[:, :, 128:130]), [rg["pbuf"]], [rg["pbuf"]])
            V(lambda e: e.tensor_tensor(out=pbuf[:, :, 2:130], in0=cc_ps, in1=cu3, op=ALU.mult), [rg_ps[4], rg_cu], [rg["pbuf"]])
            if i == 0:
                V(lambda e: e.tensor_scalar(out=pbuf[:, :, 2:130], in0=pbuf[:, :, 2:130], scalar1=hflag[:, 0:1], scalar2=None, op0=ALU.mult),
                  [rg["pbuf"], rg["hflag"]], [rg["pbuf"]])
            p4q = bank_bf(4).rearrange("p (h t) -> p h t", t=128)
            for h in range(8):
                T(lambda e, h=h: e.transpose(out=p4q[0:64, h, :], in_=qr_v[:, 64 * h:64 * (h + 1)], identity=ident_b[:]),
                  [rg_qr, rg["ident_b"]], [rg_ps[4]])
            for j in range(2):
                T(lambda e, j=j: e.transpose(out=p0[0:64, j, :], in_=qr_v[:, 512 + 64 * j:512 + 64 * (j + 1)], identity=ident_b[:]),
                  [rg_qr, rg["ident_b"]], [rg_ps[0]])
            A(lambda e: e.copy(out=qT_v, in_=p4q[0:64, 0:8, :]), [rg_ps[4]], [rg_qT])
            A(lambda e, cur=cur: e.copy(out=kT[:, cur, :, :], in_=p0[0:64, 0:2, :]), [rg_ps[0]], [rg_kT[cur]])
            yield
            if i == 0:
                return
            o_all = PS[:, 512:1536].rearrange("p (b r) -> p b r", r=512)[:, :, 0:260].rearrange("p b (g e) -> p b g e", e=65)
            for j in range(2):
                for wi, (slot, mi) in enumerate([(prv, 2 if i == 1 else 1), (cur, 0)]):
                    bk = 4 if wi == 0 else 5
                    T(lambda e, j=j, slot=slot, bk=bk: e.matmul(bank(bk), lhsT=kT[:, slot, j, :], rhs=qT_v[:, 4 * j:4 * j + 4, :],
                                                                start=True, stop=True),
                      [rg_kT[slot], rg_qT], [rg_ps[bk]])
                    A(lambda e, wi=wi, bk=bk: e.activation(out=pT_v[wi], in_=bank(bk), func=AF.Exp), [rg_ps[bk]], [rg_pT[wi]])
                    V(lambda e, wi=wi, mi=mi: e.tensor_tensor(out=pT_v[wi].rearrange("p (g t) -> p g t", t=128),
                                                              in0=pT_v[wi].rearrange("p (g t) -> p g t", t=128),
                                                              in1=masks_b[:, mi, :].unsqueeze(1).to_broadcast([128, 4, 128]), op=ALU.mult),
                      [rg_pT[wi], rg["masks_b"]], [rg_pT[wi]])
                for g in range(4):
                    for wi, slot in enumerate([prv, cur]):
                        T(lambda e, j=j, g=g, wi=wi, slot=slot: e.matmul(PS[:, 512 * (1 + j) + 65 * g:512 * (1 + j) + 65 * (g + 1)],
                                                                         lhsT=pT_v[wi][:, 128 * g:128 * (g + 1)], rhs=vaug[:, slot, j, :],
                                                                         start=(wi == 0), stop=(wi == 1)),
                          [rg_pT[wi], rg_v[slot]], [rg_ps[1 + j]])
                yield
            V(lambda e: e.tensor_tensor(out=s_den[1].rearrange("p (b g) -> p b g", g=4), in0=o_all[:, :, :, 64],
                                        in1=esink[:].rearrange("p (b g) -> p b g", g=4), op=ALU.add),
              [rg_ps[1], rg_ps[2], rg["esink"]], [s_den[0]])
            V(lambda e: e.reciprocal(out=s_rden[1], in_=s_den[1]), [s_den[0]], [s_rden[0]])
            V(lambda e: e.tensor_tensor(out=attn_v.rearrange("p (b g d) -> p b g d", g=4, d=64), in0=o_all[:, :, :, 0:64],
                                        in1=s_rden[1].rearrange("p (b g) -> p b g", g=4).unsqueeze(3).to_broadcast([128, 2, 4, 64]), op=ALU.mult),
              [rg_ps[1], rg_ps[2], s_rden[0]], [rg_attn])
            if DEBUG:
                P.dma("sync", lambda e, i=i: e.dma_start(out=dbg["attn"][128 * (i - 1):128 * i, :], in_=attn_v), sem_dbg, reads=[rg_attn])
            rms_stats(attn_v, rg_attn, c1_v, rg_c1, 512, s_sa, s_rsa, s_ra)
            V(lambda e: e.scalar_tensor_tensor(out=attn_n_v, in0=attn_v, scalar=s_ra[1], in1=attn_g[:], op0=ALU.mult, op1=ALU.mult),
              [rg_attn, s_ra[0], rg["attn_g"]], [rg_attn_n])
            for k in range(4):
                T(lambda e, k=k: e.transpose(out=p0[:, k, :], in_=attn_n_v[:, 128 * k:128 * (k + 1)], identity=ident_b[:]),
                  [rg_attn_n, rg["ident_b"]], [rg_ps[0]])
            A(lambda e: e.copy(out=attnT[:], in_=p0[:, 0:4, :]), [rg_ps[0]], [rg["attnT"]])
            yield
            for ct in range(4):
                V(lambda e, ct=ct: e.tensor_scalar(out=c13[:, ct, :], in0=pbuf[:, ct, 0:128], scalar1=convw[:, ct, 0:1], scalar2=None, op0=ALU.mult),
                  [rg["pbuf"], rg["convw"]], [rg_c1])
            for kk in (1, 2):
                for ct in range(4):
                    V(lambda e, ct=ct, kk=kk: e.scalar_tensor_tensor(out=c13[:, ct, :], in0=pbuf[:, ct, kk:kk + 128], scalar=convw[:, ct, kk:kk + 1],
                                                                      in1=c13[:, ct, :], op0=ALU.mult, op1=ALU.add),
                      [rg["pbuf"], rg["convw"], rg_c1], [rg_c1])
            V(lambda e: e.tensor_tensor(out=cv3, in0=cb_ps, in1=c13, op=ALU.mult), [rg_ps[3], rg_c1], [rg_cv])
            V(lambda e: e.tensor_tensor(out=cu3, in0=cv3, in1=cv3, op=ALU.mult), [rg_cv], [rg_cu])
            for ct in range(4):
                T(lambda e, ct=ct: e.matmul(bank(5, 1), lhsT=cu3[:, ct, :], rhs=ones_f[:, 0:1], start=(ct == 0), stop=(ct == 3)),
                  [rg_cu, rg["ones_f"]], [rg_ps[5]])
            A(lambda e: e.activation(out=s_rsc[1], in_=bank(5, 1), func=AF.Sqrt, scale=1.0 / 512, bias=eps_ap), [rg_ps[5], rg_cst], [s_rsc[0]])
            V(lambda e: e.reciprocal(out=s_rc[1], in_=s_rsc[1]), [s_rsc[0]], [s_rc[0]])
            cvT3 = cvT_v.rearrange("p (c t) -> p c t", t=128)
            for ct in range(4):
                V(lambda e, ct=ct: e.tensor_scalar(out=cvT3[:, ct, :], in0=cv3[:, ct, :], scalar1=convg[:, ct:ct + 1], scalar2=None, op0=ALU.mult),
                  [rg_cv, rg["convg"]], [rg_cvT])
            yield
            for c in range(2):
                for k in range(4):
                    T(lambda e, c=c, k=k: e.matmul(bank(1 + c), lhsT=attnT[:, k, :], rhs=w_out_sb[:, k, 512 * c:512 * (c + 1)],
                                                   start=(k == 0), stop=(k == 3)),
                      [rg["attnT"], rg["w_out"]], [rg_ps[1 + c]])
            for c in range(2):
                for ct in range(4):
                    T(lambda e, c=c, ct=ct: e.matmul(bank(3 + c), lhsT=cvT3[:, ct, :], rhs=w_out_sb[:, 4 + ct, 512 * c:512 * (c + 1)],
                                                     start=(ct == 0), stop=(ct == 3)),
                      [rg_cvT, rg["w_out"]], [rg_ps[3 + c]])
            A(lambda e: e.copy(out=BY[:], in_=PS[:, 512:1536]), [rg_ps[1], rg_ps[2]], [rg["BY"]])
            V(lambda e: e.scalar_tensor_tensor(out=BY[:], in0=PS[:, 1536:2560], scalar=s_rc[1], in1=BY[:], op0=ALU.mult, op1=ALU.add),
              [rg_ps[3], rg_ps[4], s_rc[0], rg["BY"]], [rg["BY"]])
            V(lambda e: e.tensor_tensor(out=BY[:], in0=BY[:], in1=gate1, op=ALU.mult), [rg["BY"], rg_gate1], [rg["BY"]])
            V(lambda e: e.tensor_tensor(out=BX, in0=BY[:], in1=BX, op=ALU.add), [rg["BY"], rBX], [rBX])
            row0 = 128 * (i - 1)
            if DEBUG:
                P.dma("sync", lambda e, row0=row0: e.dma_start(out=dbg["x1"][row0:row0 + 128, :], in_=BX), sem_dbg, reads=[rBX])
            yield

        def stage_R(i):
            xb = i % 2
            ib = i % 2
            BX = BX2[:, xb, :]
            rBX = rg_BX[xb]
            row0 = 128 * (i - 1)
            rms_stats(BX, rBX, BY[:], rg["BY"], D, s_ss, s_rs, s_rr)
            V(lambda e: e.scalar_tensor_tensor(out=BY[:], in0=BX, scalar=s_rr[1], in1=A2, op0=ALU.mult, op1=ALU.mult),
              [rBX, s_rr[0], rg_A2], [rg["BY"]])
            V(lambda e: e.tensor_tensor(out=BY[:], in0=BY[:], in1=shift2, op=ALU.add), [rg["BY"], rg_shift2], [rg["BY"]])
            BH, rBH = BH2[:, i % 2, :], rg_BH[i % 2]
            A(lambda e: e.copy(out=BH, in_=BY[:]), [rg["BY"]], [rBH])
            if DEBUG:
                P.dma("sync", lambda e, row0=row0: e.dma_start(out=dbg["h2"][row0:row0 + 128, :], in_=BY[:]), sem_dbg, reads=[rg["BY"]])
            yield
            p0 = bank_bf(0).rearrange("p (k t) -> p k t", t=128)
            for kt in range(8):
                T(lambda e, kt=kt: e.transpose(out=p0[:, kt, :], in_=BH[:, 128 * kt:128 * (kt + 1)], identity=ident_b[:]),
                  [rBH, rg["ident_b"]], [rg_ps[0]])
            A(lambda e: e.copy(out=BHT[:], in_=p0[:, 0:8, :]), [rg_ps[0]], [rg["BHT"]])
            yield
            wqb_r = wqb.rearrange("(kt p) n -> p kt n", p=128)

            def wq_load(c):
                P.dma("sync", lambda e, c=c: e.dma_start(out=wqc[:, c % 2, :, :], in_=wqb_r[:, :, 512 * c:512 * (c + 1)]), sem_wqc[c % 2],
                      reads=[rg_wqb, rg_wqb2], writes=[rg_wqc[c % 2]])

            wq_load(0)
            wq_load(1)
            for blk in range(16):
                c, bl = blk // 4, blk % 4
                for kt in range(8):
                    T(lambda e, blk=blk, kt=kt, c=c, bl=bl: e.matmul(PS[:, 512 + 128 * blk:512 + 128 * (blk + 1)], lhsT=wqc[:, c % 2, kt, 128 * bl:128 * (bl + 1)], rhs=BHT[:, kt, :],
                                                         start=(kt == 0), stop=(kt == 7)),
                      [rg_wqc[c % 2], rg["BHT"]], [rg_ps[1 + blk // 4]])
                if blk % 4 == 3:
                    if c + 2 < 4:
                        wq_load(c + 2)
                    yield
            A(lambda e: e.copy(out=S4B[:, 0:2048], in_=PS[:, 512:2560]), [rg_ps[1], rg_ps[2], rg_ps[3], rg_ps[4]], [rg_qpT])
            for blk in range(16):
                T(lambda e, blk=blk: e.matmul(PS[:, 512 + 128 * blk:512 + 128 * (blk + 1)], lhsT=qpT_v[:, blk, :], rhs=skT_sb[:, blk, :],
                                              start=True, stop=True),
                  [rg_qpT, rg["skT"]], [rg_ps[1 + blk // 4]])
            A(lambda e: e.copy(out=sc_v, in_=PS[:, 512:2560]), [rg_ps[1], rg_ps[2], rg_ps[3], rg_ps[4]], [rg_sc])
            yield
            for blk in range(16):
                s_in = sc_v[:, 128 * blk:128 * (blk + 1)]
                v0, v1 = vals[:, 16 * blk:16 * blk + 8], vals[:, 16 * blk + 8:16 * blk + 16]
                i0, i1 = ids[:, 16 * blk:16 * blk + 8], ids[:, 16 * blk + 8:16 * blk + 16]
                V(lambda e, s_in=s_in, v0=v0: e.max(out=v0, in_=s_in), [rg_sc], [rg["vals"]])
                V(lambda e, s_in=s_in, v0=v0, i0=i0: e.max_index(out=i0, in_max=v0, in_values=s_in), [rg_sc, rg["vals"]], [rg["ids"]])
                V(lambda e, s_in=s_in, v0=v0: e.match_replace(out=mr[:, 0:128], in_to_replace=v0, in_values=s_in, imm_value=-1e30),
                  [rg_sc, rg["vals"]], [rg["mr"]])
                V(lambda e, v1=v1: e.max(out=v1, in_=mr[:, 0:128]), [rg["mr"]], [rg["vals"]])
                V(lambda e, v1=v1, i1=i1: e.max_index(out=i1, in_max=v1, in_values=mr[:, 0:128]), [rg["mr"], rg["vals"]], [rg["ids"]])
                if blk % 4 == 3:
                    yield
            V(lambda e: e.tensor_copy(out=idsf, in_=ids), [rg["ids"]], [rg["idsf"]])
            vals4 = vals.rearrange("p (h s k) -> p h s k", s=2, k=16)
            idsf4 = idsf.rearrange("p (h s k) -> p h s k", s=2, k=16)
            cand4 = cand_v.rearrange("p (h a b) -> p h a b", a=16, b=16)
            V(lambda e: e.tensor_tensor(out=cand4, in0=vals4[:, :, 0, :].unsqueeze(3).to_broadcast([128, 8, 16, 16]),
                                        in1=vals4[:, :, 1, :].unsqueeze(2).to_broadcast([128, 8, 16, 16]), op=ALU.add),
              [rg["vals"]], [rg_cand])
            for h in range(8):
                c_in = cand_v[:, 256 * h:256 * (h + 1)]
                t0, t1 = top[:, 16 * h:16 * h + 8], top[:, 16 * h + 8:16 * h + 16]
                q0, q1 = pos[:, 16 * h:16 * h + 8], pos[:, 16 * h + 8:16 * h + 16]
                V(lambda e, c_in=c_in, t0=t0: e.max(out=t0, in_=c_in), [rg_cand], [rg["top"]])
                V(lambda e, c_in=c_in, t0=t0, q0=q0: e.max_index(out=q0, in_max=t0, in_values=c_in), [rg_cand, rg["top"]], [rg["pos"]])
                V(lambda e, c_in=c_in, t0=t0: e.match_replace(out=mr, in_to_replace=t0, in_values=c_in, imm_value=-1e30),
                  [rg_cand, rg["top"]], [rg["mr"]])
                V(lambda e, t1=t1: e.max(out=t1, in_=mr), [rg["mr"]], [rg["top"]])
                V(lambda e, t1=t1, q1=q1: e.max_index(out=q1, in_max=t1, in_values=mr), [rg["mr"], rg["top"]], [rg["pos"]])
                if h % 4 == 3:
                    yield
            V(lambda e: e.tensor_single_scalar(out=k1u, in_=pos[:], scalar=4, op=ALU.logical_shift_right), [rg["pos"]], [rg_k1u])
            V(lambda e: e.tensor_single_scalar(out=k2u, in_=pos[:], scalar=15, op=ALU.bitwise_and), [rg["pos"]], [rg_k2u])
            V(lambda e: e.tensor_copy(out=k1f, in_=k1u), [rg_k1u], [rg_k1f])
            V(lambda e: e.tensor_copy(out=k2f, in_=k2u), [rg_k2u], [rg_k2f])
            oh4 = oh_v.rearrange("p (h j k) -> p h j k", j=16, k=16)
            for side, kf, rk in ((0, k1f, rg_k1f), (1, k2f, rg_k2f)):
                V(lambda e, kf=kf: e.tensor_tensor(out=oh4, in0=kf.rearrange("p (h j) -> p h j", j=16).unsqueeze(3).to_broadcast([128, 8, 16, 16]),
                                                   in1=iota16[:].unsqueeze(1).unsqueeze(1).to_broadcast([128, 8, 16, 16]), op=ALU.is_equal),
                  [rk, rg["iota16"]], [rg_oh])
                V(lambda e, side=side: e.tensor_tensor(out=oh4, in0=oh4, in1=idsf4[:, :, side, :].unsqueeze(2).to_broadcast([128, 8, 16, 16]), op=ALU.mult),
                  [rg_oh, rg["idsf"]], [rg_oh])
                V(lambda e, side=side: e.tensor_reduce(out=isel[:, side, :], in_=oh_v.rearrange("p (a k) -> p a k", k=16), axis=AX.X, op=ALU.add),
                  [rg_oh], [rg_isel])
            yield
            V(lambda e: e.scalar_tensor_tensor(out=idxf, in0=isel[:, 0, :], scalar=128.0, in1=isel[:, 1, :], op0=ALU.mult, op1=ALU.add),
              [rg_isel], [rg_idxf])
            V(lambda e, ib=ib: e.tensor_copy(out=idxi2[:, ib, :], in_=idxf), [rg_idxf], [rg_idxi[ib]])
            top3 = top[:].rearrange("p (h j) -> p h j", j=16)
            V(lambda e: e.tensor_tensor(out=esm[:].rearrange("p (h j) -> p h j", j=16), in0=top3,
                                        in1=top3[:, :, 0:1].to_broadcast([128, 8, 16]), op=ALU.subtract),
              [rg["top"]], [rg["esm"]])
            A(lambda e: e.activation(out=esm[:], in_=esm[:], func=AF.Exp), [rg["esm"]], [rg["esm"]])
            V(lambda e: e.tensor_reduce(out=s_sm[1], in_=esm[:].rearrange("p (h j) -> p h j", j=16), axis=AX.X, op=ALU.add),
              [rg["esm"]], [s_sm[0]])
            V(lambda e: e.reciprocal(out=s_rsm[1], in_=s_sm[1]), [s_sm[0]], [s_rsm[0]])
            V(lambda e, ib=ib: e.tensor_tensor(out=gsm2[:, ib, :].rearrange("p (h j) -> p h j", j=16), in0=esm[:].rearrange("p (h j) -> p h j", j=16),
                                        in1=s_rsm[1].unsqueeze(2).to_broadcast([128, 8, 16]), op=ALU.mult),
              [rg["esm"], s_rsm[0]], [rg_gsm[ib]])
            if DEBUG:
                P.dma("sync", lambda e, row0=row0, ib=ib: e.dma_start(out=dbg["idx"][row0:row0 + 128, :], in_=idxi2[:, ib, :]), sem_dbg, reads=[rg_idxi[ib]])
                P.dma("sync", lambda e, row0=row0, ib=ib: e.dma_start(out=dbg["g"][row0:row0 + 128, :], in_=gsm2[:, ib, :]), sem_dbg, reads=[rg_gsm[ib]])
            yield

        def drain(gen):
            if gen is None:
                return
            for _ in gen:
                pass

        def step(gen):
            if gen is None:
                return None
            try:
                next(gen)
                return gen
            except StopIteration:
                return None

        def chain(*gens):
            for g_ in gens:
                for _ in g_:
                    yield

        def stage_UV(i, other):
            ib = i % 2
            BH, rBH = BH2[:, ib, :], rg_BH[ib]
            DVE_EVERY = 3

            def tail(slot, r):
                if slot % DVE_EVERY == DVE_EVERY - 1:
                    V(lambda e, slot=slot, ib=ib: e.tensor_tensor(out=wv[:, slot:slot + 1], in0=gav[:, slot:slot + 1], in1=gsm2[:, ib, slot:slot + 1], op=ALU.mult),
                      [rg_gav[slot], rg_gsm[ib]], [rg_wv[slot]])
                    if slot == DVE_EVERY - 1:
                        V(lambda e, slot=slot, r=r: e.tensor_scalar(out=accv[:], in0=ring_bf[:, r, 1024:2048], scalar1=wv[:, slot:slot + 1], scalar2=None, op0=ALU.mult),
                          [rg_ring[r], rg_wv[slot]], [rg_accv])
                    else:
                        V(lambda e, slot=slot, r=r: e.scalar_tensor_tensor(out=accv[:], in0=ring_bf[:, r, 1024:2048], scalar=wv[:, slot:slot + 1], in1=accv[:],
                                                                            op0=ALU.mult, op1=ALU.add),
                          [rg_ring[r], rg_wv[slot], rg_accv], [rg_accv])
                    return
                db = slot % 4
                V(lambda e, slot=slot, db=db, ib=ib: e.tensor_scalar(out=dg[:, db, :], in0=ident_b[:], scalar1=gav[:, slot:slot + 1],
                                                                      scalar2=gsm2[:, ib, slot:slot + 1], op0=ALU.mult, op1=ALU.mult),
                  [rg_gav[slot], rg_gsm[ib], rg["ident_b"]], [rg_dg[db]])
                for hf in range(2):
                    T(lambda e, r=r, db=db, hf=hf, slot=slot: e.matmul(PS[:, 3072 + 512 * hf:3072 + 512 * (hf + 1)], lhsT=dg[:, db, :],
                                                                        rhs=ring_bf[:, r, 1024 + 512 * hf:1024 + 512 * (hf + 1)],
                                                                        start=(slot == 0), stop=(slot == 127)),
                      [rg_dg[db], rg_ring[r]], [rg_ps[6 + hf]])

            LAG = 2
            pend = []
            for slot in range(128):
                r = gather_uv(slot, ib)
                V(lambda e, r=r: e.tensor_tensor(out=ring_bf[:, r, 0:1024], in0=ring_bf[:, r, 0:1024], in1=BH, op=ALU.mult),
                  [rg_ring[r], rBH], [rg_ring[r]])
                A(lambda e, r=r, slot=slot: e.activation(out=ring_bf[:, r, 0:1024], in_=ring_bf[:, r, 0:1024], func=AF.Copy, accum_out=av[:, slot:slot + 1]),
                  [rg_ring[r]], [rg_ring[r], rg_av[slot]])
                A(lambda e, slot=slot: e.activation(out=gav[:, slot:slot + 1], in_=av[:, slot:slot + 1], func=AF.Gelu), [rg_av[slot]], [rg_gav[slot]])
                pend.append((slot, r))
                if len(pend) > LAG:
                    tail(*pend.pop(0))
                if slot % 4 == 3:
                    other = step(other)
            while pend:
                tail(*pend.pop(0))
            drain(other)

        def stage_fin(i):
            xb = i % 2
            BX = BX2[:, xb, :]
            rBX = rg_BX[xb]
            row0 = 128 * (i - 1)
            fin_v = S8B[:, 0:1024]
            V(lambda e: e.tensor_tensor(out=fin_v, in0=PS[:, 3072:4096], in1=accv[:], op=ALU.add), [rg_ps[6], rg_ps[7], rg_accv], [rg_fin])
            V(lambda e: e.tensor_tensor(out=fin_v, in0=fin_v, in1=gate2, op=ALU.mult), [rg_fin, rg_gate2], [rg_fin])
            V(lambda e: e.tensor_tensor(out=BX, in0=fin_v, in1=BX, op=ALU.add), [rg_fin, rBX], [rBX])
            P.dma("sync", lambda e, row0=row0: e.dma_start(out=out[row0:row0 + 128, :], in_=BX), sem_o, reads=[rBX])

        if STAGE >= 1:
            drain(stage_A(0))
            last = N_TILES_RUN
            drain(stage_A(1))
            if STAGE >= 2:
                drain(stage_R(1))
            for i in range(1, last + 1):
                nxt = None
                if i < last:
                    nxt = chain(stage_A(i + 1), stage_R(i + 1)) if STAGE >= 2 else stage_A(i + 1)
                if STAGE >= 3:
                    stage_UV(i, nxt)
                    stage_fin(i)
                else:
                    drain(nxt)

        def tail_waits():
            w = [(sem_o, P.dcnt.get(sem_o, 0))] + [(sw, P.dcnt.get(sw, 0)) for sw in sem_w] + [(sv, P.dcnt.get(sv, 0)) for sv in sem_cvs]
            if DEBUG:
                w.append((sem_dbg, P.dcnt.get(sem_dbg, 0)))
            return [x for x in w if x[1] > 0]

        P.emit(block, tail_waits)
    return nc


def make_in_maps(x, c, positions, ada_w, ada_b, norm1_g, w_in, q_norm_g, k_norm_g, sinks, conv_w,
                 attn_out_g, conv_out_g, w_out, norm2_g, peer_wq, peer_subkeys, peer_u, peer_v):
    f = lambda a: np.ascontiguousarray(np.asarray(a), dtype=np.float32)
    x = f(x); c = f(c); positions = np.asarray(positions).astype(np.int32)
    S = x.shape[1]
    cpb = N_CORES // x.shape[0]
    assert S // cpb == TOK
    crow = np.zeros((1, 3584), np.float32)
    crow[0, 0:1024] = f(norm1_g)[0]
    crow[0, 1024:2048] = f(norm2_g)[0]
    crow[0, 2048:2560] = f(attn_out_g)[0]
    crow[0, 2560:3072] = np.tile(f(q_norm_g)[0], 8)
    crow[0, 3072:3200] = np.tile(f(k_norm_g)[0], 2)
    crow[0, 3200:3208] = f(sinks)[0]
    crow[0, 3208:3240] = (np.float32(10000.0) ** (-np.arange(0, 64, 2, dtype=np.float32) / np.float32(64))).astype(np.float32)
    convw = np.ascontiguousarray(f(conv_w)[0].T.reshape(4, 128, 3).transpose(1, 0, 2))
    convg = np.ascontiguousarray(f(conv_out_g)[0].reshape(4, 128).T)
    skT = np.ascontiguousarray(f(peer_subkeys)[0].reshape(16, 128, 128).transpose(2, 0, 1))
    ident = np.eye(128, dtype=np.float32)
    tk = np.arange(128)[:, None]
    tq = np.arange(128)[None, :]
    m_cur = (tk <= tq).astype(np.float32)
    m_prev = (tk > tq).astype(np.float32)
    shared = dict(ada_w=f(ada_w)[0], ada_b=f(ada_b), crow=crow, convw=convw, convg=convg, w_in=f(w_in)[0], w_out=f(w_out)[0],
                  wq=f(peer_wq)[0], skT=skT, peer_uv=np.ascontiguousarray(np.concatenate([f(peer_u)[0], f(peer_v)[0]], axis=1)), ident=ident)
    maps = []
    for core in range(N_CORES):
        b, j = core // cpb, core % cpb
        t0 = j * TOK
        xh = np.zeros((NTE * 128, D), np.float32)
        ph = np.zeros((NTE * 128,), np.int32)
        xh[128:] = x[b, t0:t0 + TOK]
        ph[128:] = positions[b, t0:t0 + TOK]
        first = (j == 0)
        if not first:
            xh[:128] = x[b, t0 - 128:t0]
            ph[:128] = positions[b, t0 - 128:t0]
        masks = np.stack([m_cur, m_prev, np.zeros_like(m_prev) if first else m_prev], axis=1)
        m = dict(shared)
        m.update(xh=xh, posi=np.ascontiguousarray(ph.reshape(NTE, 128).T),
                 ccol=np.ascontiguousarray(c[b].reshape(8, 128).T),
                 hflag=np.full((128, 1), 0.0 if first else 1.0, np.float32),
                 masks=np.ascontiguousarray(masks.astype(np.float32)))
        maps.append(m)
    return maps


_NC_CACHE = {}


def kernel(**inputs):
    maps = make_in_maps(**inputs)
    key = (DEBUG, N_TILES_RUN, STAGE, NEXP, BF16_TABLE)
    if key not in _NC_CACHE:
        _NC_CACHE[key] = build_nc()
    nc = _NC_CACHE[key]
    ncores = getattr(kernel, 'ncores', N_CORES)
    if NEXP != 16384:
        for m in maps:
            m['peer_uv'] = m['peer_uv'][:NEXP]
    res = run_bass_kernel_spmd(nc, maps[:ncores], core_ids=list(range(ncores)))
    if ncores != N_CORES:
        kernel.last = res.results
        return None
    B, S = np.asarray(inputs["x"]).shape[:2]
    outp = np.concatenate([r["out"] for r in res.results], axis=0).reshape(B, S, D).astype(np.float32)
    if DEBUG:
        kernel.last = res.results
    return outp
```

```python
import math
import os
from contextlib import ExitStack

import numpy as np
import concourse.bass as bass
import concourse.mybir as mybir
from concourse.bass_utils import run_bass_kernel_spmd

F32 = mybir.dt.float32
BF16 = mybir.dt.bfloat16
I32 = mybir.dt.int32
U32 = mybir.dt.uint32
ALU = mybir.AluOpType
AF = mybir.ActivationFunctionType
AX = mybir.AxisListType

N_CORES = 8
D = 1024
NT = 16
TOK = NT * 128
NTE = NT + 1
EPS = 1e-6
NR = 12
ENGS = ["sync", "scalar", "vector", "gpsimd", "tensor"]
DEBUG = False
N_TILES_RUN = NT
STAGE = 3
NEXP = 16384
BF16_TABLE = True


class Region:
    _all = {}

    def __init__(self, phys, lo, hi):
        self.phys, self.lo, self.hi = phys, lo, hi
        self.w = None
        self.r = []
        Region._all.setdefault(phys, []).append(self)
        self._ov = None

    def ov(self):
        return [o for o in Region._all[self.phys] if o.lo < self.hi and self.lo < o.hi]


class Op:
    __slots__ = ("eng", "fn", "deps", "is_dma", "dsem", "dval", "needs_inc", "seq", "waits")

    def __init__(self, eng, fn, deps, is_dma=False, dsem=None, dval=0):
        self.eng, self.fn, self.deps = eng, fn, deps
        self.is_dma, self.dsem, self.dval = is_dma, dsem, dval
        self.needs_inc = False
        self.seq = 0
        self.waits = []


class Prog:
    def __init__(self, sems):
        self.sems = sems
        self.ops = []
        self.dcnt = {}

    def _deps(self, reads, writes, extra):
        deps = {}
        for r in reads:
            for o in r.ov():
                if o.w is not None:
                    deps[o.w] = "raw"
        for w in writes:
            for o in w.ov():
                if o.w is not None and o.w not in deps:
                    deps[o.w] = "war"
                for rd in o.r:
                    if rd not in deps:
                        deps[rd] = "war"
        for e in extra:
            if e is not None:
                deps[e] = "raw"
        return deps

    def _commit(self, op, reads, writes):
        for r in reads:
            r.r.append(op)
        for w in writes:
            w.w = op
            w.r = []

    def op(self, eng, fn, reads=(), writes=(), extra=()):
        op = Op(eng, fn, self._deps(reads, writes, extra))
        self._commit(op, reads, writes)
        self.ops.append(op)
        return op

    def dma(self, eng, fn, sem, reads=(), writes=(), extra=()):
        self.dcnt[sem] = self.dcnt.get(sem, 0) + 16
        op = Op(eng, fn, self._deps(reads, writes, extra), True, sem, self.dcnt[sem])
        self._commit(op, reads, writes)
        self.ops.append(op)
        return op

    def finalize(self):
        for c in self.ops:
            for d, kind in c.deps.items():
                if d.is_dma:
                    c.waits.append(d)
                    continue
                if d.eng == c.eng and not c.is_dma:
                    if c.eng == "tensor":
                        continue
                    if kind != "raw":
                        continue
                d.needs_inc = True
                c.waits.append(d)
        cnt = {e: 0 for e in ENGS}
        for o in self.ops:
            if o.needs_inc and not o.is_dma:
                cnt[o.eng] += 1
                o.seq = cnt[o.eng]
        self.final_cnt = cnt

    def emit(self, block, tail_waits):
        self.finalize()
        sems = self.sems
        per = {e: [o for o in self.ops if o.eng == e] for e in ENGS}

        def run_engine(e, name):
            waited = {}
            for o in per[name]:
                for d in o.waits:
                    if d.is_dma:
                        s, v = d.dsem, d.dval
                    else:
                        s, v = sems[d.eng], d.seq
                    k = id(s)
                    if waited.get(k, 0) >= v:
                        continue
                    waited[k] = v
                    e.wait_ge(s, v)
                ins = o.fn(e)
                if o.is_dma:
                    ins.then_inc(o.dsem, 16)
                elif o.needs_inc:
                    ins.then_inc(sems[o.eng], 1)
            if name == "sync":
                for (s, v) in tail_waits():
                    e.wait_ge(s, v)

        @block.sync
        def _(e):
            run_engine(e, "sync")

        @block.scalar
        def _(e):
            run_engine(e, "scalar")

        @block.vector
        def _(e):
            run_engine(e, "vector")

        @block.gpsimd
        def _(e):
            run_engine(e, "gpsimd")

        @block.tensor
        def _(e):
            run_engine(e, "tensor")


def _split_2pi():
    two_pi = 2.0 * math.pi
    c1 = 6.28125
    r = two_pi - c1
    c2 = float(np.float32(r))
    m, e = math.frexp(c2)
    c2 = math.ldexp(round(m * 2048) / 2048, e)
    c3 = float(np.float32(two_pi - c1 - c2))
    return c1, c2, c3


def build_nc():
    Region._all = {}
    nc = bass.Bass("TRN2", target_bir_lowering=False)

    def din(name, shape, dt=F32):
        return nc.dram_tensor(name, list(shape), dt, kind="ExternalInput").ap()

    xh = din("xh", [NTE * 128, D])
    posi_d = din("posi", [128, NTE], I32)
    ccol_d = din("ccol", [128, 8])
    ada_w = din("ada_w", [D, 6 * D])
    ada_b = din("ada_b", [1, 6 * D])
    crow = din("crow", [1, 3584])
    convw_d = din("convw", [128, 4, 3])
    convg_d = din("convg", [128, 4])
    hflag_d = din("hflag", [128, 1])
    w_in = din("w_in", [D, 2304])
    w_out = din("w_out", [D, D])
    wq = din("wq", [D, 2048])
    skT_d = din("skT", [128, 16, 128])
    peer_uv = din("peer_uv", [NEXP, 2 * D])
    uvb = nc.dram_tensor("uvb", [NEXP, 2 * D], BF16, kind="Internal").ap() if BF16_TABLE else None
    wqb = nc.dram_tensor("wqb", [D, 2048], BF16, kind="Internal").ap()
    ident_d = din("ident", [128, 128])
    masks_d = din("masks", [128, 3, 128])
    out = nc.dram_tensor("out", [TOK, D], F32, kind="ExternalOutput").ap()
    dbg = {}
    if DEBUG:
        dbg["x1"] = nc.dram_tensor("dbg_x1", [TOK, D], F32, kind="ExternalOutput").ap()
        dbg["h2"] = nc.dram_tensor("dbg_h2", [TOK, D], F32, kind="ExternalOutput").ap()
        dbg["idx"] = nc.dram_tensor("dbg_idx", [TOK, 128], I32, kind="ExternalOutput").ap()
        dbg["g"] = nc.dram_tensor("dbg_g", [TOK, 128], F32, kind="ExternalOutput").ap()
        dbg["attn"] = nc.dram_tensor("dbg_attn", [TOK, 512], F32, kind="ExternalOutput").ap()
        dbg["mod"] = nc.dram_tensor("dbg_mod", [128, 6, D], F32, kind="ExternalOutput").ap()
        dbg["cs"] = nc.dram_tensor("dbg_cs", [128, 2, NTE, 32], F32, kind="ExternalOutput").ap()

    with ExitStack() as es:
        E = es.enter_context

        def sb(name, shape, dt=F32):
            return E(nc.sbuf_tensor("sb_" + name, list(shape), dt))

        ident_f = sb("ident_f", [128, 128])
        ident_b = sb("ident_b", [128, 128], BF16)
        ones_f = sb("ones_f", [128, 128])
        masks_b = sb("masks_b", [128, 3, 128], BF16)
        ccol = sb("ccol", [128, 8])
        cact = sb("cact", [128, 8])
        posi = sb("posi", [128, NTE], I32)
        posf = sb("posf", [128, NTE])
        invf = sb("invf", [128, 32])
        cos_t = sb("cos_t", [128, NTE, 32])
        sin_t = sb("sin_t", [128, NTE, 32])
        qkg = sb("qkg", [128, 640])
        attn_g = sb("attn_g", [128, 512])
        sinkb = sb("sinkb", [128, 8])
        esink = sb("esink", [128, 8])
        convw = sb("convw", [128, 4, 3])
        convg = sb("convg", [128, 4])
        hflag = sb("hflag", [128, 1])
        cst = sb("cst", [128, 4])
        iota16 = sb("iota16", [128, 16])
        mod6 = sb("mod6", [128, 6, D])
        w_in_sb = sb("w_in_sb", [128, 8, 2304], BF16)
        w_out_sb = sb("w_out_sb", [128, 8, D], BF16)
        wqc = sb("wqc", [128, 2, 8, 512], BF16)
        skT_sb = sb("skT_sb", [128, 16, 128], BF16)
        ring = sb("ring", [128, NR, D])
        BX2 = sb("BX2", [128, 2, D])
        BH2 = sb("BH2", [128, 2, D], BF16)
        BHT = sb("BHT", [128, 8, 128], BF16)
        S8A = sb("S8A", [128, 2048])
        S8B = sb("S8B", [128, 2048])
        S4A = sb("S4A", [128, 2048], BF16)
        S4B = sb("S4B", [128, 2048], BF16)
        BY = sb("BY", [128, D])
        dg = sb("dg", [128, 4, 128], BF16)
        attnT = sb("attnT", [128, 4, 128], BF16)
        kT = sb("kT", [64, 2, 2, 128], BF16)
        vaug = sb("vaug", [128, 2, 2, 65], BF16)
        pbuf = sb("pbuf", [128, 4, 130])
        st = sb("st", [128, 64])
        top = sb("top", [128, 128])
        pos = sb("pos", [128, 128], U32)
        idxi2 = sb("idxi2", [128, 2, 128], I32)
        gsm2 = sb("gsm2", [128, 2, 128])
        gav = sb("gav", [128, 128])
        esm = sb("esm", [128, 128])
        av = sb("av", [128, 128])
        PS = E(nc.psum_tensor("PS", [128, 4096], F32))

        sems = {k: E(nc.semaphore("s_" + k)) for k in ENGS}
        sem_c = E(nc.semaphore("d_c"))
        sem_w = [E(nc.semaphore("d_w%d" % i)) for i in range(4)]
        sem_ada = [E(nc.semaphore("d_ada%d" % i)) for i in range(2)]
        sem_x = E(nc.semaphore("d_x"))
        sem_o = E(nc.semaphore("d_o"))
        sem_dbg = E(nc.semaphore("d_dbg"))
        sem_ring = [E(nc.semaphore("d_r%d" % i)) for i in range(NR)]
        sem_cvl = [E(nc.semaphore("d_cvl%d" % i)) for i in range(2)]
        sem_wqc = [E(nc.semaphore("d_wqc%d" % i)) for i in range(2)]
        sem_cvs = [E(nc.semaphore("d_cvs%d" % i)) for i in range(2)]
        block = E(nc.Block())
        P = Prog(sems)

        def R(phys, lo=0, hi=1 << 30):
            return Region(phys, lo, hi)

        rg = {}
        for nm in ["ident_f", "ident_b", "ones_f", "masks_b", "ccol", "cact", "posi", "posf", "invf",
                   "cos_t", "sin_t", "qkg", "attn_g", "sinkb", "esink", "convw", "convg", "hflag", "cst", "iota16",
                   "w_in", "w_out", "skT", "BHT", "BY", "attnT", "pbuf",
                   "top", "pos",
                   "esm"]:
            rg[nm] = R(nm)
        rg_mod = [R("mod6", 4096 * k, 4096 * (k + 1)) for k in range(6)]
        rg_ring = [R("ring", 4096 * k, 4096 * (k + 1)) for k in range(NR)]
        rg_adab = [rg["BY"], rg["BY"]]
        adab = BY[0:1, 0:512].rearrange("p (s n) -> p s n", n=256)
        rg_BX = [R("BX2", 4096 * k, 4096 * (k + 1)) for k in range(2)]
        rg_idxi = [R("idxi2", 512 * k, 512 * (k + 1)) for k in range(2)]
        rg_gsm = [R("gsm2", 512 * k, 512 * (k + 1)) for k in range(2)]
        rg_BH = [R("BH2", 2048 * k, 2048 * (k + 1)) for k in range(2)]
        rg_dg = [R("dg", 256 * k, 256 * (k + 1)) for k in range(4)]
        rg_uvb = R("uvb")
        rg_uvb2 = R("uvb2")
        rg_wqb, rg_wqb2 = R("wqb"), R("wqb2")
        rg_wqc = [R("wqc", 8192 * k, 8192 * (k + 1)) for k in range(2)]
        rg_av = [R("av", 4 * k, 4 * k + 4) for k in range(128)]
        rg_gav = [R("gav", 4 * k, 4 * k + 4) for k in range(128)]
        wv = ident_f
        rg_wv = [R("ident_f", 4 * k, 4 * k + 4) for k in range(128)]
        ring_bf = ring[:].bitcast(BF16)
        rg_kT = [R("kT", 512 * k, 512 * (k + 1)) for k in range(2)]
        rg_v = [R("vaug", 260 * k, 260 * (k + 1)) for k in range(2)]
        rg_st = {}

        def ST(name, lo, hi):
            rg_st[name] = (R("st", 4 * lo, 4 * hi), st[:, lo:hi])
            return rg_st[name]

        masks_f = S8A[:, 1600:1984].rearrange("p (m t) -> p m t", t=128)
        rg["masks_f"] = R("S8A", 6400, 7936)
        qk_v, sq_v, tmp_v = S8A[:, 0:640], S8A[:, 640:1280], S8A[:, 1280:1920]
        rg_qk, rg_sq, rg_tmp = R("S8A", 0, 2560), R("S8A", 2560, 5120), R("S8A", 5120, 7680)
        sc_v, rg_sc = S8A[:, 0:2048], R("S8A", 0, 8192)
        k1u, rg_k1u = S8A[:, 0:128].bitcast(U32), R("S8A", 0, 512)
        k2u, rg_k2u = S8A[:, 128:256].bitcast(U32), R("S8A", 512, 1024)
        k1f, rg_k1f = S8A[:, 256:384], R("S8A", 1024, 1536)
        k2f, rg_k2f = S8A[:, 384:512], R("S8A", 1536, 2048)
        isel, rg_isel = S8A[:, 512:768].rearrange("p (s n) -> p s n", n=128), R("S8A", 2048, 3072)
        idxf, rg_idxf = S8A[:, 768:896], R("S8A", 3072, 3584)
        attn_v, cu_v, c1_v, cv_v = S8B[:, 0:512], S8B[:, 512:1024], S8B[:, 1024:1536], S8B[:, 1536:2048]
        rg_attn, rg_cu, rg_c1, rg_cv = R("S8B", 0, 2048), R("S8B", 2048, 4096), R("S8B", 4096, 6144), R("S8B", 6144, 8192)
        cand_v, rg_cand = S8B[:, 0:2048], R("S8B", 0, 8192)
        rg_fin = R("S8B", 0, 4096)
        qr_v, attn_n_v, cvT_v = S4A[:, 0:640], S4A[:, 640:1152], S4A[:, 1152:1664]
        rg_qr, rg_attn_n, rg_cvT = R("S4A", 0, 1280), R("S4A", 1280, 2304), R("S4A", 2304, 3328)
        vals = S4A[:, 0:512].bitcast(F32)
        ids = S4A[:, 512:1024].bitcast(U32)
        idsf = S4A[:, 1024:1536].bitcast(F32)
        mr = S4A[:, 1536:2048].bitcast(F32)
        rg["vals"], rg["ids"], rg["idsf"], rg["mr"] = R("S4A", 0, 1024), R("S4A", 1024, 2048), R("S4A", 2048, 3072), R("S4A", 3072, 4096)
        qT_v = S4B[0:64, 0:1024].rearrange("p (h t) -> p h t", t=128)
        pT_v = [S4B[:, 1024:1536], S4B[:, 1536:2048]]
        rg_qT, rg_pT = R("S4B", 0, 2048), [R("S4B", 2048, 3072), R("S4B", 3072, 4096)]
        qpT_v, rg_qpT = S4B[:, 0:2048].rearrange("p (b t) -> p b t", t=128), R("S4B", 0, 4096)
        oh_v, rg_oh = S4B[:, 0:2048], R("S4B", 0, 4096)
        junk_v, rg_junk = S4B[:, 0:1024], R("S4B", 0, 2048)
        rg_ps = [R("PS", 2048 * b, 2048 * (b + 1)) for b in range(8)]

        def bank(b, n=512, off=0):
            return PS[:, 512 * b + off:512 * b + off + n]

        def bank_bf(b):
            return PS[:, 512 * b:512 * (b + 1)].bitcast(BF16)

        grp_c, grp_w = [], [[], [], [], []]

        def ld(dst, src, regs, sem=sem_c, eng="sync"):
            o = P.dma(eng, lambda e, dst=dst, src=src: e.dma_start(out=dst, in_=src), sem, writes=regs)
            grp_c.append(o)
            return o

        ld(ident_f[:], ident_d, [rg["ident_f"]])
        ld(masks_f, masks_d, [rg["masks_f"]])
        ld(ccol[:], ccol_d, [rg["ccol"]])
        ld(posi[:], posi_d, [rg["posi"]])
        ld(convw[:], convw_d, [rg["convw"]])
        ld(convg[:], convg_d, [rg["convg"]])
        ld(hflag[:], hflag_d, [rg["hflag"]])
        if 'b' not in os.environ.get('KSKIP', ''):
          ld(mod6[:, 4, :], crow[:, 0:1024].partition_broadcast(128), [rg_mod[4]])
          ld(mod6[:, 5, :], crow[:, 1024:2048].partition_broadcast(128), [rg_mod[5]])
          ld(attn_g[:], crow[:, 2048:2560].partition_broadcast(128), [rg["attn_g"]])
          ld(qkg[:], crow[:, 2560:3200].partition_broadcast(128), [rg["qkg"]])
          ld(sinkb[:], crow[:, 3200:3208].partition_broadcast(128), [rg["sinkb"]])
          ld(invf[:], crow[:, 3208:3240].partition_broadcast(128), [rg["invf"]])

        SKIP = os.environ.get('KSKIP', '')
        for kt in range(8 if 'w' not in SKIP else 0):
            for c in range(2):
                grp_w[0].append(P.dma("gpsimd", lambda e, kt=kt, c=c: e.dma_start(out=w_in_sb[:, kt, 1152 * c:1152 * (c + 1)],
                                                                   in_=w_in[128 * kt:128 * (kt + 1), 1152 * c:1152 * (c + 1)]),
                      sem_w[0], writes=[]))
        for kt in range(8 if ('w' not in SKIP and '1' not in SKIP) else 0):
            grp_w[1].append(P.dma("gpsimd", lambda e, kt=kt: e.dma_start(out=w_out_sb[:, kt, :], in_=w_out[128 * kt:128 * (kt + 1), :]),
                  sem_w[1], writes=[]))
        for b4 in range(4 if ('w' not in SKIP and '3' not in SKIP) else 0):
            grp_w[3].append(P.dma("gpsimd", lambda e, b4=b4: e.dma_start(out=skT_sb[:, 4 * b4:4 * b4 + 4, :], in_=skT_d[:, 4 * b4:4 * b4 + 4, :]),
                  sem_w[3], writes=[]))
        for o in grp_c:
            o.dval = P.dcnt[sem_c]
        for k, nm in enumerate(["w_in", "w_out", None, "skT"]):
            for o in grp_w[k]:
                o.dval = P.dcnt.get(sem_w[k], 0)
            if grp_w[k] and nm:
                rg[nm].w = grp_w[k][-1]

        V = lambda fn, reads=(), writes=(), extra=(): P.op("vector", fn, reads, writes, extra)
        A = lambda fn, reads=(), writes=(), extra=(): P.op("scalar", fn, reads, writes, extra)
        T = lambda fn, reads=(), writes=(), extra=(): P.op("tensor", fn, reads, writes, extra)
        G = lambda fn, reads=(), writes=(), extra=(): P.op("gpsimd", fn, reads, writes, extra)

        V(lambda e: e.tensor_copy(out=ident_b[:], in_=ident_f[:]), [rg["ident_f"]], [rg["ident_b"]])
        V(lambda e: e.tensor_copy(out=masks_b[:], in_=masks_f), [rg["masks_f"]], [rg["masks_b"]])
        V(lambda e: e.memset(ones_f[:], 1.0), [], [rg["ones_f"]])
        V(lambda e: e.memset(vaug[:], 1.0), [], rg_v)
        V(lambda e: e.memset(pbuf[:], 0.0), [], [rg["pbuf"]])
        rg_cst = R("cst")
        V(lambda e: e.memset(cst[:, 0:1], EPS), [], [rg_cst])
        V(lambda e: e.memset(cst[:, 1:2], math.pi / 2), [], [rg_cst])
        if 'i' not in SKIP:
          G(lambda e: e.iota(iota16[:], pattern=[[1, 16]], base=0, channel_multiplier=0, allow_small_or_imprecise_dtypes=True),
            [], [rg["iota16"]])

        if BF16_TABLE:
            stg = [ring_bf[:, 4 + 2 * k:6 + 2 * k, :] for k in range(2)]
            rg_stg = [R("ring", 16384 + 8192 * k, 16384 + 8192 * (k + 1)) for k in range(2)]
            cv_stores = []
            NCH = NEXP // 256

            NWQ = D // 256
            def cv_src_dst(c):
                if c < NWQ:
                    return wq[256 * c:256 * (c + 1), :], wqb[256 * c:256 * (c + 1), :]
                c2 = c - NWQ
                return peer_uv[256 * c2:256 * (c2 + 1), :], uvb[256 * c2:256 * (c2 + 1), :]

            def cv_load(c):
                src = cv_src_dst(c)[0].rearrange("(p j) n -> p j n", j=2)
                P.dma("gpsimd", lambda e, c=c, src=src: e.dma_start(out=stg[c % 2], in_=src), sem_cvl[c % 2], writes=[rg_stg[c % 2]])

            def cv_store(c):
                dst = cv_src_dst(c)[1].rearrange("(p j) n -> p j n", j=2)
                cv_stores.append(P.dma("gpsimd", lambda e, c=c, dst=dst: e.dma_start(out=dst, in_=stg[c % 2]), sem_cvs[c % 2], reads=[rg_stg[c % 2]]))

            NCH = NCH + NWQ
            cv_load(0)
            for c in range(1, NCH):
                cv_load(c)
                cv_store(c - 1)
            cv_store(NCH - 1)
            rg_wqb.w = cv_stores[NWQ - 1]
            rg_wqb2.w = cv_stores[NWQ - 2]
            rg_uvb.w = cv_stores[-1]
            rg_uvb2.w = cv_stores[-2]
        V(lambda e: e.tensor_copy(out=posf[:], in_=posi[:]), [rg["posi"]], [rg["posf"]])
        A(lambda e: e.activation(out=cact[:], in_=ccol[:], func=AF.Silu), [rg["ccol"]], [rg["cact"]])
        A(lambda e: e.activation(out=esink[:], in_=sinkb[:], func=AF.Exp), [rg["sinkb"]], [rg["esink"]])
        V(lambda e: e.tensor_scalar(out=qkg[:, 0:512], in0=qkg[:, 0:512], scalar1=0.125, scalar2=None, op0=ALU.mult),
          [rg["qkg"]], [rg["qkg"]])
        cact_rep = BX2[:, 0, :].rearrange("p (k m) -> p k m", m=128)
        V(lambda e: e.tensor_copy(out=cact_rep, in_=cact[:].unsqueeze(2).to_broadcast([128, 8, 128])),
          [rg["cact"]], [rg_BX[0]])

        C1, C2, C3 = _split_2pi()
        MAGIC = 12582912.0
        PI_LO = 3.1415925
        ang = S8A[:, 0:NTE * 32].rearrange("p (i j) -> p i j", j=32)
        nn = S8A[:, 1024:1024 + NTE * 32].rearrange("p (i j) -> p i j", j=32)
        rg_ang, rg_nn = R("S8A", 0, 4096), R("S8A", 4096, 8192)
        V(lambda e: e.tensor_tensor(out=ang, in0=posf[:].unsqueeze(2).to_broadcast([128, NTE, 32]),
                                    in1=invf[:].unsqueeze(1).to_broadcast([128, NTE, 32]), op=ALU.mult),
          [rg["posf"], rg["invf"]], [rg_ang])
        V(lambda e: e.tensor_scalar(out=nn, in0=ang, scalar1=1.0 / (2 * math.pi), scalar2=MAGIC, op0=ALU.mult, op1=ALU.add),
          [rg_ang], [rg_nn])
        V(lambda e: e.tensor_scalar(out=nn, in0=nn, scalar1=MAGIC, scalar2=None, op0=ALU.subtract), [rg_nn], [rg_nn])
        for cc in (C1, C2, C3):
            V(lambda e, cc=cc: e.scalar_tensor_tensor(out=ang, in0=nn, scalar=-cc, in1=ang, op0=ALU.mult, op1=ALU.add),
              [rg_nn, rg_ang], [rg_ang])
        V(lambda e: e.tensor_scalar(out=ang, in0=ang, scalar1=PI_LO, scalar2=-PI_LO, op0=ALU.min, op1=ALU.max), [rg_ang], [rg_ang])
        A(lambda e: e.activation(out=sin_t[:], in_=ang, func=AF.Sin), [rg_ang], [rg["sin_t"]])
        A(lambda e: e.activation(out=nn, in_=ang, func=AF.Abs), [rg_ang], [rg_nn])
        A(lambda e: e.activation(out=cos_t[:], in_=nn, func=AF.Sin, scale=-1.0, bias=cst[:, 1:2]), [rg_nn, rg_cst], [rg["cos_t"]])

        ada_r = ada_w.rearrange("(k p) n -> p k n", p=128)
        chunkbuf = [ring[:, 0:2, :].rearrange("p s (k n) -> p (s k) n", n=256) if False else None, None]
        chunkbuf = [ring[:, 2 * s:2 * s + 2, :].rearrange("p s (k n) -> p (s k) n", n=256) for s in range(2)]
        rg_chunk = [R("ring", 8192 * s, 8192 * (s + 1)) for s in range(2)]
        for c in range(24 if 'a' not in SKIP else 0):
            s = c % 2
            o1 = P.dma("sync", lambda e, c=c, s=s: e.dma_start(out=chunkbuf[s], in_=ada_r[:, :, 256 * c:256 * (c + 1)]),
                       sem_ada[s], writes=[rg_chunk[s]])
            o2 = P.dma("sync", lambda e, c=c, s=s: e.dma_start(out=adab[0:1, s, :], in_=ada_b[:, 256 * c:256 * (c + 1)]),
                       sem_ada[s], writes=[rg_adab[s]])
            o1.dval = o2.dval
            pb = bank(s, 256)
            for kt in range(8):
                T(lambda e, kt=kt, s=s, pb=pb: e.matmul(pb, lhsT=cact_rep[:, kt, :], rhs=chunkbuf[s][:, kt, :], start=(kt == 0), stop=False),
                  [rg_BX[0], rg_chunk[s]], [rg_ps[s]])
            T(lambda e, s=s, pb=pb: e.matmul(pb, lhsT=ones_f[0:1, :], rhs=adab[0:1, s, :], start=False, stop=True),
              [rg["ones_f"], rg_adab[s]], [rg_ps[s]])
            sec, off = c // 4, (c % 4) * 256
            if sec in (1, 4):
                k = 4 if sec == 1 else 5
                V(lambda e, k=k, off=off, pb=pb: e.scalar_tensor_tensor(out=mod6[:, k, off:off + 256], in0=pb, scalar=1.0,
                                                                         in1=mod6[:, k, off:off + 256], op0=ALU.add, op1=ALU.mult),
                  [rg_ps[s], rg_mod[k]], [rg_mod[k]])
            else:
                k = {0: 0, 2: 1, 3: 2, 5: 3}[sec]
                A(lambda e, k=k, off=off, pb=pb: e.copy(out=mod6[:, k, off:off + 256], in_=pb), [rg_ps[s]], [rg_mod[k]])
        shift1, gate1, shift2, gate2, A1, A2 = [mod6[:, k, :] for k in range(6)]
        rg_shift1, rg_gate1, rg_shift2, rg_gate2, rg_A1, rg_A2 = rg_mod
        setup_done = V(lambda e: e.memset(st[:, 60:61], 0.0), [rg_mod[k] for k in range(6)], [R("st", 240, 244)])


        Region._all["ring"] = list(rg_ring)
        if DEBUG:
            P.dma("sync", lambda e: e.dma_start(out=dbg["mod"], in_=mod6[:]), sem_dbg, reads=rg_mod)
            P.dma("sync", lambda e: e.dma_start(out=dbg["cs"][:, 0], in_=cos_t[:]), sem_dbg, reads=[rg["cos_t"]])
            P.dma("sync", lambda e: e.dma_start(out=dbg["cs"][:, 1], in_=sin_t[:]), sem_dbg, reads=[rg["sin_t"]])

        s_ss, s_rs, s_rr = ST("ss", 0, 1), ST("rs", 1, 2), ST("rr", 2, 3)
        s_ss10, s_rs10, s_rr10 = ST("ss10", 4, 14), ST("rs10", 14, 24), ST("rr10", 24, 34)
        s_den, s_rden = ST("den", 34, 42), ST("rden", 42, 50)
        s_sa, s_rsa, s_ra = ST("sa", 50, 51), ST("rsa", 51, 52), ST("ra", 52, 53)
        s_rsc, s_rc = ST("rsc", 53, 54), ST("rc", 54, 55)
        s_sm, s_rsm = ST("sm", 40, 48), ST("rsm", 48, 56)

        eps_ap = cst[:, 0:1]

        def rms_stats(src_ap, src_rg, scratch_ap, scratch_rg, n, s_sum, s_sq, s_r):
            A(lambda e: e.activation(out=scratch_ap, in_=src_ap, func=AF.Square, accum_out=s_sum[1]),
              [src_rg], [scratch_rg, s_sum[0]])
            A(lambda e: e.activation(out=s_sq[1], in_=s_sum[1], func=AF.Sqrt, scale=1.0 / n, bias=eps_ap),
              [s_sum[0], rg_cst], [s_sq[0]])
            V(lambda e: e.reciprocal(out=s_r[1], in_=s_sq[1]), [s_sq[0]], [s_r[0]])

        gcount = [0]
        vcount = [0]

        table = uvb if BF16_TABLE else peer_uv

        def gather_uv(slot, ib):
            r = gcount[0] % NR
            gcount[0] += 1
            P.dma("gpsimd", lambda e, r=r, slot=slot, ib=ib: e.indirect_dma_start(
                out=ring_bf[:, r, :], out_offset=None, in_=table,
                in_offset=bass.IndirectOffsetOnAxis(ap=idxi2[:, ib, slot:slot + 1], axis=0)),
                sem_ring[r], reads=[rg_idxi[ib], rg_uvb, rg_uvb2], writes=[rg_ring[r]], extra=[setup_done])
            return r

        def stage_A(i):
            cur, prv = i % 2, (i - 1) % 2
            xb = i % 2
            BX = BX2[:, xb, :]
            rBX = rg_BX[xb]
            P.dma("sync", lambda e, i=i: e.dma_start(out=BX, in_=xh[128 * i:128 * (i + 1), :]), sem_x, writes=[rBX])
            rms_stats(BX, rBX, BY[:], rg["BY"], D, s_ss, s_rs, s_rr)
            V(lambda e: e.scalar_tensor_tensor(out=BY[:], in0=BX, scalar=s_rr[1], in1=A1, op0=ALU.mult, op1=ALU.mult),
              [rBX, s_rr[0], rg_A1], [rg["BY"]])
            BH, rBH = BH2[:, i % 2, :], rg_BH[i % 2]
            V(lambda e: e.tensor_tensor(out=BH, in0=BY[:], in1=shift1, op=ALU.add), [rg["BY"], rg_shift1], [rBH])
            yield
            p0 = bank_bf(0).rearrange("p (k t) -> p k t", t=128)
            for kt in range(8):
                T(lambda e, kt=kt: e.transpose(out=p0[:, kt, :], in_=BH[:, 128 * kt:128 * (kt + 1)], identity=ident_b[:]),
                  [rBH, rg["ident_b"]], [rg_ps[0]])
            A(lambda e: e.copy(out=BHT[:], in_=p0[:, 0:8, :]), [rg_ps[0]], [rg["BHT"]])
            yield
            for c in range(2):
                for kt in range(8):
                    T(lambda e, c=c, kt=kt: e.matmul(bank(1 + c, 384), lhsT=BHT[:, kt, :], rhs=w_in_sb[:, kt, 384 * c:384 * (c + 1)],
                                                     start=(kt == 0), stop=(kt == 7)),
                      [rg["BHT"], rg["w_in"]], [rg_ps[1 + c]])
            yield
            for nt_ in range(12):
                for kt in range(8):
                    T(lambda e, nt_=nt_, kt=kt: e.matmul(PS[:, 1536 + 128 * nt_:1536 + 128 * (nt_ + 1)],
                                                         lhsT=w_in_sb[:, kt, 768 + 128 * nt_:768 + 128 * (nt_ + 1)], rhs=BHT[:, kt, :],
                                                         start=(kt == 0), stop=(kt == 7)),
                      [rg["BHT"], rg["w_in"]], [rg_ps[3 + nt_ // 4]])
                if nt_ % 3 == 2:
                    yield
            A(lambda e: e.copy(out=qk_v[:, 0:384], in_=bank(1, 384)), [rg_ps[1]], [rg_qk])
            A(lambda e: e.copy(out=qk_v[:, 384:640], in_=bank(2, 256)), [rg_ps[2]], [rg_qk])
            A(lambda e, cur=cur: e.copy(out=vaug[:, cur, :, 0:64], in_=bank(2, 128, 256).rearrange("p (j d) -> p j d", d=64)),
              [rg_ps[2]], [rg_v[cur]])
            V(lambda e: e.tensor_tensor(out=sq_v, in0=qk_v, in1=qk_v, op=ALU.mult), [rg_qk], [rg_sq])
            V(lambda e: e.tensor_reduce(out=s_ss10[1], in_=sq_v.rearrange("p (h d) -> p h d", d=64), axis=AX.X, op=ALU.add),
              [rg_sq], [s_ss10[0]])
            A(lambda e: e.activation(out=s_rs10[1], in_=s_ss10[1], func=AF.Sqrt, scale=1.0 / 64, bias=eps_ap),
              [s_ss10[0], rg_cst], [s_rs10[0]])
            V(lambda e: e.reciprocal(out=s_rr10[1], in_=s_rs10[1]), [s_rs10[0]], [s_rr10[0]])
            V(lambda e: e.tensor_tensor(out=tmp_v.rearrange("p (h d) -> p h d", d=64), in0=qk_v.rearrange("p (h d) -> p h d", d=64),
                                        in1=s_rr10[1].unsqueeze(2).to_broadcast([128, 10, 64]), op=ALU.mult),
              [rg_qk, s_rr10[0]], [rg_tmp])
            V(lambda e: e.tensor_tensor(out=qk_v, in0=tmp_v, in1=qkg[:], op=ALU.mult), [rg_tmp, rg["qkg"]], [rg_qk])
            yield
            q4 = lambda ap: ap.rearrange("p (h two d) -> p h two d", two=2, d=32)
            V(lambda e, i=i: e.tensor_tensor(out=q4(sq_v), in0=q4(qk_v),
                                             in1=cos_t[:, i, :].unsqueeze(1).unsqueeze(1).to_broadcast([128, 10, 2, 32]), op=ALU.mult),
              [rg_qk, rg["cos_t"]], [rg_sq])
            V(lambda e, i=i: e.tensor_tensor(out=q4(tmp_v), in0=q4(qk_v),
                                             in1=sin_t[:, i, :].unsqueeze(1).unsqueeze(1).to_broadcast([128, 10, 2, 32]), op=ALU.mult),
              [rg_qk, rg["sin_t"]], [rg_tmp])
            V(lambda e: e.tensor_tensor(out=q4(qr_v)[:, :, 0, :], in0=q4(sq_v)[:, :, 0, :], in1=q4(tmp_v)[:, :, 1, :], op=ALU.subtract),
              [rg_sq, rg_tmp], [rg_qr])
            V(lambda e: e.tensor_tensor(out=q4(qr_v)[:, :, 1, :], in0=q4(sq_v)[:, :, 1, :], in1=q4(tmp_v)[:, :, 0, :], op=ALU.add),
              [rg_sq, rg_tmp], [rg_qr])
            yield
            cb_ps = PS[:, 1536:2048].rearrange("p (c t) -> p c t", t=128)
            cc_ps = PS[:, 2048:2560].rearrange("p (c t) -> p c t", t=128)
            cu_ps = PS[:, 2560:3072].rearrange("p (c t) -> p c t", t=128)
            cu3 = cu_v.rearrange("p (c t) -> p c t", t=128)
            c13 = c1_v.rearrange("p (c t) -> p c t", t=128)
            cv3 = cv_v.rearrange("p (c t) -> p c t", t=128)
            A(lambda e: e.copy(out=cu3, in_=cu_ps), [rg_ps[5]], [rg_cu])
            V(lambda e: e.tensor_copy(out=pbuf[:, :, 0:2], in_=pbuf[:, :, 128:130]), [rg["pbuf"]], [rg["pbuf"]])
            V(lambda e: e.tensor_tensor(out=pbuf[:, :, 2:130], in0=cc_ps, in1=cu3, op=ALU.mult), [rg_ps[4], rg_cu], [rg["pbuf"]])
            if i == 0:
                V(lambda e: e.tensor_scalar(out=pbuf[:, :, 2:130], in0=pbuf[:, :, 2:130], scalar1=hflag[:, 0:1], scalar2=None, op0=ALU.mult),
                  [rg["pbuf"], rg["hflag"]], [rg["pbuf"]])
            p4q = bank_bf(4).rearrange("p (h t) -> p h t", t=128)
            for h in range(8):
                T(lambda e, h=h: e.transpose(out=p4q[0:64, h, :], in_=qr_v[:, 64 * h:64 * (h + 1)], identity=ident_b[:]),
                  [rg_qr, rg["ident_b"]], [rg_ps[4]])
            for j in range(2):
                T(lambda e, j=j: e.transpose(out=p0[0:64, j, :], in_=qr_v[:, 512 + 64 * j:512 + 64 * (j + 1)], identity=ident_b[:]),
                  [rg_qr, rg["ident_b"]], [rg_ps[0]])
            A(lambda e: e.copy(out=qT_v, in_=p4q[0:64, 0:8, :]), [rg_ps[4]], [rg_qT])
            A(lambda e, cur=cur: e.copy(out=kT[:, cur, :, :], in_=p0[0:64, 0:2, :]), [rg_ps[0]], [rg_kT[cur]])
            yield
            if i == 0:
                return
            o_all = PS[:, 512:1536].rearrange("p (b r) -> p b r", r=512)[:, :, 0:260].rearrange("p b (g e) -> p b g e", e=65)
            for j in range(2):
                for wi, (slot, mi) in enumerate([(prv, 2 if i == 1 else 1), (cur, 0)]):
                    bk = 4 if wi == 0 else 5
                    T(lambda e, j=j, slot=slot, bk=bk: e.matmul(bank(bk), lhsT=kT[:, slot, j, :], rhs=qT_v[:, 4 * j:4 * j + 4, :],
                                                                start=True, stop=True),
                      [rg_kT[slot], rg_qT], [rg_ps[bk]])
                    A(lambda e, wi=wi, bk=bk: e.activation(out=pT_v[wi], in_=bank(bk), func=AF.Exp), [rg_ps[bk]], [rg_pT[wi]])
                    V(lambda e, wi=wi, mi=mi: e.tensor_tensor(out=pT_v[wi].rearrange("p (g t) -> p g t", t=128),
                                                              in0=pT_v[wi].rearrange("p (g t) -> p g t", t=128),
                                                              in1=masks_b[:, mi, :].unsqueeze(1).to_broadcast([128, 4, 128]), op=ALU.mult),
                      [rg_pT[wi], rg["masks_b"]], [rg_pT[wi]])
                for g in range(4):
                    for wi, slot in enumerate([prv, cur]):
                        T(lambda e, j=j, g=g, wi=wi, slot=slot: e.matmul(PS[:, 512 * (1 + j) + 65 * g:512 * (1 + j) + 65 * (g + 1)],
                                                                         lhsT=pT_v[wi][:, 128 * g:128 * (g + 1)], rhs=vaug[:, slot, j, :],
                                                                         start=(wi == 0), stop=(wi == 1)),
                          [rg_pT[wi], rg_v[slot]], [rg_ps[1 + j]])
                yield
            V(lambda e: e.tensor_tensor(out=s_den[1].rearrange("p (b g) -> p b g", g=4), in0=o_all[:, :, :, 64],
                                        in1=esink[:].rearrange("p (b g) -> p b g", g=4), op=ALU.add),
              [rg_ps[1], rg_ps[2], rg["esink"]], [s_den[0]])
            V(lambda e: e.reciprocal(out=s_rden[1], in_=s_den[1]), [s_den[0]], [s_rden[0]])
            V(lambda e: e.tensor_tensor(out=attn_v.rearrange("p (b g d) -> p b g d", g=4, d=64), in0=o_all[:, :, :, 0:64],
                                        in1=s_rden[1].rearrange("p (b g) -> p b g", g=4).unsqueeze(3).to_broadcast([128, 2, 4, 64]), op=ALU.mult),
              [rg_ps[1], rg_ps[2], s_rden[0]], [rg_attn])
            if DEBUG:
                P.dma("sync", lambda e, i=i: e.dma_start(out=dbg["attn"][128 * (i - 1):128 * i, :], in_=attn_v), sem_dbg, reads=[rg_attn])
            rms_stats(attn_v, rg_attn, c1_v, rg_c1, 512, s_sa, s_rsa, s_ra)
            V(lambda e: e.scalar_tensor_tensor(out=attn_n_v, in0=attn_v, scalar=s_ra[1], in1=attn_g[:], op0=ALU.mult, op1=ALU.mult),
              [rg_attn, s_ra[0], rg["attn_g"]], [rg_attn_n])
            for k in range(4):
                T(lambda e, k=k: e.transpose(out=p0[:, k, :], in_=attn_n_v[:, 128 * k:128 * (k + 1)], identity=ident_b[:]),
                  [rg_attn_n, rg["ident_b"]], [rg_ps[0]])
            A(lambda e: e.copy(out=attnT[:], in_=p0[:, 0:4, :]), [rg_ps[0]], [rg["attnT"]])
            yield
            for ct in range(4):
                V(lambda e, ct=ct: e.tensor_scalar(out=c13[:, ct, :], in0=pbuf[:, ct, 0:128], scalar1=convw[:, ct, 0:1], scalar2=None, op0=ALU.mult),
                  [rg["pbuf"], rg["convw"]], [rg_c1])
            for kk in (1, 2):
                for ct in range(4):
                    V(lambda e, ct=ct, kk=kk: e.scalar_tensor_tensor(out=c13[:, ct, :], in0=pbuf[:, ct, kk:kk + 128], scalar=convw[:, ct, kk:kk + 1],
                                                                      in1=c13[:, ct, :], op0=ALU.mult, op1=ALU.add),
                      [rg["pbuf"], rg["convw"], rg_c1], [rg_c1])
            V(lambda e: e.tensor_tensor(out=cv3, in0=cb_ps, in1=c13, op=ALU.mult), [rg_ps[3], rg_c1], [rg_cv])
            V(lambda e: e.tensor_tensor(out=cu3, in0=cv3, in1=cv3, op=ALU.mult), [rg_cv], [rg_cu])
            for ct in range(4):
                T(lambda e, ct=ct: e.matmul(bank(5, 1), lhsT=cu3[:, ct, :], rhs=ones_f[:, 0:1], start=(ct == 0), stop=(ct == 3)),
                  [rg_cu, rg["ones_f"]], [rg_ps[5]])
            A(lambda e: e.activation(out=s_rsc[1], in_=bank(5, 1), func=AF.Sqrt, scale=1.0 / 512, bias=eps_ap), [rg_ps[5], rg_cst], [s_rsc[0]])
            V(lambda e: e.reciprocal(out=s_rc[1], in_=s_rsc[1]), [s_rsc[0]], [s_rc[0]])
            cvT3 = cvT_v.rearrange("p (c t) -> p c t", t=128)
            for ct in range(4):
                V(lambda e, ct=ct: e.tensor_scalar(out=cvT3[:, ct, :], in0=cv3[:, ct, :], scalar1=convg[:, ct:ct + 1], scalar2=None, op0=ALU.mult),
                  [rg_cv, rg["convg"]], [rg_cvT])
            yield
            for c in range(2):
                for k in range(4):
                    T(lambda e, c=c, k=k: e.matmul(bank(1 + c), lhsT=attnT[:, k, :], rhs=w_out_sb[:, k, 512 * c:512 * (c + 1)],
                                                   start=(k == 0), stop=(k == 3)),
                      [rg["attnT"], rg["w_out"]], [rg_ps[1 + c]])
            for c in range(2):
                for ct in range(4):
                    T(lambda e, c=c, ct=ct: e.matmul(bank(3 + c), lhsT=cvT3[:, ct, :], rhs=w_out_sb[:, 4 + ct, 512 * c:512 * (c + 1)],
                                                     start=(ct == 0), stop=(ct == 3)),
                      [rg_cvT, rg["w_out"]], [rg_ps[3 + c]])
            A(lambda e: e.copy(out=BY[:], in_=PS[:, 512:1536]), [rg_ps[1], rg_ps[2]], [rg["BY"]])
            V(lambda e: e.scalar_tensor_tensor(out=BY[:], in0=PS[:, 1536:2560], scalar=s_rc[1], in1=BY[:], op0=ALU.mult, op1=ALU.add),
              [rg_ps[3], rg_ps[4], s_rc[0], rg["BY"]], [rg["BY"]])
            V(lambda e: e.tensor_tensor(out=BY[:], in0=BY[:], in1=gate1, op=ALU.mult), [rg["BY"], rg_gate1], [rg["BY"]])
            V(lambda e: e.tensor_tensor(out=BX, in0=BY[:], in1=BX, op=ALU.add), [rg["BY"], rBX], [rBX])
            row0 = 128 * (i - 1)
            if DEBUG:
                P.dma("sync", lambda e, row0=row0: e.dma_start(out=dbg["x1"][row0:row0 + 128, :], in_=BX), sem_dbg, reads=[rBX])
            yield

        def stage_R(i):
            xb = i % 2
            ib = i % 2
            BX = BX2[:, xb, :]
            rBX = rg_BX[xb]
            row0 = 128 * (i - 1)
            rms_stats(BX, rBX, BY[:], rg["BY"], D, s_ss, s_rs, s_rr)
            V(lambda e: e.scalar_tensor_tensor(out=BY[:], in0=BX, scalar=s_rr[1], in1=A2, op0=ALU.mult, op1=ALU.mult),
              [rBX, s_rr[0], rg_A2], [rg["BY"]])
            V(lambda e: e.tensor_tensor(out=BY[:], in0=BY[:], in1=shift2, op=ALU.add), [rg["BY"], rg_shift2], [rg["BY"]])
            BH, rBH = BH2[:, i % 2, :], rg_BH[i % 2]
            A(lambda e: e.copy(out=BH, in_=BY[:]), [rg["BY"]], [rBH])
            if DEBUG:
                P.dma("sync", lambda e, row0=row0: e.dma_start(out=dbg["h2"][row0:row0 + 128, :], in_=BY[:]), sem_dbg, reads=[rg["BY"]])
            yield
            p0 = bank_bf(0).rearrange("p (k t) -> p k t", t=128)
            for kt in range(8):
                T(lambda e, kt=kt: e.transpose(out=p0[:, kt, :], in_=BH[:, 128 * kt:128 * (kt + 1)], identity=ident_b[:]),
                  [rBH, rg["ident_b"]], [rg_ps[0]])
            A(lambda e: e.copy(out=BHT[:], in_=p0[:, 0:8, :]), [rg_ps[0]], [rg["BHT"]])
            yield
            wqb_r = wqb.rearrange("(kt p) n -> p kt n", p=128)

            def wq_load(c):
                P.dma("sync", lambda e, c=c: e.dma_start(out=wqc[:, c % 2, :, :], in_=wqb_r[:, :, 512 * c:512 * (c + 1)]), sem_wqc[c % 2],
                      reads=[rg_wqb, rg_wqb2], writes=[rg_wqc[c % 2]])

            wq_load(0)
            wq_load(1)
            for blk in range(16):
                c, bl = blk // 4, blk % 4
                for kt in range(8):
                    T(lambda e, blk=blk, kt=kt, c=c, bl=bl: e.matmul(PS[:, 512 + 128 * blk:512 + 128 * (blk + 1)], lhsT=wqc[:, c % 2, kt, 128 * bl:128 * (bl + 1)], rhs=BHT[:, kt, :],
                                                         start=(kt == 0), stop=(kt == 7)),
                      [rg_wqc[c % 2], rg["BHT"]], [rg_ps[1 + blk // 4]])
                if blk % 4 == 3:
                    if c + 2 < 4:
                        wq_load(c + 2)
                    yield
            A(lambda e: e.copy(out=S4B[:, 0:2048], in_=PS[:, 512:2560]), [rg_ps[1], rg_ps[2], rg_ps[3], rg_ps[4]], [rg_qpT])
            for blk in range(16):
                T(lambda e, blk=blk: e.matmul(PS[:, 512 + 128 * blk:512 + 128 * (blk + 1)], lhsT=qpT_v[:, blk, :], rhs=skT_sb[:, blk, :],
                                              start=True, stop=True),
                  [rg_qpT, rg["skT"]], [rg_ps[1 + blk // 4]])
            A(lambda e: e.copy(out=sc_v, in_=PS[:, 512:2560]), [rg_ps[1], rg_ps[2], rg_ps[3], rg_ps[4]], [rg_sc])
            yield
            mr4 = S8B[:, 0:512]
            rg_mr4 = R("S8B", 0, 2048)
            for g4 in range(4):
                blks = range(4 * g4, 4 * g4 + 4)
                sin_ = {b_: sc_v[:, 128 * b_:128 * (b_ + 1)] for b_ in blks}
                v0 = {b_: vals[:, 16 * b_:16 * b_ + 8] for b_ in blks}
                v1 = {b_: vals[:, 16 * b_ + 8:16 * b_ + 16] for b_ in blks}
                i0 = {b_: ids[:, 16 * b_:16 * b_ + 8] for b_ in blks}
                i1 = {b_: ids[:, 16 * b_ + 8:16 * b_ + 16] for b_ in blks}
                m_ = {b_: mr4[:, 128 * (b_ % 4):128 * (b_ % 4 + 1)] for b_ in blks}
                for b_ in blks:
                    V(lambda e, o_=v0[b_], x_=sin_[b_]: e.max(out=o_, in_=x_), [rg_sc], [rg["vals"]])
                for b_ in blks:
                    V(lambda e, o_=i0[b_], m__=v0[b_], x_=sin_[b_]: e.max_index(out=o_, in_max=m__, in_values=x_), [rg_sc, rg["vals"]], [rg["ids"]])
                for b_ in blks:
                    V(lambda e, o_=m_[b_], m__=v0[b_], x_=sin_[b_]: e.match_replace(out=o_, in_to_replace=m__, in_values=x_, imm_value=-1e30),
                      [rg_sc, rg["vals"]], [rg_mr4])
                for b_ in blks:
                    V(lambda e, o_=v1[b_], x_=m_[b_]: e.max(out=o_, in_=x_), [rg_mr4], [rg["vals"]])
                for b_ in blks:
                    V(lambda e, o_=i1[b_], m__=v1[b_], x_=m_[b_]: e.max_index(out=o_, in_max=m__, in_values=x_), [rg_mr4, rg["vals"]], [rg["ids"]])
                yield
            V(lambda e: e.tensor_copy(out=idsf, in_=ids), [rg["ids"]], [rg["idsf"]])
            vals4 = vals.rearrange("p (h s k) -> p h s k", s=2, k=16)
            idsf4 = idsf.rearrange("p (h s k) -> p h s k", s=2, k=16)
            cand4 = cand_v.rearrange("p (h a b) -> p h a b", a=16, b=16)
            V(lambda e: e.tensor_tensor(out=cand4, in0=vals4[:, :, 0, :].unsqueeze(3).to_broadcast([128, 8, 16, 16]),
                                        in1=vals4[:, :, 1, :].unsqueeze(2).to_broadcast([128, 8, 16, 16]), op=ALU.add),
              [rg["vals"]], [rg_cand])
            for h in range(8):
                c_in = cand_v[:, 256 * h:256 * (h + 1)]
                t0, t1 = top[:, 16 * h:16 * h + 8], top[:, 16 * h + 8:16 * h + 16]
                q0, q1 = pos[:, 16 * h:16 * h + 8], pos[:, 16 * h + 8:16 * h + 16]
                V(lambda e, c_in=c_in, t0=t0: e.max(out=t0, in_=c_in), [rg_cand], [rg["top"]])
                V(lambda e, c_in=c_in, t0=t0, q0=q0: e.max_index(out=q0, in_max=t0, in_values=c_in), [rg_cand, rg["top"]], [rg["pos"]])
                V(lambda e, c_in=c_in, t0=t0: e.match_replace(out=mr, in_to_replace=t0, in_values=c_in, imm_value=-1e30),
                  [rg_cand, rg["top"]], [rg["mr"]])
                V(lambda e, t1=t1: e.max(out=t1, in_=mr), [rg["mr"]], [rg["top"]])
                V(lambda e, t1=t1, q1=q1: e.max_index(out=q1, in_max=t1, in_values=mr), [rg["mr"], rg["top"]], [rg["pos"]])
                if h % 4 == 3:
                    yield
            V(lambda e: e.tensor_single_scalar(out=k1u, in_=pos[:], scalar=4, op=ALU.logical_shift_right), [rg["pos"]], [rg_k1u])
            V(lambda e: e.tensor_single_scalar(out=k2u, in_=pos[:], scalar=15, op=ALU.bitwise_and), [rg["pos"]], [rg_k2u])
            V(lambda e: e.tensor_copy(out=k1f, in_=k1u), [rg_k1u], [rg_k1f])
            V(lambda e: e.tensor_copy(out=k2f, in_=k2u), [rg_k2u], [rg_k2f])
            oh4 = oh_v.rearrange("p (h j k) -> p h j k", j=16, k=16)
            for side, kf, rk in ((0, k1f, rg_k1f), (1, k2f, rg_k2f)):
                V(lambda e, kf=kf: e.tensor_tensor(out=oh4, in0=kf.rearrange("p (h j) -> p h j", j=16).unsqueeze(3).to_broadcast([128, 8, 16, 16]),
                                                   in1=iota16[:].unsqueeze(1).unsqueeze(1).to_broadcast([128, 8, 16, 16]), op=ALU.is_equal),
                  [rk, rg["iota16"]], [rg_oh])
                V(lambda e, side=side: e.tensor_tensor(out=oh4, in0=oh4, in1=idsf4[:, :, side, :].unsqueeze(2).to_broadcast([128, 8, 16, 16]), op=ALU.mult),
                  [rg_oh, rg["idsf"]], [rg_oh])
                V(lambda e, side=side: e.tensor_reduce(out=isel[:, side, :], in_=oh_v.rearrange("p (a k) -> p a k", k=16), axis=AX.X, op=ALU.add),
                  [rg_oh], [rg_isel])
            yield
            V(lambda e: e.scalar_tensor_tensor(out=idxf, in0=isel[:, 0, :], scalar=128.0, in1=isel[:, 1, :], op0=ALU.mult, op1=ALU.add),
              [rg_isel], [rg_idxf])
            V(lambda e, ib=ib: e.tensor_copy(out=idxi2[:, ib, :], in_=idxf), [rg_idxf], [rg_idxi[ib]])
            top3 = top[:].rearrange("p (h j) -> p h j", j=16)
            V(lambda e: e.tensor_tensor(out=esm[:].rearrange("p (h j) -> p h j", j=16), in0=top3,
                                        in1=top3[:, :, 0:1].to_broadcast([128, 8, 16]), op=ALU.subtract),
              [rg["top"]], [rg["esm"]])
            A(lambda e: e.activation(out=esm[:], in_=esm[:], func=AF.Exp), [rg["esm"]], [rg["esm"]])
            V(lambda e: e.tensor_reduce(out=s_sm[1], in_=esm[:].rearrange("p (h j) -> p h j", j=16), axis=AX.X, op=ALU.add),
              [rg["esm"]], [s_sm[0]])
            V(lambda e: e.reciprocal(out=s_rsm[1], in_=s_sm[1]), [s_sm[0]], [s_rsm[0]])
            V(lambda e, ib=ib: e.tensor_tensor(out=gsm2[:, ib, :].rearrange("p (h j) -> p h j", j=16), in0=esm[:].rearrange("p (h j) -> p h j", j=16),
                                        in1=s_rsm[1].unsqueeze(2).to_broadcast([128, 8, 16]), op=ALU.mult),
              [rg["esm"], s_rsm[0]], [rg_gsm[ib]])
            if DEBUG:
                P.dma("sync", lambda e, row0=row0, ib=ib: e.dma_start(out=dbg["idx"][row0:row0 + 128, :], in_=idxi2[:, ib, :]), sem_dbg, reads=[rg_idxi[ib]])
                P.dma("sync", lambda e, row0=row0, ib=ib: e.dma_start(out=dbg["g"][row0:row0 + 128, :], in_=gsm2[:, ib, :]), sem_dbg, reads=[rg_gsm[ib]])
            yield

        def drain(gen):
            if gen is None:
                return
            for _ in gen:
                pass

        def step(gen):
            if gen is None:
                return None
            try:
                next(gen)
                return gen
            except StopIteration:
                return None

        def chain(*gens):
            for g_ in gens:
                for _ in g_:
                    yield

        def stage_UV(i, other):
            ib = i % 2
            BH, rBH = BH2[:, ib, :], rg_BH[ib]
            def tail(slot, r):
                db = slot % 4
                V(lambda e, slot=slot, db=db, ib=ib: e.tensor_scalar(out=dg[:, db, :], in0=ident_b[:], scalar1=gav[:, slot:slot + 1],
                                                                      scalar2=gsm2[:, ib, slot:slot + 1], op0=ALU.mult, op1=ALU.mult),
                  [rg_gav[slot], rg_gsm[ib], rg["ident_b"]], [rg_dg[db]])
                for hf in range(2):
                    T(lambda e, r=r, db=db, hf=hf, slot=slot: e.matmul(PS[:, 3072 + 512 * hf:3072 + 512 * (hf + 1)], lhsT=dg[:, db, :],
                                                                        rhs=ring_bf[:, r, 1024 + 512 * hf:1024 + 512 * (hf + 1)],
                                                                        start=(slot == 0), stop=(slot == 127)),
                      [rg_dg[db], rg_ring[r]], [rg_ps[6 + hf]])

            LAG = 2
            pend = []
            for slot in range(128):
                r = gather_uv(slot, ib)
                V(lambda e, r=r: e.tensor_tensor(out=ring_bf[:, r, 0:1024], in0=ring_bf[:, r, 0:1024], in1=BH, op=ALU.mult),
                  [rg_ring[r], rBH], [rg_ring[r]])
                A(lambda e, r=r, slot=slot: e.activation(out=ring_bf[:, r, 0:1024], in_=ring_bf[:, r, 0:1024], func=AF.Copy, accum_out=av[:, slot:slot + 1]),
                  [rg_ring[r]], [rg_ring[r], rg_av[slot]])
                A(lambda e, slot=slot: e.activation(out=gav[:, slot:slot + 1], in_=av[:, slot:slot + 1], func=AF.Gelu), [rg_av[slot]], [rg_gav[slot]])
                pend.append((slot, r))
                if len(pend) > LAG:
                    tail(*pend.pop(0))
                if slot % 2 == 1:
                    other = step(other)
            while pend:
                tail(*pend.pop(0))
            drain(other)

        def stage_fin(i):
            xb = i % 2
            BX = BX2[:, xb, :]
            rBX = rg_BX[xb]
            row0 = 128 * (i - 1)
            fin_v = S8B[:, 0:1024]
            V(lambda e: e.tensor_tensor(out=fin_v, in0=PS[:, 3072:4096], in1=gate2, op=ALU.mult), [rg_ps[6], rg_ps[7], rg_gate2], [rg_fin])
            V(lambda e: e.tensor_tensor(out=BX, in0=fin_v, in1=BX, op=ALU.add), [rg_fin, rBX], [rBX])
            P.dma("sync", lambda e, row0=row0: e.dma_start(out=out[row0:row0 + 128, :], in_=BX), sem_o, reads=[rBX])

        if STAGE >= 1:
            drain(stage_A(0))
            last = N_TILES_RUN
            drain(stage_A(1))
            if STAGE >= 2:
                drain(stage_R(1))
            for i in range(1, last + 1):
                nxt = None
                if i < last:
                    nxt = chain(stage_A(i + 1), stage_R(i + 1)) if STAGE >= 2 else stage_A(i + 1)
                if STAGE >= 3:
                    stage_UV(i, nxt)
                    stage_fin(i)
                else:
                    drain(nxt)

        def tail_waits():
            w = [(sem_o, P.dcnt.get(sem_o, 0))] + [(sw, P.dcnt.get(sw, 0)) for sw in sem_w] + [(sv, P.dcnt.get(sv, 0)) for sv in sem_cvs]
            if DEBUG:
                w.append((sem_dbg, P.dcnt.get(sem_dbg, 0)))
            return [x for x in w if x[1] > 0]

        P.emit(block, tail_waits)
    return nc


def make_in_maps(x, c, positions, ada_w, ada_b, norm1_g, w_in, q_norm_g, k_norm_g, sinks, conv_w,
                 attn_out_g, conv_out_g, w_out, norm2_g, peer_wq, peer_subkeys, peer_u, peer_v):
    f = lambda a: np.ascontiguousarray(np.asarray(a), dtype=np.float32)
    x = f(x); c = f(c); positions = np.asarray(positions).astype(np.int32)
    S = x.shape[1]
    cpb = N_CORES // x.shape[0]
    assert S // cpb == TOK
    crow = np.zeros((1, 3584), np.float32)
    crow[0, 0:1024] = f(norm1_g)[0]
    crow[0, 1024:2048] = f(norm2_g)[0]
    crow[0, 2048:2560] = f(attn_out_g)[0]
    crow[0, 2560:3072] = np.tile(f(q_norm_g)[0], 8)
    crow[0, 3072:3200] = np.tile(f(k_norm_g)[0], 2)
    crow[0, 3200:3208] = f(sinks)[0]
    crow[0, 3208:3240] = (np.float32(10000.0) ** (-np.arange(0, 64, 2, dtype=np.float32) / np.float32(64))).astype(np.float32)
    convw = np.ascontiguousarray(f(conv_w)[0].T.reshape(4, 128, 3).transpose(1, 0, 2))
    convg = np.ascontiguousarray(f(conv_out_g)[0].reshape(4, 128).T)
    skT = np.ascontiguousarray(f(peer_subkeys)[0].reshape(16, 128, 128).transpose(2, 0, 1))
    ident = np.eye(128, dtype=np.float32)
    tk = np.arange(128)[:, None]
    tq = np.arange(128)[None, :]
    m_cur = (tk <= tq).astype(np.float32)
    m_prev = (tk > tq).astype(np.float32)
    shared = dict(ada_w=f(ada_w)[0], ada_b=f(ada_b), crow=crow, convw=convw, convg=convg, w_in=f(w_in)[0], w_out=f(w_out)[0],
                  wq=f(peer_wq)[0], skT=skT, peer_uv=np.ascontiguousarray(np.concatenate([f(peer_u)[0], f(peer_v)[0]], axis=1)), ident=ident)
    maps = []
    for core in range(N_CORES):
        b, j = core // cpb, core % cpb
        t0 = j * TOK
        xh = np.zeros((NTE * 128, D), np.float32)
        ph = np.zeros((NTE * 128,), np.int32)
        xh[128:] = x[b, t0:t0 + TOK]
        ph[128:] = positions[b, t0:t0 + TOK]
        first = (j == 0)
        if not first:
            xh[:128] = x[b, t0 - 128:t0]
            ph[:128] = positions[b, t0 - 128:t0]
        masks = np.stack([m_cur, m_prev, np.zeros_like(m_prev) if first else m_prev], axis=1)
        m = dict(shared)
        m.update(xh=xh, posi=np.ascontiguousarray(ph.reshape(NTE, 128).T),
                 ccol=np.ascontiguousarray(c[b].reshape(8, 128).T),
                 hflag=np.full((128, 1), 0.0 if first else 1.0, np.float32),
                 masks=np.ascontiguousarray(masks.astype(np.float32)))
        maps.append(m)
    return maps


_NC_CACHE = {}


def kernel(**inputs):
    maps = make_in_maps(**inputs)
    key = (DEBUG, N_TILES_RUN, STAGE, NEXP, BF16_TABLE)
    if key not in _NC_CACHE:
        _NC_CACHE[key] = build_nc()
    nc = _NC_CACHE[key]
    ncores = getattr(kernel, 'ncores', N_CORES)
    if NEXP != 16384:
        for m in maps:
            m['peer_uv'] = m['peer_uv'][:NEXP]
    res = run_bass_kernel_spmd(nc, maps[:ncores], core_ids=list(range(ncores)))
    if ncores != N_CORES:
        kernel.last = res.results
        return None
    B, S = np.asarray(inputs["x"]).shape[:2]
    outp = np.concatenate([r["out"] for r in res.results], axis=0).reshape(B, S, D).astype(np.float32)
    if DEBUG:
        kernel.last = res.results
    return outp
```

```python
import math
import os
from contextlib import ExitStack

import numpy as np
import concourse.bass as bass
import concourse.mybir as mybir
from concourse.bass_utils import run_bass_kernel_spmd

F32 = mybir.dt.float32
BF16 = mybir.dt.bfloat16
I32 = mybir.dt.int32
U32 = mybir.dt.uint32
ALU = mybir.AluOpType
AF = mybir.ActivationFunctionType
AX = mybir.AxisListType

N_CORES = 8
D = 1024
NT = 16
TOK = NT * 128
NTE = NT + 1
EPS = 1e-6
NR = 12
ENGS = ["sync", "scalar", "vector", "gpsimd", "tensor"]
DEBUG = False
N_TILES_RUN = NT
STAGE = 3
NEXP = 16384
BF16_TABLE = True


class Region:
    _all = {}

    def __init__(self, phys, lo, hi):
        self.phys, self.lo, self.hi = phys, lo, hi
        self.w = None
        self.r = []
        Region._all.setdefault(phys, []).append(self)
        self._ov = None

    def ov(self):
        return [o for o in Region._all[self.phys] if o.lo < self.hi and self.lo < o.hi]


class Op:
    __slots__ = ("eng", "fn", "deps", "is_dma", "dsem", "dval", "needs_inc", "seq", "waits")

    def __init__(self, eng, fn, deps, is_dma=False, dsem=None, dval=0):
        self.eng, self.fn, self.deps = eng, fn, deps
        self.is_dma, self.dsem, self.dval = is_dma, dsem, dval
        self.needs_inc = False
        self.seq = 0
        self.waits = []


class Prog:
    def __init__(self, sems):
        self.sems = sems
        self.ops = []
        self.dcnt = {}

    def _deps(self, reads, writes, extra):
        deps = {}
        for r in reads:
            for o in r.ov():
                if o.w is not None:
                    deps[o.w] = "raw"
        for w in writes:
            for o in w.ov():
                if o.w is not None and o.w not in deps:
                    deps[o.w] = "war"
                for rd in o.r:
                    if rd not in deps:
                        deps[rd] = "war"
        for e in extra:
            if e is not None:
                deps[e] = "raw"
        return deps

    def _commit(self, op, reads, writes):
        for r in reads:
            r.r.append(op)
        for w in writes:
            w.w = op
            w.r = []

    def op(self, eng, fn, reads=(), writes=(), extra=()):
        op = Op(eng, fn, self._deps(reads, writes, extra))
        self._commit(op, reads, writes)
        self.ops.append(op)
        return op

    def dma(self, eng, fn, sem, reads=(), writes=(), extra=()):
        self.dcnt[sem] = self.dcnt.get(sem, 0) + 16
        op = Op(eng, fn, self._deps(reads, writes, extra), True, sem, self.dcnt[sem])
        self._commit(op, reads, writes)
        self.ops.append(op)
        return op

    def finalize(self):
        for c in self.ops:
            for d, kind in c.deps.items():
                if d.is_dma:
                    c.waits.append(d)
                    continue
                if d.eng == c.eng and not c.is_dma:
                    if c.eng == "tensor":
                        continue
                    if kind != "raw":
                        continue
                d.needs_inc = True
                c.waits.append(d)
        cnt = {e: 0 for e in ENGS}
        for o in self.ops:
            if o.needs_inc and not o.is_dma:
                cnt[o.eng] += 1
                o.seq = cnt[o.eng]
        self.final_cnt = cnt

    def emit(self, block, tail_waits):
        self.finalize()
        sems = self.sems
        per = {e: [o for o in self.ops if o.eng == e] for e in ENGS}

        def run_engine(e, name):
            waited = {}
            for o in per[name]:
                for d in o.waits:
                    if d.is_dma:
                        s, v = d.dsem, d.dval
                    else:
                        s, v = sems[d.eng], d.seq
                    k = id(s)
                    if waited.get(k, 0) >= v:
                        continue
                    waited[k] = v
                    e.wait_ge(s, v)
                ins = o.fn(e)
                if o.is_dma:
                    ins.then_inc(o.dsem, 16)
                elif o.needs_inc:
                    ins.then_inc(sems[o.eng], 1)
            if name == "sync":
                for (s, v) in tail_waits():
                    e.wait_ge(s, v)

        @block.sync
        def _(e):
            run_engine(e, "sync")

        @block.scalar
        def _(e):
            run_engine(e, "scalar")

        @block.vector
        def _(e):
            run_engine(e, "vector")

        @block.gpsimd
        def _(e):
            run_engine(e, "gpsimd")

        @block.tensor
        def _(e):
            run_engine(e, "tensor")


def _split_2pi():
    two_pi = 2.0 * math.pi
    c1 = 6.28125
    r = two_pi - c1
    c2 = float(np.float32(r))
    m, e = math.frexp(c2)
    c2 = math.ldexp(round(m * 2048) / 2048, e)
    c3 = float(np.float32(two_pi - c1 - c2))
    return c1, c2, c3


def build_nc():
    Region._all = {}
    nc = bass.Bass("TRN2", target_bir_lowering=False)

    def din(name, shape, dt=F32):
        return nc.dram_tensor(name, list(shape), dt, kind="ExternalInput").ap()

    xh = din("xh", [NTE * 128, D])
    posi_d = din("posi", [128, NTE], I32)
    ccol_d = din("ccol", [128, 8])
    ada_w = din("ada_w", [D, 6 * D])
    ada_b = din("ada_b", [1, 6 * D])
    crow = din("crow", [1, 3584])
    convw_d = din("convw", [128, 4, 3])
    convg_d = din("convg", [128, 4])
    hflag_d = din("hflag", [128, 1])
    w_in = din("w_in", [D, 2304])
    w_out = din("w_out", [D, D])
    wq = din("wq", [D, 2048])
    skT_d = din("skT", [128, 16, 128])
    peer_uv = din("peer_uv", [NEXP, 2 * D])
    uvb = nc.dram_tensor("uvb", [NEXP, 2 * D], BF16, kind="Internal").ap() if BF16_TABLE else None
    wqb = nc.dram_tensor("wqb", [D, 2048], BF16, kind="Internal").ap()
    ident_d = din("ident", [128, 128])
    masks_d = din("masks", [128, 3, 128])
    out = nc.dram_tensor("out", [TOK, D], F32, kind="ExternalOutput").ap()
    dbg = {}
    if DEBUG:
        dbg["x1"] = nc.dram_tensor("dbg_x1", [TOK, D], F32, kind="ExternalOutput").ap()
        dbg["h2"] = nc.dram_tensor("dbg_h2", [TOK, D], F32, kind="ExternalOutput").ap()
        dbg["idx"] = nc.dram_tensor("dbg_idx", [TOK, 128], I32, kind="ExternalOutput").ap()
        dbg["g"] = nc.dram_tensor("dbg_g", [TOK, 128], F32, kind="ExternalOutput").ap()
        dbg["attn"] = nc.dram_tensor("dbg_attn", [TOK, 512], F32, kind="ExternalOutput").ap()
        dbg["mod"] = nc.dram_tensor("dbg_mod", [128, 6, D], F32, kind="ExternalOutput").ap()
        dbg["cs"] = nc.dram_tensor("dbg_cs", [128, 2, NTE, 32], F32, kind="ExternalOutput").ap()

    with ExitStack() as es:
        E = es.enter_context

        def sb(name, shape, dt=F32):
            return E(nc.sbuf_tensor("sb_" + name, list(shape), dt))

        ident_f = sb("ident_f", [128, 128])
        ident_b = sb("ident_b", [128, 128], BF16)
        ones_f = sb("ones_f", [128, 128])
        masks_b = sb("masks_b", [128, 3, 128], BF16)
        ccol = sb("ccol", [128, 8])
        cact = sb("cact", [128, 8])
        posi = sb("posi", [128, NTE], I32)
        posf = sb("posf", [128, NTE])
        invf = sb("invf", [128, 32])
        cos_t = sb("cos_t", [128, NTE, 32])
        sin_t = sb("sin_t", [128, NTE, 32])
        qkg = sb("qkg", [128, 640])
        attn_g = sb("attn_g", [128, 512])
        sinkb = sb("sinkb", [128, 8])
        esink = sb("esink", [128, 8])
        convw = sb("convw", [128, 4, 3])
        convg = sb("convg", [128, 4])
        hflag = sb("hflag", [128, 1])
        cst = sb("cst", [128, 4])
        iota16 = sb("iota16", [128, 16])
        mod6 = sb("mod6", [128, 6, D])
        w_in_sb = sb("w_in_sb", [128, 8, 2304], BF16)
        w_out_sb = sb("w_out_sb", [128, 8, D], BF16)
        wqc = sb("wqc", [128, 2, 8, 512], BF16)
        skT_sb = sb("skT_sb", [128, 16, 128], BF16)
        ring = sb("ring", [128, NR, D])
        BX2 = sb("BX2", [128, 2, D])
        BH2 = sb("BH2", [128, 2, D], BF16)
        BHT = sb("BHT", [128, 8, 128], BF16)
        S8A = sb("S8A", [128, 2048])
        S8B = sb("S8B", [128, 2048])
        S4A = sb("S4A", [128, 2048], BF16)
        S4B = sb("S4B", [128, 2048], BF16)
        BY = sb("BY", [128, D])
        dg = sb("dg", [128, 4, 128], BF16)
        attnT = sb("attnT", [128, 4, 128], BF16)
        kT = sb("kT", [64, 2, 2, 128], BF16)
        vaug = sb("vaug", [128, 2, 2, 65], BF16)
        pbuf = sb("pbuf", [128, 4, 130])
        st = sb("st", [128, 64])
        top = sb("top", [128, 128])
        pos = sb("pos", [128, 128], U32)
        idxi2 = sb("idxi2", [128, 2, 128], I32)
        gsm2 = sb("gsm2", [128, 2, 128])
        gav = sb("gav", [128, 128])
        esm = sb("esm", [128, 128])
        av = sb("av", [128, 128])
        PS = E(nc.psum_tensor("PS", [128, 4096], F32))

        sems = {k: E(nc.semaphore("s_" + k)) for k in ENGS}
        sem_c = E(nc.semaphore("d_c"))
        sem_w = [E(nc.semaphore("d_w%d" % i)) for i in range(4)]
        sem_ada = [E(nc.semaphore("d_ada%d" % i)) for i in range(2)]
        sem_x = E(nc.semaphore("d_x"))
        sem_o = E(nc.semaphore("d_o"))
        sem_dbg = E(nc.semaphore("d_dbg"))
        sem_ring = [E(nc.semaphore("d_r%d" % i)) for i in range(NR)]
        sem_cvl = [E(nc.semaphore("d_cvl%d" % i)) for i in range(2)]
        sem_wqc = [E(nc.semaphore("d_wqc%d" % i)) for i in range(2)]
        sem_cvs = [E(nc.semaphore("d_cvs%d" % i)) for i in range(2)]
        block = E(nc.Block())
        P = Prog(sems)

        def R(phys, lo=0, hi=1 << 30):
            return Region(phys, lo, hi)

        rg = {}
        for nm in ["ident_f", "ident_b", "ones_f", "masks_b", "ccol", "cact", "posi", "posf", "invf",
                   "cos_t", "sin_t", "qkg", "attn_g", "sinkb", "esink", "convw", "convg", "hflag", "cst", "iota16",
                   "w_in", "w_out", "skT", "BHT", "BY", "attnT", "pbuf",
                   "top", "pos",
                   "esm"]:
            rg[nm] = R(nm)
        rg_mod = [R("mod6", 4096 * k, 4096 * (k + 1)) for k in range(6)]
        rg_ring = [R("ring", 4096 * k, 4096 * (k + 1)) for k in range(NR)]
        rg_adab = [rg["BY"], rg["BY"]]
        adab = BY[0:1, 0:512].rearrange("p (s n) -> p s n", n=256)
        rg_BX = [R("BX2", 4096 * k, 4096 * (k + 1)) for k in range(2)]
        rg_idxi = [R("idxi2", 512 * k, 512 * (k + 1)) for k in range(2)]
        rg_gsm = [R("gsm2", 512 * k, 512 * (k + 1)) for k in range(2)]
        rg_BH = [R("BH2", 2048 * k, 2048 * (k + 1)) for k in range(2)]
        rg_dg = [R("dg", 256 * k, 256 * (k + 1)) for k in range(4)]
        rg_uvb = R("uvb")
        rg_uvb2 = R("uvb2")
        rg_wqb, rg_wqb2 = R("wqb"), R("wqb2")
        rg_wqc = [R("wqc", 8192 * k, 8192 * (k + 1)) for k in range(2)]
        rg_av = [R("av", 4 * k, 4 * k + 4) for k in range(128)]
        rg_gav = [R("gav", 4 * k, 4 * k + 4) for k in range(128)]
        wv = ident_f
        rg_wv = [R("ident_f", 4 * k, 4 * k + 4) for k in range(128)]
        ring_bf = ring[:].bitcast(BF16)
        rg_kT = [R("kT", 512 * k, 512 * (k + 1)) for k in range(2)]
        rg_v = [R("vaug", 260 * k, 260 * (k + 1)) for k in range(2)]
        rg_st = {}

        def ST(name, lo, hi):
            rg_st[name] = (R("st", 4 * lo, 4 * hi), st[:, lo:hi])
            return rg_st[name]

        masks_f = S8A[:, 1600:1984].rearrange("p (m t) -> p m t", t=128)
        rg["masks_f"] = R("S8A", 6400, 7936)
        qk_v, sq_v, tmp_v = S8A[:, 0:640], S8A[:, 640:1280], S8A[:, 1280:1920]
        rg_qk, rg_sq, rg_tmp = R("S8A", 0, 2560), R("S8A", 2560, 5120), R("S8A", 5120, 7680)
        sc_v, rg_sc = S8A[:, 0:2048], R("S8A", 0, 8192)
        k1u, rg_k1u = S8A[:, 0:128].bitcast(U32), R("S8A", 0, 512)
        k2u, rg_k2u = S8A[:, 128:256].bitcast(U32), R("S8A", 512, 1024)
        k1f, rg_k1f = S8A[:, 256:384], R("S8A", 1024, 1536)
        k2f, rg_k2f = S8A[:, 384:512], R("S8A", 1536, 2048)
        isel, rg_isel = S8A[:, 512:768].rearrange("p (s n) -> p s n", n=128), R("S8A", 2048, 3072)
        idxf, rg_idxf = S8A[:, 768:896], R("S8A", 3072, 3584)
        attn_v, cu_v, c1_v, cv_v = S8B[:, 0:512], S8B[:, 512:1024], S8B[:, 1024:1536], S8B[:, 1536:2048]
        rg_attn, rg_cu, rg_c1, rg_cv = R("S8B", 0, 2048), R("S8B", 2048, 4096), R("S8B", 4096, 6144), R("S8B", 6144, 8192)
        cand_v, rg_cand = S8B[:, 0:2048], R("S8B", 0, 8192)
        rg_fin = R("S8B", 0, 4096)
        qr_v, attn_n_v, cvT_v = S4A[:, 0:640], S4A[:, 640:1152], S4A[:, 1152:1664]
        rg_qr, rg_attn_n, rg_cvT = R("S4A", 0, 1280), R("S4A", 1280, 2304), R("S4A", 2304, 3328)
        vals = S4A[:, 0:512].bitcast(F32)
        ids = S4A[:, 512:1024].bitcast(U32)
        idsf = S4A[:, 1024:1536].bitcast(F32)
        mr = S4A[:, 1536:2048].bitcast(F32)
        rg["vals"], rg["ids"], rg["idsf"], rg["mr"] = R("S4A", 0, 1024), R("S4A", 1024, 2048), R("S4A", 2048, 3072), R("S4A", 3072, 4096)
        qT_v = S4B[0:64, 0:1024].rearrange("p (h t) -> p h t", t=128)
        pT_v = [S4B[:, 1024:1536], S4B[:, 1536:2048]]
        rg_qT, rg_pT = R("S4B", 0, 2048), [R("S4B", 2048, 3072), R("S4B", 3072, 4096)]
        qpT_v, rg_qpT = S4B[:, 0:2048].rearrange("p (b t) -> p b t", t=128), R("S4B", 0, 4096)
        oh_v, rg_oh = S4B[:, 0:2048], R("S4B", 0, 4096)
        junk_v, rg_junk = S4B[:, 0:1024], R("S4B", 0, 2048)
        rg_ps = [R("PS", 2048 * b, 2048 * (b + 1)) for b in range(8)]

        def bank(b, n=512, off=0):
            return PS[:, 512 * b + off:512 * b + off + n]

        def bank_bf(b):
            return PS[:, 512 * b:512 * (b + 1)].bitcast(BF16)

        grp_c, grp_w = [], [[], [], [], []]

        def ld(dst, src, regs, sem=sem_c, eng="sync"):
            o = P.dma(eng, lambda e, dst=dst, src=src: e.dma_start(out=dst, in_=src), sem, writes=regs)
            grp_c.append(o)
            return o

        ld(ident_f[:], ident_d, [rg["ident_f"]])
        ld(masks_f, masks_d, [rg["masks_f"]])
        ld(ccol[:], ccol_d, [rg["ccol"]])
        ld(posi[:], posi_d, [rg["posi"]])
        ld(convw[:], convw_d, [rg["convw"]])
        ld(convg[:], convg_d, [rg["convg"]])
        ld(hflag[:], hflag_d, [rg["hflag"]])
        if 'b' not in os.environ.get('KSKIP', ''):
          ld(mod6[:, 4, :], crow[:, 0:1024].partition_broadcast(128), [rg_mod[4]])
          ld(mod6[:, 5, :], crow[:, 1024:2048].partition_broadcast(128), [rg_mod[5]])
          ld(attn_g[:], crow[:, 2048:2560].partition_broadcast(128), [rg["attn_g"]])
          ld(qkg[:], crow[:, 2560:3200].partition_broadcast(128), [rg["qkg"]])
          ld(sinkb[:], crow[:, 3200:3208].partition_broadcast(128), [rg["sinkb"]])
          ld(invf[:], crow[:, 3208:3240].partition_broadcast(128), [rg["invf"]])

        SKIP = os.environ.get('KSKIP', '')
        for kt in range(8 if 'w' not in SKIP else 0):
            for c in range(2):
                grp_w[0].append(P.dma("gpsimd", lambda e, kt=kt, c=c: e.dma_start(out=w_in_sb[:, kt, 1152 * c:1152 * (c + 1)],
                                                                   in_=w_in[128 * kt:128 * (kt + 1), 1152 * c:1152 * (c + 1)]),
                      sem_w[0], writes=[]))
        for kt in range(8 if ('w' not in SKIP and '1' not in SKIP) else 0):
            grp_w[1].append(P.dma("gpsimd", lambda e, kt=kt: e.dma_start(out=w_out_sb[:, kt, :], in_=w_out[128 * kt:128 * (kt + 1), :]),
                  sem_w[1], writes=[]))
        for b4 in range(4 if ('w' not in SKIP and '3' not in SKIP) else 0):
            grp_w[3].append(P.dma("gpsimd", lambda e, b4=b4: e.dma_start(out=skT_sb[:, 4 * b4:4 * b4 + 4, :], in_=skT_d[:, 4 * b4:4 * b4 + 4, :]),
                  sem_w[3], writes=[]))
        for o in grp_c:
            o.dval = P.dcnt[sem_c]
        for k, nm in enumerate(["w_in", "w_out", None, "skT"]):
            for o in grp_w[k]:
                o.dval = P.dcnt.get(sem_w[k], 0)
            if grp_w[k] and nm:
                rg[nm].w = grp_w[k][-1]

        V = lambda fn, reads=(), writes=(), extra=(): P.op("vector", fn, reads, writes, extra)
        A = lambda fn, reads=(), writes=(), extra=(): P.op("scalar", fn, reads, writes, extra)
        T = lambda fn, reads=(), writes=(), extra=(): P.op("tensor", fn, reads, writes, extra)
        G = lambda fn, reads=(), writes=(), extra=(): P.op("gpsimd", fn, reads, writes, extra)

        V(lambda e: e.tensor_copy(out=ident_b[:], in_=ident_f[:]), [rg["ident_f"]], [rg["ident_b"]])
        V(lambda e: e.tensor_copy(out=masks_b[:], in_=masks_f), [rg["masks_f"]], [rg["masks_b"]])
        V(lambda e: e.memset(ones_f[:], 1.0), [], [rg["ones_f"]])
        V(lambda e: e.memset(vaug[:], 1.0), [], rg_v)
        V(lambda e: e.memset(pbuf[:], 0.0), [], [rg["pbuf"]])
        rg_cst = R("cst")
        V(lambda e: e.memset(cst[:, 0:1], EPS), [], [rg_cst])
        V(lambda e: e.memset(cst[:, 1:2], math.pi / 2), [], [rg_cst])
        if 'i' not in SKIP:
          G(lambda e: e.iota(iota16[:], pattern=[[1, 16]], base=0, channel_multiplier=0, allow_small_or_imprecise_dtypes=True),
            [], [rg["iota16"]])

        if BF16_TABLE:
            stg = [ring_bf[:, 4 + 2 * k:6 + 2 * k, :] for k in range(2)]
            rg_stg = [R("ring", 16384 + 8192 * k, 16384 + 8192 * (k + 1)) for k in range(2)]
            cv_stores = []
            NCH = NEXP // 256

            NWQ = D // 256
            def cv_src_dst(c):
                if c < NWQ:
                    return wq[256 * c:256 * (c + 1), :], wqb[256 * c:256 * (c + 1), :]
                c2 = c - NWQ
                return peer_uv[256 * c2:256 * (c2 + 1), :], uvb[256 * c2:256 * (c2 + 1), :]

            def cv_load(c):
                src = cv_src_dst(c)[0].rearrange("(p j) n -> p j n", j=2)
                P.dma("gpsimd", lambda e, c=c, src=src: e.dma_start(out=stg[c % 2], in_=src), sem_cvl[c % 2], writes=[rg_stg[c % 2]])

            def cv_store(c):
                dst = cv_src_dst(c)[1].rearrange("(p j) n -> p j n", j=2)
                cv_stores.append(P.dma("gpsimd", lambda e, c=c, dst=dst: e.dma_start(out=dst, in_=stg[c % 2]), sem_cvs[c % 2], reads=[rg_stg[c % 2]]))

            NCH = NCH + NWQ
            cv_load(0)
            for c in range(1, NCH):
                cv_load(c)
                cv_store(c - 1)
            cv_store(NCH - 1)
            rg_wqb.w = cv_stores[NWQ - 1]
            rg_wqb2.w = cv_stores[NWQ - 2]
            rg_uvb.w = cv_stores[-1]
            rg_uvb2.w = cv_stores[-2]
        V(lambda e: e.tensor_copy(out=posf[:], in_=posi[:]), [rg["posi"]], [rg["posf"]])
        A(lambda e: e.activation(out=cact[:], in_=ccol[:], func=AF.Silu), [rg["ccol"]], [rg["cact"]])
        A(lambda e: e.activation(out=esink[:], in_=sinkb[:], func=AF.Exp), [rg["sinkb"]], [rg["esink"]])
        V(lambda e: e.tensor_scalar(out=qkg[:, 0:512], in0=qkg[:, 0:512], scalar1=0.125, scalar2=None, op0=ALU.mult),
          [rg["qkg"]], [rg["qkg"]])
        cact_rep = BX2[:, 0, :].rearrange("p (k m) -> p k m", m=128)
        V(lambda e: e.tensor_copy(out=cact_rep, in_=cact[:].unsqueeze(2).to_broadcast([128, 8, 128])),
          [rg["cact"]], [rg_BX[0]])

        C1, C2, C3 = _split_2pi()
        MAGIC = 12582912.0
        PI_LO = 3.1415925
        ang = S8A[:, 0:NTE * 32].rearrange("p (i j) -> p i j", j=32)
        nn = S8A[:, 1024:1024 + NTE * 32].rearrange("p (i j) -> p i j", j=32)
        rg_ang, rg_nn = R("S8A", 0, 4096), R("S8A", 4096, 8192)
        V(lambda e: e.tensor_tensor(out=ang, in0=posf[:].unsqueeze(2).to_broadcast([128, NTE, 32]),
                                    in1=invf[:].unsqueeze(1).to_broadcast([128, NTE, 32]), op=ALU.mult),
          [rg["posf"], rg["invf"]], [rg_ang])
        V(lambda e: e.tensor_scalar(out=nn, in0=ang, scalar1=1.0 / (2 * math.pi), scalar2=MAGIC, op0=ALU.mult, op1=ALU.add),
          [rg_ang], [rg_nn])
        V(lambda e: e.tensor_scalar(out=nn, in0=nn, scalar1=MAGIC, scalar2=None, op0=ALU.subtract), [rg_nn], [rg_nn])
        for cc in (C1, C2, C3):
            V(lambda e, cc=cc: e.scalar_tensor_tensor(out=ang, in0=nn, scalar=-cc, in1=ang, op0=ALU.mult, op1=ALU.add),
              [rg_nn, rg_ang], [rg_ang])
        V(lambda e: e.tensor_scalar(out=ang, in0=ang, scalar1=PI_LO, scalar2=-PI_LO, op0=ALU.min, op1=ALU.max), [rg_ang], [rg_ang])
        A(lambda e: e.activation(out=sin_t[:], in_=ang, func=AF.Sin), [rg_ang], [rg["sin_t"]])
        A(lambda e: e.activation(out=nn, in_=ang, func=AF.Abs), [rg_ang], [rg_nn])
        A(lambda e: e.activation(out=cos_t[:], in_=nn, func=AF.Sin, scale=-1.0, bias=cst[:, 1:2]), [rg_nn, rg_cst], [rg["cos_t"]])

        ada_r = ada_w.rearrange("(k p) n -> p k n", p=128)
        chunkbuf = [ring[:, 0:2, :].rearrange("p s (k n) -> p (s k) n", n=256) if False else None, None]
        chunkbuf = [ring[:, 2 * s:2 * s + 2, :].rearrange("p s (k n) -> p (s k) n", n=256) for s in range(2)]
        rg_chunk = [R("ring", 8192 * s, 8192 * (s + 1)) for s in range(2)]
        for c in range(24 if 'a' not in SKIP else 0):
            s = c % 2
            o1 = P.dma("sync", lambda e, c=c, s=s: e.dma_start(out=chunkbuf[s], in_=ada_r[:, :, 256 * c:256 * (c + 1)]),
                       sem_ada[s], writes=[rg_chunk[s]])
            o2 = P.dma("sync", lambda e, c=c, s=s: e.dma_start(out=adab[0:1, s, :], in_=ada_b[:, 256 * c:256 * (c + 1)]),
                       sem_ada[s], writes=[rg_adab[s]])
            o1.dval = o2.dval
            pb = bank(s, 256)
            for kt in range(8):
                T(lambda e, kt=kt, s=s, pb=pb: e.matmul(pb, lhsT=cact_rep[:, kt, :], rhs=chunkbuf[s][:, kt, :], start=(kt == 0), stop=False),
                  [rg_BX[0], rg_chunk[s]], [rg_ps[s]])
            T(lambda e, s=s, pb=pb: e.matmul(pb, lhsT=ones_f[0:1, :], rhs=adab[0:1, s, :], start=False, stop=True),
              [rg["ones_f"], rg_adab[s]], [rg_ps[s]])
            sec, off = c // 4, (c % 4) * 256
            if sec in (1, 4):
                k = 4 if sec == 1 else 5
                V(lambda e, k=k, off=off, pb=pb: e.scalar_tensor_tensor(out=mod6[:, k, off:off + 256], in0=pb, scalar=1.0,
                                                                         in1=mod6[:, k, off:off + 256], op0=ALU.add, op1=ALU.mult),
                  [rg_ps[s], rg_mod[k]], [rg_mod[k]])
            else:
                k = {0: 0, 2: 1, 3: 2, 5: 3}[sec]
                A(lambda e, k=k, off=off, pb=pb: e.copy(out=mod6[:, k, off:off + 256], in_=pb), [rg_ps[s]], [rg_mod[k]])
        shift1, gate1, shift2, gate2, A1, A2 = [mod6[:, k, :] for k in range(6)]
        rg_shift1, rg_gate1, rg_shift2, rg_gate2, rg_A1, rg_A2 = rg_mod
        setup_done = V(lambda e: e.memset(st[:, 60:61], 0.0), [rg_mod[k] for k in range(6)], [R("st", 240, 244)])


        Region._all["ring"] = list(rg_ring)
        if DEBUG:
            P.dma("sync", lambda e: e.dma_start(out=dbg["mod"], in_=mod6[:]), sem_dbg, reads=rg_mod)
            P.dma("sync", lambda e: e.dma_start(out=dbg["cs"][:, 0], in_=cos_t[:]), sem_dbg, reads=[rg["cos_t"]])
            P.dma("sync", lambda e: e.dma_start(out=dbg["cs"][:, 1], in_=sin_t[:]), sem_dbg, reads=[rg["sin_t"]])

        s_ss, s_rs, s_rr = ST("ss", 0, 1), ST("rs", 1, 2), ST("rr", 2, 3)
        s_ss10, s_rs10, s_rr10 = ST("ss10", 4, 14), ST("rs10", 14, 24), ST("rr10", 24, 34)
        s_den, s_rden = ST("den", 34, 42), ST("rden", 42, 50)
        s_sa, s_rsa, s_ra = ST("sa", 50, 51), ST("rsa", 51, 52), ST("ra", 52, 53)
        s_rsc, s_rc = ST("rsc", 53, 54), ST("rc", 54, 55)
        s_sm, s_rsm = ST("sm", 40, 48), ST("rsm", 48, 56)

        eps_ap = cst[:, 0:1]

        def rms_stats(src_ap, src_rg, scratch_ap, scratch_rg, n, s_sum, s_sq, s_r):
            A(lambda e: e.activation(out=scratch_ap, in_=src_ap, func=AF.Square, accum_out=s_sum[1]),
              [src_rg], [scratch_rg, s_sum[0]])
            A(lambda e: e.activation(out=s_sq[1], in_=s_sum[1], func=AF.Sqrt, scale=1.0 / n, bias=eps_ap),
              [s_sum[0], rg_cst], [s_sq[0]])
            V(lambda e: e.reciprocal(out=s_r[1], in_=s_sq[1]), [s_sq[0]], [s_r[0]])

        gcount = [0]
        vcount = [0]

        table = uvb if BF16_TABLE else peer_uv

        def gather_uv(slot, ib):
            r = gcount[0] % NR
            gcount[0] += 1
            P.dma("gpsimd", lambda e, r=r, slot=slot, ib=ib: e.indirect_dma_start(
                out=ring_bf[:, r, :], out_offset=None, in_=table,
                in_offset=bass.IndirectOffsetOnAxis(ap=idxi2[:, ib, slot:slot + 1], axis=0)),
                sem_ring[r], reads=[rg_idxi[ib], rg_uvb, rg_uvb2], writes=[rg_ring[r]], extra=[setup_done])
            return r

        def stage_A(i):
            cur, prv = i % 2, (i - 1) % 2
            xb = i % 2
            BX = BX2[:, xb, :]
            rBX = rg_BX[xb]
            P.dma("sync", lambda e, i=i: e.dma_start(out=BX, in_=xh[128 * i:128 * (i + 1), :]), sem_x, writes=[rBX])
            rms_stats(BX, rBX, BY[:], rg["BY"], D, s_ss, s_rs, s_rr)
            V(lambda e: e.scalar_tensor_tensor(out=BY[:], in0=BX, scalar=s_rr[1], in1=A1, op0=ALU.mult, op1=ALU.mult),
              [rBX, s_rr[0], rg_A1], [rg["BY"]])
            BH, rBH = BH2[:, i % 2, :], rg_BH[i % 2]
            V(lambda e: e.tensor_tensor(out=BH, in0=BY[:], in1=shift1, op=ALU.add), [rg["BY"], rg_shift1], [rBH])
            yield
            p0 = bank_bf(0).rearrange("p (k t) -> p k t", t=128)
            for kt in range(8):
                T(lambda e, kt=kt: e.transpose(out=p0[:, kt, :], in_=BH[:, 128 * kt:128 * (kt + 1)], identity=ident_b[:]),
                  [rBH, rg["ident_b"]], [rg_ps[0]])
            A(lambda e: e.copy(out=BHT[:], in_=p0[:, 0:8, :]), [rg_ps[0]], [rg["BHT"]])
            yield
            for c in range(2):
                for kt in range(8):
                    T(lambda e, c=c, kt=kt: e.matmul(bank(1 + c, 384), lhsT=BHT[:, kt, :], rhs=w_in_sb[:, kt, 384 * c:384 * (c + 1)],
                                                     start=(kt == 0), stop=(kt == 7)),
                      [rg["BHT"], rg["w_in"]], [rg_ps[1 + c]])
            yield
            for nt_ in range(12):
                for kt in range(8):
                    T(lambda e, nt_=nt_, kt=kt: e.matmul(PS[:, 1536 + 128 * nt_:1536 + 128 * (nt_ + 1)],
                                                         lhsT=w_in_sb[:, kt, 768 + 128 * nt_:768 + 128 * (nt_ + 1)], rhs=BHT[:, kt, :],
                                                         start=(kt == 0), stop=(kt == 7)),
                      [rg["BHT"], rg["w_in"]], [rg_ps[3 + nt_ // 4]])
                if nt_ % 3 == 2:
                    yield
            A(lambda e: e.copy(out=qk_v[:, 0:384], in_=bank(1, 384)), [rg_ps[1]], [rg_qk])
            A(lambda e: e.copy(out=qk_v[:, 384:640], in_=bank(2, 256)), [rg_ps[2]], [rg_qk])
            A(lambda e, cur=cur: e.copy(out=vaug[:, cur, :, 0:64], in_=bank(2, 128, 256).rearrange("p (j d) -> p j d", d=64)),
              [rg_ps[2]], [rg_v[cur]])
            V(lambda e: e.tensor_tensor(out=sq_v, in0=qk_v, in1=qk_v, op=ALU.mult), [rg_qk], [rg_sq])
            V(lambda e: e.tensor_reduce(out=s_ss10[1], in_=sq_v.rearrange("p (h d) -> p h d", d=64), axis=AX.X, op=ALU.add),
              [rg_sq], [s_ss10[0]])
            A(lambda e: e.activation(out=s_rs10[1], in_=s_ss10[1], func=AF.Sqrt, scale=1.0 / 64, bias=eps_ap),
              [s_ss10[0], rg_cst], [s_rs10[0]])
            V(lambda e: e.reciprocal(out=s_rr10[1], in_=s_rs10[1]), [s_rs10[0]], [s_rr10[0]])
            V(lambda e: e.tensor_tensor(out=tmp_v.rearrange("p (h d) -> p h d", d=64), in0=qk_v.rearrange("p (h d) -> p h d", d=64),
                                        in1=s_rr10[1].unsqueeze(2).to_broadcast([128, 10, 64]), op=ALU.mult),
              [rg_qk, s_rr10[0]], [rg_tmp])
            V(lambda e: e.tensor_tensor(out=qk_v, in0=tmp_v, in1=qkg[:], op=ALU.mult), [rg_tmp, rg["qkg"]], [rg_qk])
            yield
            q4 = lambda ap: ap.rearrange("p (h two d) -> p h two d", two=2, d=32)
            V(lambda e, i=i: e.tensor_tensor(out=q4(sq_v), in0=q4(qk_v),
                                             in1=cos_t[:, i, :].unsqueeze(1).unsqueeze(1).to_broadcast([128, 10, 2, 32]), op=ALU.mult),
              [rg_qk, rg["cos_t"]], [rg_sq])
            V(lambda e, i=i: e.tensor_tensor(out=q4(tmp_v), in0=q4(qk_v),
                                             in1=sin_t[:, i, :].unsqueeze(1).unsqueeze(1).to_broadcast([128, 10, 2, 32]), op=ALU.mult),
              [rg_qk, rg["sin_t"]], [rg_tmp])
            V(lambda e: e.tensor_tensor(out=q4(qr_v)[:, :, 0, :], in0=q4(sq_v)[:, :, 0, :], in1=q4(tmp_v)[:, :, 1, :], op=ALU.subtract),
              [rg_sq, rg_tmp], [rg_qr])
            V(lambda e: e.tensor_tensor(out=q4(qr_v)[:, :, 1, :], in0=q4(sq_v)[:, :, 1, :], in1=q4(tmp_v)[:, :, 0, :], op=ALU.add),
              [rg_sq, rg_tmp], [rg_qr])
            yield
            cb_ps = PS[:, 1536:2048].rearrange("p (c t) -> p c t", t=128)
            cc_ps = PS[:, 2048:2560].rearrange("p (c t) -> p c t", t=128)
            cu_ps = PS[:, 2560:3072].rearrange("p (c t) -> p c t", t=128)
            cu3 = cu_v.rearrange("p (c t) -> p c t", t=128)
            c13 = c1_v.rearrange("p (c t) -> p c t", t=128)
            cv3 = cv_v.rearrange("p (c t) -> p c t", t=128)
            A(lambda e: e.copy(out=cu3, in_=cu_ps), [rg_ps[5]], [rg_cu])
            V(lambda e: e.tensor_copy(out=pbuf[:, :, 0:2], in_=pbuf[:, :, 128:130]), [rg["pbuf"]], [rg["pbuf"]])
            V(lambda e: e.tensor_tensor(out=pbuf[:, :, 2:130], in0=cc_ps, in1=cu3, op=ALU.mult), [rg_ps[4], rg_cu], [rg["pbuf"]])
            if i == 0:
                V(lambda e: e.tensor_scalar(out=pbuf[:, :, 2:130], in0=pbuf[:, :, 2:130], scalar1=hflag[:, 0:1], scalar2=None, op0=ALU.mult),
                  [rg["pbuf"], rg["hflag"]], [rg["pbuf"]])
            p4q = bank_bf(4).rearrange("p (h t) -> p h t", t=128)
            for h in range(8):
                T(lambda e, h=h: e.transpose(out=p4q[0:64, h, :], in_=qr_v[:, 64 * h:64 * (h + 1)], identity=ident_b[:]),
                  [rg_qr, rg["ident_b"]], [rg_ps[4]])
            for j in range(2):
                T(lambda e, j=j: e.transpose(out=p0[0:64, j, :], in_=qr_v[:, 512 + 64 * j:512 + 64 * (j + 1)], identity=ident_b[:]),
                  [rg_qr, rg["ident_b"]], [rg_ps[0]])
            A(lambda e: e.copy(out=qT_v, in_=p4q[0:64, 0:8, :]), [rg_ps[4]], [rg_qT])
            A(lambda e, cur=cur: e.copy(out=kT[:, cur, :, :], in_=p0[0:64, 0:2, :]), [rg_ps[0]], [rg_kT[cur]])
            yield
            if i == 0:
                return
            o_all = PS[:, 512:1536].rearrange("p (b r) -> p b r", r=512)[:, :, 0:260].rearrange("p b (g e) -> p b g e", e=65)
            for j in range(2):
                for wi, (slot, mi) in enumerate([(prv, 2 if i == 1 else 1), (cur, 0)]):
                    bk = 4 if wi == 0 else 5
                    T(lambda e, j=j, slot=slot, bk=bk: e.matmul(bank(bk), lhsT=kT[:, slot, j, :], rhs=qT_v[:, 4 * j:4 * j + 4, :],
                                                                start=True, stop=True),
                      [rg_kT[slot], rg_qT], [rg_ps[bk]])
                    A(lambda e, wi=wi, bk=bk: e.activation(out=pT_v[wi], in_=bank(bk), func=AF.Exp), [rg_ps[bk]], [rg_pT[wi]])
                    V(lambda e, wi=wi, mi=mi: e.tensor_tensor(out=pT_v[wi].rearrange("p (g t) -> p g t", t=128),
                                                              in0=pT_v[wi].rearrange("p (g t) -> p g t", t=128),
                                                              in1=masks_b[:, mi, :].unsqueeze(1).to_broadcast([128, 4, 128]), op=ALU.mult),
                      [rg_pT[wi], rg["masks_b"]], [rg_pT[wi]])
                for g in range(4):
                    for wi, slot in enumerate([prv, cur]):
                        T(lambda e, j=j, g=g, wi=wi, slot=slot: e.matmul(PS[:, 512 * (1 + j) + 65 * g:512 * (1 + j) + 65 * (g + 1)],
                                                                         lhsT=pT_v[wi][:, 128 * g:128 * (g + 1)], rhs=vaug[:, slot, j, :],
                                                                         start=(wi == 0), stop=(wi == 1)),
                          [rg_pT[wi], rg_v[slot]], [rg_ps[1 + j]])
                yield
            V(lambda e: e.tensor_tensor(out=s_den[1].rearrange("p (b g) -> p b g", g=4), in0=o_all[:, :, :, 64],
                                        in1=esink[:].rearrange("p (b g) -> p b g", g=4), op=ALU.add),
              [rg_ps[1], rg_ps[2], rg["esink"]], [s_den[0]])
            V(lambda e: e.reciprocal(out=s_rden[1], in_=s_den[1]), [s_den[0]], [s_rden[0]])
            V(lambda e: e.tensor_tensor(out=attn_v.rearrange("p (b g d) -> p b g d", g=4, d=64), in0=o_all[:, :, :, 0:64],
                                        in1=s_rden[1].rearrange("p (b g) -> p b g", g=4).unsqueeze(3).to_broadcast([128, 2, 4, 64]), op=ALU.mult),
              [rg_ps[1], rg_ps[2], s_rden[0]], [rg_attn])
            if DEBUG:
                P.dma("sync", lambda e, i=i: e.dma_start(out=dbg["attn"][128 * (i - 1):128 * i, :], in_=attn_v), sem_dbg, reads=[rg_attn])
            rms_stats(attn_v, rg_attn, c1_v, rg_c1, 512, s_sa, s_rsa, s_ra)
            V(lambda e: e.scalar_tensor_tensor(out=attn_n_v, in0=attn_v, scalar=s_ra[1], in1=attn_g[:], op0=ALU.mult, op1=ALU.mult),
              [rg_attn, s_ra[0], rg["attn_g"]], [rg_attn_n])
            for k in range(4):
                T(lambda e, k=k: e.transpose(out=p0[:, k, :], in_=attn_n_v[:, 128 * k:128 * (k + 1)], identity=ident_b[:]),
                  [rg_attn_n, rg["ident_b"]], [rg_ps[0]])
            A(lambda e: e.copy(out=attnT[:], in_=p0[:, 0:4, :]), [rg_ps[0]], [rg["attnT"]])
            yield
            for ct in range(4):
                V(lambda e, ct=ct: e.tensor_scalar(out=c13[:, ct, :], in0=pbuf[:, ct, 0:128], scalar1=convw[:, ct, 0:1], scalar2=None, op0=ALU.mult),
                  [rg["pbuf"], rg["convw"]], [rg_c1])
            for kk in (1, 2):
                for ct in range(4):
                    V(lambda e, ct=ct, kk=kk: e.scalar_tensor_tensor(out=c13[:, ct, :], in0=pbuf[:, ct, kk:kk + 128], scalar=convw[:, ct, kk:kk + 1],
                                                                      in1=c13[:, ct, :], op0=ALU.mult, op1=ALU.add),
                      [rg["pbuf"], rg["convw"], rg_c1], [rg_c1])
            V(lambda e: e.tensor_tensor(out=cv3, in0=cb_ps, in1=c13, op=ALU.mult), [rg_ps[3], rg_c1], [rg_cv])
            V(lambda e: e.tensor_tensor(out=cu3, in0=cv3, in1=cv3, op=ALU.mult), [rg_cv], [rg_cu])
            for ct in range(4):
                T(lambda e, ct=ct: e.matmul(bank(5, 1), lhsT=cu3[:, ct, :], rhs=ones_f[:, 0:1], start=(ct == 0), stop=(ct == 3)),
                  [rg_cu, rg["ones_f"]], [rg_ps[5]])
            A(lambda e: e.activation(out=s_rsc[1], in_=bank(5, 1), func=AF.Sqrt, scale=1.0 / 512, bias=eps_ap), [rg_ps[5], rg_cst], [s_rsc[0]])
            V(lambda e: e.reciprocal(out=s_rc[1], in_=s_rsc[1]), [s_rsc[0]], [s_rc[0]])
            cvT3 = cvT_v.rearrange("p (c t) -> p c t", t=128)
            for ct in range(4):
                V(lambda e, ct=ct: e.tensor_scalar(out=cvT3[:, ct, :], in0=cv3[:, ct, :], scalar1=convg[:, ct:ct + 1], scalar2=None, op0=ALU.mult),
                  [rg_cv, rg["convg"]], [rg_cvT])
            yield
            for c in range(2):
                for k in range(4):
                    T(lambda e, c=c, k=k: e.matmul(bank(1 + c), lhsT=attnT[:, k, :], rhs=w_out_sb[:, k, 512 * c:512 * (c + 1)],
                                                   start=(k == 0), stop=(k == 3)),
                      [rg["attnT"], rg["w_out"]], [rg_ps[1 + c]])
            for c in range(2):
                for ct in range(4):
                    T(lambda e, c=c, ct=ct: e.matmul(bank(3 + c), lhsT=cvT3[:, ct, :], rhs=w_out_sb[:, 4 + ct, 512 * c:512 * (c + 1)],
                                                     start=(ct == 0), stop=(ct == 3)),
                      [rg_cvT, rg["w_out"]], [rg_ps[3 + c]])
            A(lambda e: e.copy(out=BY[:], in_=PS[:, 512:1536]), [rg_ps[1], rg_ps[2]], [rg["BY"]])
            V(lambda e: e.scalar_tensor_tensor(out=BY[:], in0=PS[:, 1536:2560], scalar=s_rc[1], in1=BY[:], op0=ALU.mult, op1=ALU.add),
              [rg_ps[3], rg_ps[4], s_rc[0], rg["BY"]], [rg["BY"]])
            V(lambda e: e.tensor_tensor(out=BY[:], in0=BY[:], in1=gate1, op=ALU.mult), [rg["BY"], rg_gate1], [rg["BY"]])
            V(lambda e: e.tensor_tensor(out=BX, in0=BY[:], in1=BX, op=ALU.add), [rg["BY"], rBX], [rBX])
            row0 = 128 * (i - 1)
            if DEBUG:
                P.dma("sync", lambda e, row0=row0: e.dma_start(out=dbg["x1"][row0:row0 + 128, :], in_=BX), sem_dbg, reads=[rBX])
            yield

        def stage_R(i):
            xb = i % 2
            ib = i % 2
            BX = BX2[:, xb, :]
            rBX = rg_BX[xb]
            row0 = 128 * (i - 1)
            rms_stats(BX, rBX, BY[:], rg["BY"], D, s_ss, s_rs, s_rr)
            V(lambda e: e.scalar_tensor_tensor(out=BY[:], in0=BX, scalar=s_rr[1], in1=A2, op0=ALU.mult, op1=ALU.mult),
              [rBX, s_rr[0], rg_A2], [rg["BY"]])
            V(lambda e: e.tensor_tensor(out=BY[:], in0=BY[:], in1=shift2, op=ALU.add), [rg["BY"], rg_shift2], [rg["BY"]])
            BH, rBH = BH2[:, i % 2, :], rg_BH[i % 2]
            A(lambda e: e.copy(out=BH, in_=BY[:]), [rg["BY"]], [rBH])
            if DEBUG:
                P.dma("sync", lambda e, row0=row0: e.dma_start(out=dbg["h2"][row0:row0 + 128, :], in_=BY[:]), sem_dbg, reads=[rg["BY"]])
            yield
            p0 = bank_bf(0).rearrange("p (k t) -> p k t", t=128)
            for kt in range(8):
                T(lambda e, kt=kt: e.transpose(out=p0[:, kt, :], in_=BH[:, 128 * kt:128 * (kt + 1)], identity=ident_b[:]),
                  [rBH, rg["ident_b"]], [rg_ps[0]])
            A(lambda e: e.copy(out=BHT[:], in_=p0[:, 0:8, :]), [rg_ps[0]], [rg["BHT"]])
            yield
            wqb_r = wqb.rearrange("(kt p) n -> p kt n", p=128)

            def wq_load(c):
                P.dma("sync", lambda e, c=c: e.dma_start(out=wqc[:, c % 2, :, :], in_=wqb_r[:, :, 512 * c:512 * (c + 1)]), sem_wqc[c % 2],
                      reads=[rg_wqb, rg_wqb2], writes=[rg_wqc[c % 2]])

            wq_load(0)
            wq_load(1)
            for blk in range(16):
                c, bl = blk // 4, blk % 4
                for kt in range(8):
                    T(lambda e, blk=blk, kt=kt, c=c, bl=bl: e.matmul(PS[:, 512 + 128 * blk:512 + 128 * (blk + 1)], lhsT=wqc[:, c % 2, kt, 128 * bl:128 * (bl + 1)], rhs=BHT[:, kt, :],
                                                         start=(kt == 0), stop=(kt == 7)),
                      [rg_wqc[c % 2], rg["BHT"]], [rg_ps[1 + blk // 4]])
                if blk % 4 == 3:
                    if c + 2 < 4:
                        wq_load(c + 2)
                    yield
            A(lambda e: e.copy(out=S4B[:, 0:2048], in_=PS[:, 512:2560]), [rg_ps[1], rg_ps[2], rg_ps[3], rg_ps[4]], [rg_qpT])
            for blk in range(16):
                T(lambda e, blk=blk: e.matmul(PS[:, 512 + 128 * blk:512 + 128 * (blk + 1)], lhsT=qpT_v[:, blk, :], rhs=skT_sb[:, blk, :],
                                              start=True, stop=True),
                  [rg_qpT, rg["skT"]], [rg_ps[1 + blk // 4]])
            A(lambda e: e.copy(out=sc_v, in_=PS[:, 512:2560]), [rg_ps[1], rg_ps[2], rg_ps[3], rg_ps[4]], [rg_sc])
            yield
            mr4 = S8B[:, 0:512]
            rg_mr4 = R("S8B", 0, 2048)
            for g4 in range(4):
                blks = range(4 * g4, 4 * g4 + 4)
                sin_ = {b_: sc_v[:, 128 * b_:128 * (b_ + 1)] for b_ in blks}
                v0 = {b_: vals[:, 16 * b_:16 * b_ + 8] for b_ in blks}
                v1 = {b_: vals[:, 16 * b_ + 8:16 * b_ + 16] for b_ in blks}
                i0 = {b_: ids[:, 16 * b_:16 * b_ + 8] for b_ in blks}
                i1 = {b_: ids[:, 16 * b_ + 8:16 * b_ + 16] for b_ in blks}
                m_ = {b_: mr4[:, 128 * (b_ % 4):128 * (b_ % 4 + 1)] for b_ in blks}
                for b_ in blks:
                    V(lambda e, o_=v0[b_], x_=sin_[b_]: e.max(out=o_, in_=x_), [rg_sc], [rg["vals"]])
                for b_ in blks:
                    V(lambda e, o_=i0[b_], m__=v0[b_], x_=sin_[b_]: e.max_index(out=o_, in_max=m__, in_values=x_), [rg_sc, rg["vals"]], [rg["ids"]])
                for b_ in blks:
                    V(lambda e, o_=m_[b_], m__=v0[b_], x_=sin_[b_]: e.match_replace(out=o_, in_to_replace=m__, in_values=x_, imm_value=-1e30),
                      [rg_sc, rg["vals"]], [rg_mr4])
                for b_ in blks:
                    V(lambda e, o_=v1[b_], x_=m_[b_]: e.max(out=o_, in_=x_), [rg_mr4], [rg["vals"]])
                for b_ in blks:
                    V(lambda e, o_=i1[b_], m__=v1[b_], x_=m_[b_]: e.max_index(out=o_, in_max=m__, in_values=x_), [rg_mr4, rg["vals"]], [rg["ids"]])
                yield
            V(lambda e: e.tensor_copy(out=idsf, in_=ids), [rg["ids"]], [rg["idsf"]])
            vals4 = vals.rearrange("p (h s k) -> p h s k", s=2, k=16)
            idsf4 = idsf.rearrange("p (h s k) -> p h s k", s=2, k=16)
            cand4 = cand_v.rearrange("p (h a b) -> p h a b", a=16, b=16)
            V(lambda e: e.tensor_tensor(out=cand4, in0=vals4[:, :, 0, :].unsqueeze(3).to_broadcast([128, 8, 16, 16]),
                                        in1=vals4[:, :, 1, :].unsqueeze(2).to_broadcast([128, 8, 16, 16]), op=ALU.add),
              [rg["vals"]], [rg_cand])
            for h in range(8):
                c_in = cand_v[:, 256 * h:256 * (h + 1)]
                t0, t1 = top[:, 16 * h:16 * h + 8], top[:, 16 * h + 8:16 * h + 16]
                q0, q1 = pos[:, 16 * h:16 * h + 8], pos[:, 16 * h + 8:16 * h + 16]
                V(lambda e, c_in=c_in, t0=t0: e.max(out=t0, in_=c_in), [rg_cand], [rg["top"]])
                V(lambda e, c_in=c_in, t0=t0, q0=q0: e.max_index(out=q0, in_max=t0, in_values=c_in), [rg_cand, rg["top"]], [rg["pos"]])
                V(lambda e, c_in=c_in, t0=t0: e.match_replace(out=mr, in_to_replace=t0, in_values=c_in, imm_value=-1e30),
                  [rg_cand, rg["top"]], [rg["mr"]])
                V(lambda e, t1=t1: e.max(out=t1, in_=mr), [rg["mr"]], [rg["top"]])
                V(lambda e, t1=t1, q1=q1: e.max_index(out=q1, in_max=t1, in_values=mr), [rg["mr"], rg["top"]], [rg["pos"]])
                if h % 4 == 3:
                    yield
            V(lambda e: e.tensor_single_scalar(out=k1u, in_=pos[:], scalar=4, op=ALU.logical_shift_right), [rg["pos"]], [rg_k1u])
            V(lambda e: e.tensor_single_scalar(out=k2u, in_=pos[:], scalar=15, op=ALU.bitwise_and), [rg["pos"]], [rg_k2u])
            V(lambda e: e.tensor_copy(out=k1f, in_=k1u), [rg_k1u], [rg_k1f])
            V(lambda e: e.tensor_copy(out=k2f, in_=k2u), [rg_k2u], [rg_k2f])
            oh4 = oh_v.rearrange("p (h j k) -> p h j k", j=16, k=16)
            for side, kf, rk in ((0, k1f, rg_k1f), (1, k2f, rg_k2f)):
                V(lambda e, kf=kf: e.tensor_tensor(out=oh4, in0=kf.rearrange("p (h j) -> p h j", j=16).unsqueeze(3).to_broadcast([128, 8, 16, 16]),
                                                   in1=iota16[:].unsqueeze(1).unsqueeze(1).to_broadcast([128, 8, 16, 16]), op=ALU.is_equal),
                  [rk, rg["iota16"]], [rg_oh])
                V(lambda e, side=side: e.tensor_tensor(out=oh4, in0=oh4, in1=idsf4[:, :, side, :].unsqueeze(2).to_broadcast([128, 8, 16, 16]), op=ALU.mult),
                  [rg_oh, rg["idsf"]], [rg_oh])
                V(lambda e, side=side: e.tensor_reduce(out=isel[:, side, :], in_=oh_v.rearrange("p (a k) -> p a k", k=16), axis=AX.X, op=ALU.add),
                  [rg_oh], [rg_isel])
            yield
            V(lambda e: e.scalar_tensor_tensor(out=idxf, in0=isel[:, 0, :], scalar=128.0, in1=isel[:, 1, :], op0=ALU.mult, op1=ALU.add),
              [rg_isel], [rg_idxf])
            V(lambda e, ib=ib: e.tensor_copy(out=idxi2[:, ib, :], in_=idxf), [rg_idxf], [rg_idxi[ib]])
            top3 = top[:].rearrange("p (h j) -> p h j", j=16)
            V(lambda e: e.tensor_tensor(out=esm[:].rearrange("p (h j) -> p h j", j=16), in0=top3,
                                        in1=top3[:, :, 0:1].to_broadcast([128, 8, 16]), op=ALU.subtract),
              [rg["top"]], [rg["esm"]])
            A(lambda e: e.activation(out=esm[:], in_=esm[:], func=AF.Exp), [rg["esm"]], [rg["esm"]])
            V(lambda e: e.tensor_reduce(out=s_sm[1], in_=esm[:].rearrange("p (h j) -> p h j", j=16), axis=AX.X, op=ALU.add),
              [rg["esm"]], [s_sm[0]])
            V(lambda e: e.reciprocal(out=s_rsm[1], in_=s_sm[1]), [s_sm[0]], [s_rsm[0]])
            V(lambda e, ib=ib: e.tensor_tensor(out=gsm2[:, ib, :].rearrange("p (h j) -> p h j", j=16), in0=esm[:].rearrange("p (h j) -> p h j", j=16),
                                        in1=s_rsm[1].unsqueeze(2).to_broadcast([128, 8, 16]), op=ALU.mult),
              [rg["esm"], s_rsm[0]], [rg_gsm[ib]])
            if DEBUG:
                P.dma("sync", lambda e, row0=row0, ib=ib: e.dma_start(out=dbg["idx"][row0:row0 + 128, :], in_=idxi2[:, ib, :]), sem_dbg, reads=[rg_idxi[ib]])
                P.dma("sync", lambda e, row0=row0, ib=ib: e.dma_start(out=dbg["g"][row0:row0 + 128, :], in_=gsm2[:, ib, :]), sem_dbg, reads=[rg_gsm[ib]])
            yield

        def drain(gen):
            if gen is None:
                return
            for _ in gen:
                pass

        def step(gen):
            if gen is None:
                return None
            try:
                next(gen)
                return gen
            except StopIteration:
                return None

        def chain(*gens):
            for g_ in gens:
                for _ in g_:
                    yield

        def stage_UV(i, other):
            ib = i % 2
            BH, rBH = BH2[:, ib, :], rg_BH[ib]
            def tail(slot, r):
                db = slot % 4
                V(lambda e, slot=slot, db=db, ib=ib: e.tensor_scalar(out=dg[:, db, :], in0=ident_b[:], scalar1=gav[:, slot:slot + 1],
                                                                      scalar2=gsm2[:, ib, slot:slot + 1], op0=ALU.mult, op1=ALU.mult),
                  [rg_gav[slot], rg_gsm[ib], rg["ident_b"]], [rg_dg[db]])
                for hf in range(2):
                    T(lambda e, r=r, db=db, hf=hf, slot=slot: e.matmul(PS[:, 3072 + 512 * hf:3072 + 512 * (hf + 1)], lhsT=dg[:, db, :],
                                                                        rhs=ring_bf[:, r, 1024 + 512 * hf:1024 + 512 * (hf + 1)],
                                                                        start=(slot == 0), stop=(slot == 127)),
                      [rg_dg[db], rg_ring[r]], [rg_ps[6 + hf]])

            LAG = 2
            pend = []
            for slot in range(128):
                r = gather_uv(slot, ib)
                V(lambda e, r=r: e.tensor_tensor(out=ring_bf[:, r, 0:1024], in0=ring_bf[:, r, 0:1024], in1=BH, op=ALU.mult),
                  [rg_ring[r], rBH], [rg_ring[r]])
                A(lambda e, r=r, slot=slot: e.activation(out=ring_bf[:, r, 0:1024], in_=ring_bf[:, r, 0:1024], func=AF.Copy, accum_out=av[:, slot:slot + 1]),
                  [rg_ring[r]], [rg_ring[r], rg_av[slot]])
                A(lambda e, slot=slot: e.activation(out=gav[:, slot:slot + 1], in_=av[:, slot:slot + 1], func=AF.Gelu), [rg_av[slot]], [rg_gav[slot]])
                pend.append((slot, r))
                if len(pend) > LAG:
                    tail(*pend.pop(0))
                other = step(other)
            while pend:
                tail(*pend.pop(0))
            drain(other)

        def stage_fin(i):
            xb = i % 2
            BX = BX2[:, xb, :]
            rBX = rg_BX[xb]
            row0 = 128 * (i - 1)
            fin_v = S8B[:, 0:1024]
            V(lambda e: e.tensor_tensor(out=fin_v, in0=PS[:, 3072:4096], in1=gate2, op=ALU.mult), [rg_ps[6], rg_ps[7], rg_gate2], [rg_fin])
            V(lambda e: e.tensor_tensor(out=BX, in0=fin_v, in1=BX, op=ALU.add), [rg_fin, rBX], [rBX])
            P.dma("sync", lambda e, row0=row0: e.dma_start(out=out[row0:row0 + 128, :], in_=BX), sem_o, reads=[rBX])

        if STAGE >= 1:
            drain(stage_A(0))
            last = N_TILES_RUN
            drain(stage_A(1))
            if STAGE >= 2:
                drain(stage_R(1))
            for i in range(1, last + 1):
                nxt = None
                if i < last:
                    nxt = chain(stage_A(i + 1), stage_R(i + 1)) if STAGE >= 2 else stage_A(i + 1)
                if STAGE >= 3:
                    stage_UV(i, nxt)
                    stage_fin(i)
                else:
                    drain(nxt)

        def tail_waits():
            w = [(sem_o, P.dcnt.get(sem_o, 0))] + [(sw, P.dcnt.get(sw, 0)) for sw in sem_w] + [(sv, P.dcnt.get(sv, 0)) for sv in sem_cvs]
            if DEBUG:
                w.append((sem_dbg, P.dcnt.get(sem_dbg, 0)))
            return [x for x in w if x[1] > 0]

        P.emit(block, tail_waits)
    return nc


def make_in_maps(x, c, positions, ada_w, ada_b, norm1_g, w_in, q_norm_g, k_norm_g, sinks, conv_w,
                 attn_out_g, conv_out_g, w_out, norm2_g, peer_wq, peer_subkeys, peer_u, peer_v):
    f = lambda a: np.ascontiguousarray(np.asarray(a), dtype=np.float32)
    x = f(x); c = f(c); positions = np.asarray(positions).astype(np.int32)
    S = x.shape[1]
    cpb = N_CORES // x.shape[0]
    assert S // cpb == TOK
    crow = np.zeros((1, 3584), np.float32)
    crow[0, 0:1024] = f(norm1_g)[0]
    crow[0, 1024:2048] = f(norm2_g)[0]
    crow[0, 2048:2560] = f(attn_out_g)[0]
    crow[0, 2560:3072] = np.tile(f(q_norm_g)[0], 8)
    crow[0, 3072:3200] = np.tile(f(k_norm_g)[0], 2)
    crow[0, 3200:3208] = f(sinks)[0]
    crow[0, 3208:3240] = (np.float32(10000.0) ** (-np.arange(0, 64, 2, dtype=np.float32) / np.float32(64))).astype(np.float32)
    convw = np.ascontiguousarray(f(conv_w)[0].T.reshape(4, 128, 3).transpose(1, 0, 2))
    convg = np.ascontiguousarray(f(conv_out_g)[0].reshape(4, 128).T)
    skT = np.ascontiguousarray(f(peer_subkeys)[0].reshape(16, 128, 128).transpose(2, 0, 1))
    ident = np.eye(128, dtype=np.float32)
    tk = np.arange(128)[:, None]
    tq = np.arange(128)[None, :]
    m_cur = (tk <= tq).astype(np.float32)
    m_prev = (tk > tq).astype(np.float32)
    shared = dict(ada_w=f(ada_w)[0], ada_b=f(ada_b), crow=crow, convw=convw, convg=convg, w_in=f(w_in)[0], w_out=f(w_out)[0],
                  wq=f(peer_wq)[0], skT=skT, peer_uv=np.ascontiguousarray(np.concatenate([f(peer_u)[0], f(peer_v)[0]], axis=1)), ident=ident)
    maps = []
    for core in range(N_CORES):
        b, j = core // cpb, core % cpb
        t0 = j * TOK
        xh = np.zeros((NTE * 128, D), np.float32)
        ph = np.zeros((NTE * 128,), np.int32)
        xh[128:] = x[b, t0:t0 + TOK]
        ph[128:] = positions[b, t0:t0 + TOK]
        first = (j == 0)
        if not first:
            xh[:128] = x[b, t0 - 128:t0]
            ph[:128] = positions[b, t0 - 128:t0]
        masks = np.stack([m_cur, m_prev, np.zeros_like(m_prev) if first else m_prev], axis=1)
        m = dict(shared)
        m.update(xh=xh, posi=np.ascontiguousarray(ph.reshape(NTE, 128).T),
                 ccol=np.ascontiguousarray(c[b].reshape(8, 128).T),
                 hflag=np.full((128, 1), 0.0 if first else 1.0, np.float32),
                 masks=np.ascontiguousarray(masks.astype(np.float32)))
        maps.append(m)
    return maps


_NC_CACHE = {}


def kernel(**inputs):
    maps = make_in_maps(**inputs)
    key = (DEBUG, N_TILES_RUN, STAGE, NEXP, BF16_TABLE)
    if key not in _NC_CACHE:
        _NC_CACHE[key] = build_nc()
    nc = _NC_CACHE[key]
    ncores = getattr(kernel, 'ncores', N_CORES)
    if NEXP != 16384:
        for m in maps:
            m['peer_uv'] = m['peer_uv'][:NEXP]
    res = run_bass_kernel_spmd(nc, maps[:ncores], core_ids=list(range(ncores)))
    if ncores != N_CORES:
        kernel.last = res.results
        return None
    B, S = np.asarray(inputs["x"]).shape[:2]
    outp = np.concatenate([r["out"] for r in res.results], axis=0).reshape(B, S, D).astype(np.float32)
    if DEBUG:
        kernel.last = res.results
    return outp
```

```python
import math
import os
from contextlib import ExitStack

import numpy as np
import concourse.bass as bass
import concourse.mybir as mybir
from concourse.bass_utils import run_bass_kernel_spmd

F32 = mybir.dt.float32
BF16 = mybir.dt.bfloat16
I32 = mybir.dt.int32
U32 = mybir.dt.uint32
ALU = mybir.AluOpType
AF = mybir.ActivationFunctionType
AX = mybir.AxisListType

N_CORES = 8
D = 1024
NT = 16
TOK = NT * 128
NTE = NT + 1
EPS = 1e-6
NR = 12
ENGS = ["sync", "scalar", "vector", "gpsimd", "tensor"]
DEBUG = False
N_TILES_RUN = NT
STAGE = 3
NEXP = 16384
BF16_TABLE = True


class Region:
    _all = {}

    def __init__(self, phys, lo, hi):
        self.phys, self.lo, self.hi = phys, lo, hi
        self.w = None
        self.r = []
        Region._all.setdefault(phys, []).append(self)
        self._ov = None

    def ov(self):
        return [o for o in Region._all[self.phys] if o.lo < self.hi and self.lo < o.hi]


class Op:
    __slots__ = ("eng", "fn", "deps", "is_dma", "dsem", "dval", "needs_inc", "seq", "waits")

    def __init__(self, eng, fn, deps, is_dma=False, dsem=None, dval=0):
        self.eng, self.fn, self.deps = eng, fn, deps
        self.is_dma, self.dsem, self.dval = is_dma, dsem, dval
        self.needs_inc = False
        self.seq = 0
        self.waits = []


class Prog:
    def __init__(self, sems):
        self.sems = sems
        self.ops = []
        self.dcnt = {}

    def _deps(self, reads, writes, extra):
        deps = {}
        for r in reads:
            for o in r.ov():
                if o.w is not None:
                    deps[o.w] = "raw"
        for w in writes:
            for o in w.ov():
                if o.w is not None and o.w not in deps:
                    deps[o.w] = "war"
                for rd in o.r:
                    if rd not in deps:
                        deps[rd] = "war"
        for e in extra:
            if e is not None:
                deps[e] = "raw"
        return deps

    def _commit(self, op, reads, writes):
        for r in reads:
            r.r.append(op)
        for w in writes:
            w.w = op
            w.r = []

    def op(self, eng, fn, reads=(), writes=(), extra=()):
        op = Op(eng, fn, self._deps(reads, writes, extra))
        self._commit(op, reads, writes)
        self.ops.append(op)
        return op

    def dma(self, eng, fn, sem, reads=(), writes=(), extra=()):
        self.dcnt[sem] = self.dcnt.get(sem, 0) + 16
        op = Op(eng, fn, self._deps(reads, writes, extra), True, sem, self.dcnt[sem])
        self._commit(op, reads, writes)
        self.ops.append(op)
        return op

    def finalize(self):
        for c in self.ops:
            for d, kind in c.deps.items():
                if d.is_dma:
                    c.waits.append(d)
                    continue
                if d.eng == c.eng and not c.is_dma:
                    if c.eng == "tensor":
                        continue
                    if kind != "raw":
                        continue
                d.needs_inc = True
                c.waits.append(d)
        cnt = {e: 0 for e in ENGS}
        for o in self.ops:
            if o.needs_inc and not o.is_dma:
                cnt[o.eng] += 1
                o.seq = cnt[o.eng]
        self.final_cnt = cnt

    def emit(self, block, tail_waits):
        self.finalize()
        sems = self.sems
        per = {e: [o for o in self.ops if o.eng == e] for e in ENGS}

        def run_engine(e, name):
            waited = {}
            for o in per[name]:
                for d in o.waits:
                    if d.is_dma:
                        s, v = d.dsem, d.dval
                    else:
                        s, v = sems[d.eng], d.seq
                    k = id(s)
                    if waited.get(k, 0) >= v:
                        continue
                    waited[k] = v
                    e.wait_ge(s, v)
                ins = o.fn(e)
                if o.is_dma:
                    ins.then_inc(o.dsem, 16)
                elif o.needs_inc:
                    ins.then_inc(sems[o.eng], 1)
            if name == "sync":
                for (s, v) in tail_waits():
                    e.wait_ge(s, v)

        @block.sync
        def _(e):
            run_engine(e, "sync")

        @block.scalar
        def _(e):
            run_engine(e, "scalar")

        @block.vector
        def _(e):
            run_engine(e, "vector")

        @block.gpsimd
        def _(e):
            run_engine(e, "gpsimd")

        @block.tensor
        def _(e):
            run_engine(e, "tensor")


def _split_2pi():
    two_pi = 2.0 * math.pi
    c1 = 6.28125
    r = two_pi - c1
    c2 = float(np.float32(r))
    m, e = math.frexp(c2)
    c2 = math.ldexp(round(m * 2048) / 2048, e)
    c3 = float(np.float32(two_pi - c1 - c2))
    return c1, c2, c3


def build_nc():
    Region._all = {}
    nc = bass.Bass("TRN2", target_bir_lowering=False)

    def din(name, shape, dt=F32):
        return nc.dram_tensor(name, list(shape), dt, kind="ExternalInput").ap()

    xh = din("xh", [NTE * 128, D])
    posi_d = din("posi", [128, NTE], I32)
    ccol_d = din("ccol", [128, 8])
    ada_w = din("ada_w", [D, 6 * D])
    ada_b = din("ada_b", [1, 6 * D])
    crow = din("crow", [1, 3584])
    convw_d = din("convw", [128, 4, 3])
    convg_d = din("convg", [128, 4])
    hflag_d = din("hflag", [128, 1])
    w_in = din("w_in", [D, 2304])
    w_out = din("w_out", [D, D])
    wq = din("wq", [D, 2048])
    skT_d = din("skT", [128, 16, 128])
    peer_uv = din("peer_uv", [NEXP, 2 * D])
    uvb = nc.dram_tensor("uvb", [NEXP, 2 * D], BF16, kind="Internal").ap() if BF16_TABLE else None
    wqb = nc.dram_tensor("wqb", [D, 2048], BF16, kind="Internal").ap()
    ident_d = din("ident", [128, 128])
    masks_d = din("masks", [128, 3, 128])
    out = nc.dram_tensor("out", [TOK, D], F32, kind="ExternalOutput").ap()
    dbg = {}
    if DEBUG:
        dbg["x1"] = nc.dram_tensor("dbg_x1", [TOK, D], F32, kind="ExternalOutput").ap()
        dbg["h2"] = nc.dram_tensor("dbg_h2", [TOK, D], F32, kind="ExternalOutput").ap()
        dbg["idx"] = nc.dram_tensor("dbg_idx", [TOK, 128], I32, kind="ExternalOutput").ap()
        dbg["g"] = nc.dram_tensor("dbg_g", [TOK, 128], F32, kind="ExternalOutput").ap()
        dbg["attn"] = nc.dram_tensor("dbg_attn", [TOK, 512], F32, kind="ExternalOutput").ap()
        dbg["mod"] = nc.dram_tensor("dbg_mod", [128, 6, D], F32, kind="ExternalOutput").ap()
        dbg["cs"] = nc.dram_tensor("dbg_cs", [128, 2, NTE, 32], F32, kind="ExternalOutput").ap()

    with ExitStack() as es:
        E = es.enter_context

        def sb(name, shape, dt=F32):
            return E(nc.sbuf_tensor("sb_" + name, list(shape), dt))

        ident_f = sb("ident_f", [128, 128])
        ident_b = sb("ident_b", [128, 128], BF16)
        ones_f = sb("ones_f", [128, 128])
        masks_b = sb("masks_b", [128, 3, 128], BF16)
        ccol = sb("ccol", [128, 8])
        cact = sb("cact", [128, 8])
        posi = sb("posi", [128, NTE], I32)
        posf = sb("posf", [128, NTE])
        invf = sb("invf", [128, 32])
        cos_t = sb("cos_t", [128, NTE, 32])
        sin_t = sb("sin_t", [128, NTE, 32])
        qkg = sb("qkg", [128, 640])
        attn_g = sb("attn_g", [128, 512])
        sinkb = sb("sinkb", [128, 8])
        esink = sb("esink", [128, 8])
        convw = sb("convw", [128, 4, 3])
        convg = sb("convg", [128, 4])
        hflag = sb("hflag", [128, 1])
        cst = sb("cst", [128, 4])
        iota16 = sb("iota16", [128, 16])
        mod6 = sb("mod6", [128, 6, D])
        w_in_sb = sb("w_in_sb", [128, 8, 2304], BF16)
        w_out_sb = sb("w_out_sb", [128, 8, D], BF16)
        wqc = sb("wqc", [128, 2, 8, 512], BF16)
        skT_sb = sb("skT_sb", [128, 16, 128], BF16)
        ring = sb("ring", [128, NR, D])
        BX2 = sb("BX2", [128, 2, D])
        BH2 = sb("BH2", [128, 2, D], BF16)
        BHT = sb("BHT", [128, 8, 128], BF16)
        S8A = sb("S8A", [128, 2048])
        S8B = sb("S8B", [128, 2048])
        S4A = sb("S4A", [128, 2048], BF16)
        S4B = sb("S4B", [128, 2048], BF16)
        BY = sb("BY", [128, D])
        dg = sb("dg", [128, 4, 128], BF16)
        attnT = sb("attnT", [128, 4, 128], BF16)
        kT = sb("kT", [64, 2, 2, 128], BF16)
        vaug = sb("vaug", [128, 2, 2, 65], BF16)
        pbuf = sb("pbuf", [128, 4, 130])
        st = sb("st", [128, 64])
        top = sb("top", [128, 128])
        pos = sb("pos", [128, 128], U32)
        idxi2 = sb("idxi2", [128, 2, 128], I32)
        gsm2 = sb("gsm2", [128, 2, 128])
        gav = sb("gav", [128, 128])
        esm = sb("esm", [128, 128])
        av = sb("av", [128, 128])
        PS = E(nc.psum_tensor("PS", [128, 4096], F32))

        sems = {k: E(nc.semaphore("s_" + k)) for k in ENGS}
        sem_c = E(nc.semaphore("d_c"))
        sem_w = [E(nc.semaphore("d_w%d" % i)) for i in range(4)]
        sem_ada = [E(nc.semaphore("d_ada%d" % i)) for i in range(2)]
        sem_x = [E(nc.semaphore("d_x%d" % i)) for i in range(2)]
        sem_o = [E(nc.semaphore("d_o%d" % i)) for i in range(2)]
        sem_dbg = E(nc.semaphore("d_dbg"))
        sem_ring = [E(nc.semaphore("d_r%d" % i)) for i in range(NR)]
        sem_cvl = [E(nc.semaphore("d_cvl%d" % i)) for i in range(2)]
        sem_wqc = [E(nc.semaphore("d_wqc%d" % i)) for i in range(2)]
        sem_cvs = [E(nc.semaphore("d_cvs%d" % i)) for i in range(2)]
        block = E(nc.Block())
        P = Prog(sems)

        def R(phys, lo=0, hi=1 << 30):
            return Region(phys, lo, hi)

        rg = {}
        for nm in ["ident_f", "ident_b", "ones_f", "masks_b", "ccol", "cact", "posi", "posf", "invf",
                   "cos_t", "sin_t", "qkg", "attn_g", "sinkb", "esink", "convw", "convg", "hflag", "cst", "iota16",
                   "w_in", "w_out", "skT", "BHT", "BY", "attnT", "pbuf",
                   "top", "pos",
                   "esm"]:
            rg[nm] = R(nm)
        rg_mod = [R("mod6", 4096 * k, 4096 * (k + 1)) for k in range(6)]
        rg_ring = [R("ring", 4096 * k, 4096 * (k + 1)) for k in range(NR)]
        rg_adab = [rg["BY"], rg["BY"]]
        adab = BY[0:1, 0:512].rearrange("p (s n) -> p s n", n=256)
        rg_BX = [R("BX2", 4096 * k, 4096 * (k + 1)) for k in range(2)]
        rg_idxi = [R("idxi2", 512 * k, 512 * (k + 1)) for k in range(2)]
        rg_gsm = [R("gsm2", 512 * k, 512 * (k + 1)) for k in range(2)]
        rg_BH = [R("BH2", 2048 * k, 2048 * (k + 1)) for k in range(2)]
        rg_dg = [R("dg", 256 * k, 256 * (k + 1)) for k in range(4)]
        rg_uvb = R("uvb")
        rg_uvb2 = R("uvb2")
        rg_wqb, rg_wqb2 = R("wqb"), R("wqb2")
        rg_wqc = [R("wqc", 8192 * k, 8192 * (k + 1)) for k in range(2)]
        rg_av = [R("av", 4 * k, 4 * k + 4) for k in range(128)]
        rg_gav = [R("gav", 4 * k, 4 * k + 4) for k in range(128)]
        wv = ident_f
        rg_wv = [R("ident_f", 4 * k, 4 * k + 4) for k in range(128)]
        ring_bf = ring[:].bitcast(BF16)
        rg_kT = [R("kT", 512 * k, 512 * (k + 1)) for k in range(2)]
        rg_v = [R("vaug", 260 * k, 260 * (k + 1)) for k in range(2)]
        rg_st = {}

        def ST(name, lo, hi):
            rg_st[name] = (R("st", 4 * lo, 4 * hi), st[:, lo:hi])
            return rg_st[name]

        masks_f = S8A[:, 1600:1984].rearrange("p (m t) -> p m t", t=128)
        rg["masks_f"] = R("S8A", 6400, 7936)
        qk_v, sq_v, tmp_v = S8A[:, 0:640], S8A[:, 640:1280], S8A[:, 1280:1920]
        rg_qk, rg_sq, rg_tmp = R("S8A", 0, 2560), R("S8A", 2560, 5120), R("S8A", 5120, 7680)
        sc_v, rg_sc = S8A[:, 0:2048], R("S8A", 0, 8192)
        k1u, rg_k1u = S8A[:, 0:128].bitcast(U32), R("S8A", 0, 512)
        k2u, rg_k2u = S8A[:, 128:256].bitcast(U32), R("S8A", 512, 1024)
        k1f, rg_k1f = S8A[:, 256:384], R("S8A", 1024, 1536)
        k2f, rg_k2f = S8A[:, 384:512], R("S8A", 1536, 2048)
        isel, rg_isel = S8A[:, 512:768].rearrange("p (s n) -> p s n", n=128), R("S8A", 2048, 3072)
        idxf, rg_idxf = S8A[:, 768:896], R("S8A", 3072, 3584)
        attn_v, cu_v, c1_v, cv_v = S8B[:, 0:512], S8B[:, 512:1024], S8B[:, 1024:1536], S8B[:, 1536:2048]
        rg_attn, rg_cu, rg_c1, rg_cv = R("S8B", 0, 2048), R("S8B", 2048, 4096), R("S8B", 4096, 6144), R("S8B", 6144, 8192)
        cand_v, rg_cand = S8B[:, 0:2048], R("S8B", 0, 8192)
        rg_fin = R("S8B", 0, 4096)
        qr_v, attn_n_v, cvT_v = S4A[:, 0:640], S4A[:, 640:1152], S4A[:, 1152:1664]
        rg_qr, rg_attn_n, rg_cvT = R("S4A", 0, 1280), R("S4A", 1280, 2304), R("S4A", 2304, 3328)
        vals = S4A[:, 0:512].bitcast(F32)
        ids = S4A[:, 512:1024].bitcast(U32)
        idsf = S4A[:, 1024:1536].bitcast(F32)
        mr = S4A[:, 1536:2048].bitcast(F32)
        rg["vals"], rg["ids"], rg["idsf"], rg["mr"] = R("S4A", 0, 1024), R("S4A", 1024, 2048), R("S4A", 2048, 3072), R("S4A", 3072, 4096)
        qT_v = S4B[0:64, 0:1024].rearrange("p (h t) -> p h t", t=128)
        pT_v = [S4B[:, 1024:1536], S4B[:, 1536:2048]]
        rg_qT, rg_pT = R("S4B", 0, 2048), [R("S4B", 2048, 3072), R("S4B", 3072, 4096)]
        qpT_v, rg_qpT = S4B[:, 0:2048].rearrange("p (b t) -> p b t", t=128), R("S4B", 0, 4096)
        oh_v, rg_oh = S4B[:, 0:2048], R("S4B", 0, 4096)
        junk_v, rg_junk = S4B[:, 0:1024], R("S4B", 0, 2048)
        rg_ps = [R("PS", 2048 * b, 2048 * (b + 1)) for b in range(8)]

        def bank(b, n=512, off=0):
            return PS[:, 512 * b + off:512 * b + off + n]

        def bank_bf(b):
            return PS[:, 512 * b:512 * (b + 1)].bitcast(BF16)

        grp_c, grp_w = [], [[], [], [], []]

        def ld(dst, src, regs, sem=sem_c, eng="sync"):
            o = P.dma(eng, lambda e, dst=dst, src=src: e.dma_start(out=dst, in_=src), sem, writes=regs)
            grp_c.append(o)
            return o

        ld(ident_f[:], ident_d, [rg["ident_f"]])
        ld(masks_f, masks_d, [rg["masks_f"]])
        ld(ccol[:], ccol_d, [rg["ccol"]])
        ld(posi[:], posi_d, [rg["posi"]])
        ld(convw[:], convw_d, [rg["convw"]])
        ld(convg[:], convg_d, [rg["convg"]])
        ld(hflag[:], hflag_d, [rg["hflag"]])
        if 'b' not in os.environ.get('KSKIP', ''):
          ld(mod6[:, 4, :], crow[:, 0:1024].partition_broadcast(128), [rg_mod[4]])
          ld(mod6[:, 5, :], crow[:, 1024:2048].partition_broadcast(128), [rg_mod[5]])
          ld(attn_g[:], crow[:, 2048:2560].partition_broadcast(128), [rg["attn_g"]])
          ld(qkg[:], crow[:, 2560:3200].partition_broadcast(128), [rg["qkg"]])
          ld(sinkb[:], crow[:, 3200:3208].partition_broadcast(128), [rg["sinkb"]])
          ld(invf[:], crow[:, 3208:3240].partition_broadcast(128), [rg["invf"]])

        SKIP = os.environ.get('KSKIP', '')
        for kt in range(8 if 'w' not in SKIP else 0):
            for c in range(2):
                grp_w[0].append(P.dma("gpsimd", lambda e, kt=kt, c=c: e.dma_start(out=w_in_sb[:, kt, 1152 * c:1152 * (c + 1)],
                                                                   in_=w_in[128 * kt:128 * (kt + 1), 1152 * c:1152 * (c + 1)]),
                      sem_w[0], writes=[]))
        for kt in range(8 if ('w' not in SKIP and '1' not in SKIP) else 0):
            grp_w[1].append(P.dma("gpsimd", lambda e, kt=kt: e.dma_start(out=w_out_sb[:, kt, :], in_=w_out[128 * kt:128 * (kt + 1), :]),
                  sem_w[1], writes=[]))
        for b4 in range(4 if ('w' not in SKIP and '3' not in SKIP) else 0):
            grp_w[3].append(P.dma("gpsimd", lambda e, b4=b4: e.dma_start(out=skT_sb[:, 4 * b4:4 * b4 + 4, :], in_=skT_d[:, 4 * b4:4 * b4 + 4, :]),
                  sem_w[3], writes=[]))
        for o in grp_c:
            o.dval = P.dcnt[sem_c]
        for k, nm in enumerate(["w_in", "w_out", None, "skT"]):
            for o in grp_w[k]:
                o.dval = P.dcnt.get(sem_w[k], 0)
            if grp_w[k] and nm:
                rg[nm].w = grp_w[k][-1]

        V = lambda fn, reads=(), writes=(), extra=(): P.op("vector", fn, reads, writes, extra)
        A = lambda fn, reads=(), writes=(), extra=(): P.op("scalar", fn, reads, writes, extra)
        T = lambda fn, reads=(), writes=(), extra=(): P.op("tensor", fn, reads, writes, extra)
        G = lambda fn, reads=(), writes=(), extra=(): P.op("gpsimd", fn, reads, writes, extra)

        V(lambda e: e.tensor_copy(out=ident_b[:], in_=ident_f[:]), [rg["ident_f"]], [rg["ident_b"]])
        V(lambda e: e.tensor_copy(out=masks_b[:], in_=masks_f), [rg["masks_f"]], [rg["masks_b"]])
        V(lambda e: e.memset(ones_f[:], 1.0), [], [rg["ones_f"]])
        V(lambda e: e.memset(vaug[:], 1.0), [], rg_v)
        V(lambda e: e.memset(pbuf[:], 0.0), [], [rg["pbuf"]])
        rg_cst = R("cst")
        V(lambda e: e.memset(cst[:, 0:1], EPS), [], [rg_cst])
        V(lambda e: e.memset(cst[:, 1:2], math.pi / 2), [], [rg_cst])
        if 'i' not in SKIP:
          G(lambda e: e.iota(iota16[:], pattern=[[1, 16]], base=0, channel_multiplier=0, allow_small_or_imprecise_dtypes=True),
            [], [rg["iota16"]])

        if BF16_TABLE:
            stg = [ring_bf[:, 4 + 2 * k:6 + 2 * k, :] for k in range(2)]
            rg_stg = [R("ring", 16384 + 8192 * k, 16384 + 8192 * (k + 1)) for k in range(2)]
            cv_stores = []
            NCH = NEXP // 256

            NWQ = D // 256
            def cv_src_dst(c):
                if c < NWQ:
                    return wq[256 * c:256 * (c + 1), :], wqb[256 * c:256 * (c + 1), :]
                c2 = c - NWQ
                return peer_uv[256 * c2:256 * (c2 + 1), :], uvb[256 * c2:256 * (c2 + 1), :]

            def cv_load(c):
                src = cv_src_dst(c)[0].rearrange("(p j) n -> p j n", j=2)
                P.dma("gpsimd", lambda e, c=c, src=src: e.dma_start(out=stg[c % 2], in_=src), sem_cvl[c % 2], writes=[rg_stg[c % 2]])

            def cv_store(c):
                dst = cv_src_dst(c)[1].rearrange("(p j) n -> p j n", j=2)
                cv_stores.append(P.dma("gpsimd", lambda e, c=c, dst=dst: e.dma_start(out=dst, in_=stg[c % 2]), sem_cvs[c % 2], reads=[rg_stg[c % 2]]))

            NCH = NCH + NWQ
            cv_load(0)
            for c in range(1, NCH):
                cv_load(c)
                cv_store(c - 1)
            cv_store(NCH - 1)
            rg_wqb.w = cv_stores[NWQ - 1]
            rg_wqb2.w = cv_stores[NWQ - 2]
            rg_uvb.w = cv_stores[-1]
            rg_uvb2.w = cv_stores[-2]
        V(lambda e: e.tensor_copy(out=posf[:], in_=posi[:]), [rg["posi"]], [rg["posf"]])
        A(lambda e: e.activation(out=cact[:], in_=ccol[:], func=AF.Silu), [rg["ccol"]], [rg["cact"]])
        A(lambda e: e.activation(out=esink[:], in_=sinkb[:], func=AF.Exp), [rg["sinkb"]], [rg["esink"]])
        V(lambda e: e.tensor_scalar(out=qkg[:, 0:512], in0=qkg[:, 0:512], scalar1=0.125, scalar2=None, op0=ALU.mult),
          [rg["qkg"]], [rg["qkg"]])
        cact_rep = BX2[:, 0, :].rearrange("p (k m) -> p k m", m=128)
        V(lambda e: e.tensor_copy(out=cact_rep, in_=cact[:].unsqueeze(2).to_broadcast([128, 8, 128])),
          [rg["cact"]], [rg_BX[0]])

        C1, C2, C3 = _split_2pi()
        MAGIC = 12582912.0
        PI_LO = 3.1415925
        ang = S8A[:, 0:NTE * 32].rearrange("p (i j) -> p i j", j=32)
        nn = S8A[:, 1024:1024 + NTE * 32].rearrange("p (i j) -> p i j", j=32)
        rg_ang, rg_nn = R("S8A", 0, 4096), R("S8A", 4096, 8192)
        V(lambda e: e.tensor_tensor(out=ang, in0=posf[:].unsqueeze(2).to_broadcast([128, NTE, 32]),
                                    in1=invf[:].unsqueeze(1).to_broadcast([128, NTE, 32]), op=ALU.mult),
          [rg["posf"], rg["invf"]], [rg_ang])
        V(lambda e: e.tensor_scalar(out=nn, in0=ang, scalar1=1.0 / (2 * math.pi), scalar2=MAGIC, op0=ALU.mult, op1=ALU.add),
          [rg_ang], [rg_nn])
        V(lambda e: e.tensor_scalar(out=nn, in0=nn, scalar1=MAGIC, scalar2=None, op0=ALU.subtract), [rg_nn], [rg_nn])
        for cc in (C1, C2, C3):
            V(lambda e, cc=cc: e.scalar_tensor_tensor(out=ang, in0=nn, scalar=-cc, in1=ang, op0=ALU.mult, op1=ALU.add),
              [rg_nn, rg_ang], [rg_ang])
        V(lambda e: e.tensor_scalar(out=ang, in0=ang, scalar1=PI_LO, scalar2=-PI_LO, op0=ALU.min, op1=ALU.max), [rg_ang], [rg_ang])
        A(lambda e: e.activation(out=sin_t[:], in_=ang, func=AF.Sin), [rg_ang], [rg["sin_t"]])
        A(lambda e: e.activation(out=nn, in_=ang, func=AF.Abs), [rg_ang], [rg_nn])
        A(lambda e: e.activation(out=cos_t[:], in_=nn, func=AF.Sin, scale=-1.0, bias=cst[:, 1:2]), [rg_nn, rg_cst], [rg["cos_t"]])

        ada_r = ada_w.rearrange("(k p) n -> p k n", p=128)
        chunkbuf = [ring[:, 0:2, :].rearrange("p s (k n) -> p (s k) n", n=256) if False else None, None]
        chunkbuf = [ring[:, 2 * s:2 * s + 2, :].rearrange("p s (k n) -> p (s k) n", n=256) for s in range(2)]
        rg_chunk = [R("ring", 8192 * s, 8192 * (s + 1)) for s in range(2)]
        for c in range(24 if 'a' not in SKIP else 0):
            s = c % 2
            o1 = P.dma("sync", lambda e, c=c, s=s: e.dma_start(out=chunkbuf[s], in_=ada_r[:, :, 256 * c:256 * (c + 1)]),
                       sem_ada[s], writes=[rg_chunk[s]])
            o2 = P.dma("sync", lambda e, c=c, s=s: e.dma_start(out=adab[0:1, s, :], in_=ada_b[:, 256 * c:256 * (c + 1)]),
                       sem_ada[s], writes=[rg_adab[s]])
            o1.dval = o2.dval
            pb = bank(s, 256)
            for kt in range(8):
                T(lambda e, kt=kt, s=s, pb=pb: e.matmul(pb, lhsT=cact_rep[:, kt, :], rhs=chunkbuf[s][:, kt, :], start=(kt == 0), stop=False),
                  [rg_BX[0], rg_chunk[s]], [rg_ps[s]])
            T(lambda e, s=s, pb=pb: e.matmul(pb, lhsT=ones_f[0:1, :], rhs=adab[0:1, s, :], start=False, stop=True),
              [rg["ones_f"], rg_adab[s]], [rg_ps[s]])
            sec, off = c // 4, (c % 4) * 256
            if sec in (1, 4):
                k = 4 if sec == 1 else 5
                V(lambda e, k=k, off=off, pb=pb: e.scalar_tensor_tensor(out=mod6[:, k, off:off + 256], in0=pb, scalar=1.0,
                                                                         in1=mod6[:, k, off:off + 256], op0=ALU.add, op1=ALU.mult),
                  [rg_ps[s], rg_mod[k]], [rg_mod[k]])
            else:
                k = {0: 0, 2: 1, 3: 2, 5: 3}[sec]
                A(lambda e, k=k, off=off, pb=pb: e.copy(out=mod6[:, k, off:off + 256], in_=pb), [rg_ps[s]], [rg_mod[k]])
        shift1, gate1, shift2, gate2, A1, A2 = [mod6[:, k, :] for k in range(6)]
        rg_shift1, rg_gate1, rg_shift2, rg_gate2, rg_A1, rg_A2 = rg_mod
        setup_done = V(lambda e: e.memset(st[:, 60:61], 0.0), [rg_mod[k] for k in range(6)], [R("st", 240, 244)])


        Region._all["ring"] = list(rg_ring)
        if DEBUG:
            P.dma("sync", lambda e: e.dma_start(out=dbg["mod"], in_=mod6[:]), sem_dbg, reads=rg_mod)
            P.dma("sync", lambda e: e.dma_start(out=dbg["cs"][:, 0], in_=cos_t[:]), sem_dbg, reads=[rg["cos_t"]])
            P.dma("sync", lambda e: e.dma_start(out=dbg["cs"][:, 1], in_=sin_t[:]), sem_dbg, reads=[rg["sin_t"]])

        s_ss, s_rs, s_rr = ST("ss", 0, 1), ST("rs", 1, 2), ST("rr", 2, 3)
        s_ss10, s_rs10, s_rr10 = ST("ss10", 4, 14), ST("rs10", 14, 24), ST("rr10", 24, 34)
        s_den, s_rden = ST("den", 34, 42), ST("rden", 42, 50)
        s_sa, s_rsa, s_ra = ST("sa", 50, 51), ST("rsa", 51, 52), ST("ra", 52, 53)
        s_rsc, s_rc = ST("rsc", 53, 54), ST("rc", 54, 55)
        s_sm, s_rsm = ST("sm", 40, 48), ST("rsm", 48, 56)

        eps_ap = cst[:, 0:1]

        def rms_stats(src_ap, src_rg, scratch_ap, scratch_rg, n, s_sum, s_sq, s_r):
            A(lambda e: e.activation(out=scratch_ap, in_=src_ap, func=AF.Square, accum_out=s_sum[1]),
              [src_rg], [scratch_rg, s_sum[0]])
            A(lambda e: e.activation(out=s_sq[1], in_=s_sum[1], func=AF.Sqrt, scale=1.0 / n, bias=eps_ap),
              [s_sum[0], rg_cst], [s_sq[0]])
            V(lambda e: e.reciprocal(out=s_r[1], in_=s_sq[1]), [s_sq[0]], [s_r[0]])

        gcount = [0]
        vcount = [0]

        table = uvb if BF16_TABLE else peer_uv

        def gather_uv(slot, ib):
            r = gcount[0] % NR
            gcount[0] += 1
            P.dma("gpsimd", lambda e, r=r, slot=slot, ib=ib: e.indirect_dma_start(
                out=ring_bf[:, r, :], out_offset=None, in_=table,
                in_offset=bass.IndirectOffsetOnAxis(ap=idxi2[:, ib, slot:slot + 1], axis=0)),
                sem_ring[r], reads=[rg_idxi[ib], rg_uvb, rg_uvb2], writes=[rg_ring[r]], extra=[setup_done])
            return r

        def stage_A(i):
            cur, prv = i % 2, (i - 1) % 2
            xb = i % 2
            BX = BX2[:, xb, :]
            rBX = rg_BX[xb]
            P.dma("sync", lambda e, i=i: e.dma_start(out=BX, in_=xh[128 * i:128 * (i + 1), :]), sem_x[xb], writes=[rBX])
            rms_stats(BX, rBX, BY[:], rg["BY"], D, s_ss, s_rs, s_rr)
            V(lambda e: e.scalar_tensor_tensor(out=BY[:], in0=BX, scalar=s_rr[1], in1=A1, op0=ALU.mult, op1=ALU.mult),
              [rBX, s_rr[0], rg_A1], [rg["BY"]])
            BH, rBH = BH2[:, i % 2, :], rg_BH[i % 2]
            V(lambda e: e.tensor_tensor(out=BH, in0=BY[:], in1=shift1, op=ALU.add), [rg["BY"], rg_shift1], [rBH])
            yield
            p0 = bank_bf(0).rearrange("p (k t) -> p k t", t=128)
            for kt in range(8):
                T(lambda e, kt=kt: e.transpose(out=p0[:, kt, :], in_=BH[:, 128 * kt:128 * (kt + 1)], identity=ident_b[:]),
                  [rBH, rg["ident_b"]], [rg_ps[0]])
            A(lambda e: e.copy(out=BHT[:], in_=p0[:, 0:8, :]), [rg_ps[0]], [rg["BHT"]])
            yield
            for c in range(2):
                for kt in range(8):
                    T(lambda e, c=c, kt=kt: e.matmul(bank(1 + c, 384), lhsT=BHT[:, kt, :], rhs=w_in_sb[:, kt, 384 * c:384 * (c + 1)],
                                                     start=(kt == 0), stop=(kt == 7)),
                      [rg["BHT"], rg["w_in"]], [rg_ps[1 + c]])
            yield
            for nt_ in range(12):
                for kt in range(8):
                    T(lambda e, nt_=nt_, kt=kt: e.matmul(PS[:, 1536 + 128 * nt_:1536 + 128 * (nt_ + 1)],
                                                         lhsT=w_in_sb[:, kt, 768 + 128 * nt_:768 + 128 * (nt_ + 1)], rhs=BHT[:, kt, :],
                                                         start=(kt == 0), stop=(kt == 7)),
                      [rg["BHT"], rg["w_in"]], [rg_ps[3 + nt_ // 4]])
                if nt_ % 3 == 2:
                    yield
            A(lambda e: e.copy(out=qk_v[:, 0:384], in_=bank(1, 384)), [rg_ps[1]], [rg_qk])
            A(lambda e: e.copy(out=qk_v[:, 384:640], in_=bank(2, 256)), [rg_ps[2]], [rg_qk])
            A(lambda e, cur=cur: e.copy(out=vaug[:, cur, :, 0:64], in_=bank(2, 128, 256).rearrange("p (j d) -> p j d", d=64)),
              [rg_ps[2]], [rg_v[cur]])
            V(lambda e: e.tensor_tensor(out=sq_v, in0=qk_v, in1=qk_v, op=ALU.mult), [rg_qk], [rg_sq])
            V(lambda e: e.tensor_reduce(out=s_ss10[1], in_=sq_v.rearrange("p (h d) -> p h d", d=64), axis=AX.X, op=ALU.add),
              [rg_sq], [s_ss10[0]])
            A(lambda e: e.activation(out=s_rs10[1], in_=s_ss10[1], func=AF.Sqrt, scale=1.0 / 64, bias=eps_ap),
              [s_ss10[0], rg_cst], [s_rs10[0]])
            V(lambda e: e.reciprocal(out=s_rr10[1], in_=s_rs10[1]), [s_rs10[0]], [s_rr10[0]])
            V(lambda e: e.tensor_tensor(out=tmp_v.rearrange("p (h d) -> p h d", d=64), in0=qk_v.rearrange("p (h d) -> p h d", d=64),
                                        in1=s_rr10[1].unsqueeze(2).to_broadcast([128, 10, 64]), op=ALU.mult),
              [rg_qk, s_rr10[0]], [rg_tmp])
            V(lambda e: e.tensor_tensor(out=qk_v, in0=tmp_v, in1=qkg[:], op=ALU.mult), [rg_tmp, rg["qkg"]], [rg_qk])
            yield
            q4 = lambda ap: ap.rearrange("p (h two d) -> p h two d", two=2, d=32)
            V(lambda e, i=i: e.tensor_tensor(out=q4(sq_v), in0=q4(qk_v),
                                             in1=cos_t[:, i, :].unsqueeze(1).unsqueeze(1).to_broadcast([128, 10, 2, 32]), op=ALU.mult),
              [rg_qk, rg["cos_t"]], [rg_sq])
            V(lambda e, i=i: e.tensor_tensor(out=q4(tmp_v), in0=q4(qk_v),
                                             in1=sin_t[:, i, :].unsqueeze(1).unsqueeze(1).to_broadcast([128, 10, 2, 32]), op=ALU.mult),
              [rg_qk, rg["sin_t"]], [rg_tmp])
            V(lambda e: e.tensor_tensor(out=q4(qr_v)[:, :, 0, :], in0=q4(sq_v)[:, :, 0, :], in1=q4(tmp_v)[:, :, 1, :], op=ALU.subtract),
              [rg_sq, rg_tmp], [rg_qr])
            V(lambda e: e.tensor_tensor(out=q4(qr_v)[:, :, 1, :], in0=q4(sq_v)[:, :, 1, :], in1=q4(tmp_v)[:, :, 0, :], op=ALU.add),
              [rg_sq, rg_tmp], [rg_qr])
            yield
            cb_ps = PS[:, 1536:2048].rearrange("p (c t) -> p c t", t=128)
            cc_ps = PS[:, 2048:2560].rearrange("p (c t) -> p c t", t=128)
            cu_ps = PS[:, 2560:3072].rearrange("p (c t) -> p c t", t=128)
            cu3 = cu_v.rearrange("p (c t) -> p c t", t=128)
            c13 = c1_v.rearrange("p (c t) -> p c t", t=128)
            cv3 = cv_v.rearrange("p (c t) -> p c t", t=128)
            A(lambda e: e.copy(out=cu3, in_=cu_ps), [rg_ps[5]], [rg_cu])
            V(lambda e: e.tensor_copy(out=pbuf[:, :, 0:2], in_=pbuf[:, :, 128:130]), [rg["pbuf"]], [rg["pbuf"]])
            V(lambda e: e.tensor_tensor(out=pbuf[:, :, 2:130], in0=cc_ps, in1=cu3, op=ALU.mult), [rg_ps[4], rg_cu], [rg["pbuf"]])
            if i == 0:
                V(lambda e: e.tensor_scalar(out=pbuf[:, :, 2:130], in0=pbuf[:, :, 2:130], scalar1=hflag[:, 0:1], scalar2=None, op0=ALU.mult),
                  [rg["pbuf"], rg["hflag"]], [rg["pbuf"]])
            p4q = bank_bf(4).rearrange("p (h t) -> p h t", t=128)
            for h in range(8):
                T(lambda e, h=h: e.transpose(out=p4q[0:64, h, :], in_=qr_v[:, 64 * h:64 * (h + 1)], identity=ident_b[:]),
                  [rg_qr, rg["ident_b"]], [rg_ps[4]])
            for j in range(2):
                T(lambda e, j=j: e.transpose(out=p0[0:64, j, :], in_=qr_v[:, 512 + 64 * j:512 + 64 * (j + 1)], identity=ident_b[:]),
                  [rg_qr, rg["ident_b"]], [rg_ps[0]])
            A(lambda e: e.copy(out=qT_v, in_=p4q[0:64, 0:8, :]), [rg_ps[4]], [rg_qT])
            A(lambda e, cur=cur: e.copy(out=kT[:, cur, :, :], in_=p0[0:64, 0:2, :]), [rg_ps[0]], [rg_kT[cur]])
            yield
            if i == 0:
                return
            o_all = PS[:, 512:1536].rearrange("p (b r) -> p b r", r=512)[:, :, 0:260].rearrange("p b (g e) -> p b g e", e=65)
            for j in range(2):
                for wi, (slot, mi) in enumerate([(prv, 2 if i == 1 else 1), (cur, 0)]):
                    bk = 4 if wi == 0 else 5
                    T(lambda e, j=j, slot=slot, bk=bk: e.matmul(bank(bk), lhsT=kT[:, slot, j, :], rhs=qT_v[:, 4 * j:4 * j + 4, :],
                                                                start=True, stop=True),
                      [rg_kT[slot], rg_qT], [rg_ps[bk]])
                    A(lambda e, wi=wi, bk=bk: e.activation(out=pT_v[wi], in_=bank(bk), func=AF.Exp), [rg_ps[bk]], [rg_pT[wi]])
                    V(lambda e, wi=wi, mi=mi: e.tensor_tensor(out=pT_v[wi].rearrange("p (g t) -> p g t", t=128),
                                                              in0=pT_v[wi].rearrange("p (g t) -> p g t", t=128),
                                                              in1=masks_b[:, mi, :].unsqueeze(1).to_broadcast([128, 4, 128]), op=ALU.mult),
                      [rg_pT[wi], rg["masks_b"]], [rg_pT[wi]])
                for g in range(4):
                    for wi, slot in enumerate([prv, cur]):
                        T(lambda e, j=j, g=g, wi=wi, slot=slot: e.matmul(PS[:, 512 * (1 + j) + 65 * g:512 * (1 + j) + 65 * (g + 1)],
                                                                         lhsT=pT_v[wi][:, 128 * g:128 * (g + 1)], rhs=vaug[:, slot, j, :],
                                                                         start=(wi == 0), stop=(wi == 1)),
                          [rg_pT[wi], rg_v[slot]], [rg_ps[1 + j]])
                yield
            V(lambda e: e.tensor_tensor(out=s_den[1].rearrange("p (b g) -> p b g", g=4), in0=o_all[:, :, :, 64],
                                        in1=esink[:].rearrange("p (b g) -> p b g", g=4), op=ALU.add),
              [rg_ps[1], rg_ps[2], rg["esink"]], [s_den[0]])
            V(lambda e: e.reciprocal(out=s_rden[1], in_=s_den[1]), [s_den[0]], [s_rden[0]])
            V(lambda e: e.tensor_tensor(out=attn_v.rearrange("p (b g d) -> p b g d", g=4, d=64), in0=o_all[:, :, :, 0:64],
                                        in1=s_rden[1].rearrange("p (b g) -> p b g", g=4).unsqueeze(3).to_broadcast([128, 2, 4, 64]), op=ALU.mult),
              [rg_ps[1], rg_ps[2], s_rden[0]], [rg_attn])
            if DEBUG:
                P.dma("sync", lambda e, i=i: e.dma_start(out=dbg["attn"][128 * (i - 1):128 * i, :], in_=attn_v), sem_dbg, reads=[rg_attn])
            rms_stats(attn_v, rg_attn, c1_v, rg_c1, 512, s_sa, s_rsa, s_ra)
            V(lambda e: e.scalar_tensor_tensor(out=attn_n_v, in0=attn_v, scalar=s_ra[1], in1=attn_g[:], op0=ALU.mult, op1=ALU.mult),
              [rg_attn, s_ra[0], rg["attn_g"]], [rg_attn_n])
            for k in range(4):
                T(lambda e, k=k: e.transpose(out=p0[:, k, :], in_=attn_n_v[:, 128 * k:128 * (k + 1)], identity=ident_b[:]),
                  [rg_attn_n, rg["ident_b"]], [rg_ps[0]])
            A(lambda e: e.copy(out=attnT[:], in_=p0[:, 0:4, :]), [rg_ps[0]], [rg["attnT"]])
            yield
            for ct in range(4):
                V(lambda e, ct=ct: e.tensor_scalar(out=c13[:, ct, :], in0=pbuf[:, ct, 0:128], scalar1=convw[:, ct, 0:1], scalar2=None, op0=ALU.mult),
                  [rg["pbuf"], rg["convw"]], [rg_c1])
            for kk in (1, 2):
                for ct in range(4):
                    V(lambda e, ct=ct, kk=kk: e.scalar_tensor_tensor(out=c13[:, ct, :], in0=pbuf[:, ct, kk:kk + 128], scalar=convw[:, ct, kk:kk + 1],
                                                                      in1=c13[:, ct, :], op0=ALU.mult, op1=ALU.add),
                      [rg["pbuf"], rg["convw"], rg_c1], [rg_c1])
            V(lambda e: e.tensor_tensor(out=cv3, in0=cb_ps, in1=c13, op=ALU.mult), [rg_ps[3], rg_c1], [rg_cv])
            V(lambda e: e.tensor_tensor(out=cu3, in0=cv3, in1=cv3, op=ALU.mult), [rg_cv], [rg_cu])
            for ct in range(4):
                T(lambda e, ct=ct: e.matmul(bank(5, 1), lhsT=cu3[:, ct, :], rhs=ones_f[:, 0:1], start=(ct == 0), stop=(ct == 3)),
                  [rg_cu, rg["ones_f"]], [rg_ps[5]])
            A(lambda e: e.activation(out=s_rsc[1], in_=bank(5, 1), func=AF.Sqrt, scale=1.0 / 512, bias=eps_ap), [rg_ps[5], rg_cst], [s_rsc[0]])
            V(lambda e: e.reciprocal(out=s_rc[1], in_=s_rsc[1]), [s_rsc[0]], [s_rc[0]])
            cvT3 = cvT_v.rearrange("p (c t) -> p c t", t=128)
            for ct in range(4):
                V(lambda e, ct=ct: e.tensor_scalar(out=cvT3[:, ct, :], in0=cv3[:, ct, :], scalar1=convg[:, ct:ct + 1], scalar2=None, op0=ALU.mult),
                  [rg_cv, rg["convg"]], [rg_cvT])
            yield
            for c in range(2):
                for k in range(4):
                    T(lambda e, c=c, k=k: e.matmul(bank(1 + c), lhsT=attnT[:, k, :], rhs=w_out_sb[:, k, 512 * c:512 * (c + 1)],
                                                   start=(k == 0), stop=(k == 3)),
                      [rg["attnT"], rg["w_out"]], [rg_ps[1 + c]])
            for c in range(2):
                for ct in range(4):
                    T(lambda e, c=c, ct=ct: e.matmul(bank(3 + c), lhsT=cvT3[:, ct, :], rhs=w_out_sb[:, 4 + ct, 512 * c:512 * (c + 1)],
                                                     start=(ct == 0), stop=(ct == 3)),
                      [rg_cvT, rg["w_out"]], [rg_ps[3 + c]])
            A(lambda e: e.copy(out=BY[:], in_=PS[:, 512:1536]), [rg_ps[1], rg_ps[2]], [rg["BY"]])
            V(lambda e: e.scalar_tensor_tensor(out=BY[:], in0=PS[:, 1536:2560], scalar=s_rc[1], in1=BY[:], op0=ALU.mult, op1=ALU.add),
              [rg_ps[3], rg_ps[4], s_rc[0], rg["BY"]], [rg["BY"]])
            V(lambda e: e.tensor_tensor(out=BY[:], in0=BY[:], in1=gate1, op=ALU.mult), [rg["BY"], rg_gate1], [rg["BY"]])
            V(lambda e: e.tensor_tensor(out=BX, in0=BY[:], in1=BX, op=ALU.add), [rg["BY"], rBX], [rBX])
            row0 = 128 * (i - 1)
            if DEBUG:
                P.dma("sync", lambda e, row0=row0: e.dma_start(out=dbg["x1"][row0:row0 + 128, :], in_=BX), sem_dbg, reads=[rBX])
            yield

        def stage_R(i):
            xb = i % 2
            ib = i % 2
            BX = BX2[:, xb, :]
            rBX = rg_BX[xb]
            row0 = 128 * (i - 1)
            rms_stats(BX, rBX, BY[:], rg["BY"], D, s_ss, s_rs, s_rr)
            V(lambda e: e.scalar_tensor_tensor(out=BY[:], in0=BX, scalar=s_rr[1], in1=A2, op0=ALU.mult, op1=ALU.mult),
              [rBX, s_rr[0], rg_A2], [rg["BY"]])
            V(lambda e: e.tensor_tensor(out=BY[:], in0=BY[:], in1=shift2, op=ALU.add), [rg["BY"], rg_shift2], [rg["BY"]])
            BH, rBH = BH2[:, i % 2, :], rg_BH[i % 2]
            A(lambda e: e.copy(out=BH, in_=BY[:]), [rg["BY"]], [rBH])
            if DEBUG:
                P.dma("sync", lambda e, row0=row0: e.dma_start(out=dbg["h2"][row0:row0 + 128, :], in_=BY[:]), sem_dbg, reads=[rg["BY"]])
            yield
            p0 = bank_bf(0).rearrange("p (k t) -> p k t", t=128)
            for kt in range(8):
                T(lambda e, kt=kt: e.transpose(out=p0[:, kt, :], in_=BH[:, 128 * kt:128 * (kt + 1)], identity=ident_b[:]),
                  [rBH, rg["ident_b"]], [rg_ps[0]])
            A(lambda e: e.copy(out=BHT[:], in_=p0[:, 0:8, :]), [rg_ps[0]], [rg["BHT"]])
            yield
            wqb_r = wqb.rearrange("(kt p) n -> p kt n", p=128)

            def wq_load(c):
                P.dma("sync", lambda e, c=c: e.dma_start(out=wqc[:, c % 2, :, :], in_=wqb_r[:, :, 512 * c:512 * (c + 1)]), sem_wqc[c % 2],
                      reads=[rg_wqb, rg_wqb2], writes=[rg_wqc[c % 2]])

            wq_load(0)
            wq_load(1)
            for blk in range(16):
                c, bl = blk // 4, blk % 4
                for kt in range(8):
                    T(lambda e, blk=blk, kt=kt, c=c, bl=bl: e.matmul(PS[:, 512 + 128 * blk:512 + 128 * (blk + 1)], lhsT=wqc[:, c % 2, kt, 128 * bl:128 * (bl + 1)], rhs=BHT[:, kt, :],
                                                         start=(kt == 0), stop=(kt == 7)),
                      [rg_wqc[c % 2], rg["BHT"]], [rg_ps[1 + blk // 4]])
                if blk % 4 == 3:
                    if c + 2 < 4:
                        wq_load(c + 2)
                    yield
            A(lambda e: e.copy(out=S4B[:, 0:2048], in_=PS[:, 512:2560]), [rg_ps[1], rg_ps[2], rg_ps[3], rg_ps[4]], [rg_qpT])
            for blk in range(16):
                T(lambda e, blk=blk: e.matmul(PS[:, 512 + 128 * blk:512 + 128 * (blk + 1)], lhsT=qpT_v[:, blk, :], rhs=skT_sb[:, blk, :],
                                              start=True, stop=True),
                  [rg_qpT, rg["skT"]], [rg_ps[1 + blk // 4]])
            A(lambda e: e.copy(out=sc_v, in_=PS[:, 512:2560]), [rg_ps[1], rg_ps[2], rg_ps[3], rg_ps[4]], [rg_sc])
            yield
            mr4 = S8B[:, 0:512]
            rg_mr4 = R("S8B", 0, 2048)
            for g4 in range(4):
                blks = range(4 * g4, 4 * g4 + 4)
                sin_ = {b_: sc_v[:, 128 * b_:128 * (b_ + 1)] for b_ in blks}
                v0 = {b_: vals[:, 16 * b_:16 * b_ + 8] for b_ in blks}
                v1 = {b_: vals[:, 16 * b_ + 8:16 * b_ + 16] for b_ in blks}
                i0 = {b_: ids[:, 16 * b_:16 * b_ + 8] for b_ in blks}
                i1 = {b_: ids[:, 16 * b_ + 8:16 * b_ + 16] for b_ in blks}
                m_ = {b_: mr4[:, 128 * (b_ % 4):128 * (b_ % 4 + 1)] for b_ in blks}
                for b_ in blks:
                    V(lambda e, o_=v0[b_], x_=sin_[b_]: e.max(out=o_, in_=x_), [rg_sc], [rg["vals"]])
                for b_ in blks:
                    V(lambda e, o_=i0[b_], m__=v0[b_], x_=sin_[b_]: e.max_index(out=o_, in_max=m__, in_values=x_), [rg_sc, rg["vals"]], [rg["ids"]])
                for b_ in blks:
                    V(lambda e, o_=m_[b_], m__=v0[b_], x_=sin_[b_]: e.match_replace(out=o_, in_to_replace=m__, in_values=x_, imm_value=-1e30),
                      [rg_sc, rg["vals"]], [rg_mr4])
                for b_ in blks:
                    V(lambda e, o_=v1[b_], x_=m_[b_]: e.max(out=o_, in_=x_), [rg_mr4], [rg["vals"]])
                for b_ in blks:
                    V(lambda e, o_=i1[b_], m__=v1[b_], x_=m_[b_]: e.max_index(out=o_, in_max=m__, in_values=x_), [rg_mr4, rg["vals"]], [rg["ids"]])
                yield
            V(lambda e: e.tensor_copy(out=idsf, in_=ids), [rg["ids"]], [rg["idsf"]])
            vals4 = vals.rearrange("p (h s k) -> p h s k", s=2, k=16)
            idsf4 = idsf.rearrange("p (h s k) -> p h s k", s=2, k=16)
            cand4 = cand_v.rearrange("p (h a b) -> p h a b", a=16, b=16)
            V(lambda e: e.tensor_tensor(out=cand4, in0=vals4[:, :, 0, :].unsqueeze(3).to_broadcast([128, 8, 16, 16]),
                                        in1=vals4[:, :, 1, :].unsqueeze(2).to_broadcast([128, 8, 16, 16]), op=ALU.add),
              [rg["vals"]], [rg_cand])
            for h in range(8):
                c_in = cand_v[:, 256 * h:256 * (h + 1)]
                t0, t1 = top[:, 16 * h:16 * h + 8], top[:, 16 * h + 8:16 * h + 16]
                q0, q1 = pos[:, 16 * h:16 * h + 8], pos[:, 16 * h + 8:16 * h + 16]
                V(lambda e, c_in=c_in, t0=t0: e.max(out=t0, in_=c_in), [rg_cand], [rg["top"]])
                V(lambda e, c_in=c_in, t0=t0, q0=q0: e.max_index(out=q0, in_max=t0, in_values=c_in), [rg_cand, rg["top"]], [rg["pos"]])
                V(lambda e, c_in=c_in, t0=t0: e.match_replace(out=mr, in_to_replace=t0, in_values=c_in, imm_value=-1e30),
                  [rg_cand, rg["top"]], [rg["mr"]])
                V(lambda e, t1=t1: e.max(out=t1, in_=mr), [rg["mr"]], [rg["top"]])
                V(lambda e, t1=t1, q1=q1: e.max_index(out=q1, in_max=t1, in_values=mr), [rg["mr"], rg["top"]], [rg["pos"]])
                if h % 4 == 3:
                    yield
            V(lambda e: e.tensor_single_scalar(out=k1u, in_=pos[:], scalar=4, op=ALU.logical_shift_right), [rg["pos"]], [rg_k1u])
            V(lambda e: e.tensor_single_scalar(out=k2u, in_=pos[:], scalar=15, op=ALU.bitwise_and), [rg["pos"]], [rg_k2u])
            V(lambda e: e.tensor_copy(out=k1f, in_=k1u), [rg_k1u], [rg_k1f])
            V(lambda e: e.tensor_copy(out=k2f, in_=k2u), [rg_k2u], [rg_k2f])
            oh4 = oh_v.rearrange("p (h j k) -> p h j k", j=16, k=16)
            for side, kf, rk in ((0, k1f, rg_k1f), (1, k2f, rg_k2f)):
                V(lambda e, kf=kf: e.tensor_tensor(out=oh4, in0=kf.rearrange("p (h j) -> p h j", j=16).unsqueeze(3).to_broadcast([128, 8, 16, 16]),
                                                   in1=iota16[:].unsqueeze(1).unsqueeze(1).to_broadcast([128, 8, 16, 16]), op=ALU.is_equal),
                  [rk, rg["iota16"]], [rg_oh])
                V(lambda e, side=side: e.tensor_tensor(out=oh4, in0=oh4, in1=idsf4[:, :, side, :].unsqueeze(2).to_broadcast([128, 8, 16, 16]), op=ALU.mult),
                  [rg_oh, rg["idsf"]], [rg_oh])
                V(lambda e, side=side: e.tensor_reduce(out=isel[:, side, :], in_=oh_v.rearrange("p (a k) -> p a k", k=16), axis=AX.X, op=ALU.add),
                  [rg_oh], [rg_isel])
            yield
            V(lambda e: e.scalar_tensor_tensor(out=idxf, in0=isel[:, 0, :], scalar=128.0, in1=isel[:, 1, :], op0=ALU.mult, op1=ALU.add),
              [rg_isel], [rg_idxf])
            V(lambda e, ib=ib: e.tensor_copy(out=idxi2[:, ib, :], in_=idxf), [rg_idxf], [rg_idxi[ib]])
            top3 = top[:].rearrange("p (h j) -> p h j", j=16)
            V(lambda e: e.tensor_tensor(out=esm[:].rearrange("p (h j) -> p h j", j=16), in0=top3,
                                        in1=top3[:, :, 0:1].to_broadcast([128, 8, 16]), op=ALU.subtract),
              [rg["top"]], [rg["esm"]])
            A(lambda e: e.activation(out=esm[:], in_=esm[:], func=AF.Exp), [rg["esm"]], [rg["esm"]])
            V(lambda e: e.tensor_reduce(out=s_sm[1], in_=esm[:].rearrange("p (h j) -> p h j", j=16), axis=AX.X, op=ALU.add),
              [rg["esm"]], [s_sm[0]])
            V(lambda e: e.reciprocal(out=s_rsm[1], in_=s_sm[1]), [s_sm[0]], [s_rsm[0]])
            V(lambda e, ib=ib: e.tensor_tensor(out=gsm2[:, ib, :].rearrange("p (h j) -> p h j", j=16), in0=esm[:].rearrange("p (h j) -> p h j", j=16),
                                        in1=s_rsm[1].unsqueeze(2).to_broadcast([128, 8, 16]), op=ALU.mult),
              [rg["esm"], s_rsm[0]], [rg_gsm[ib]])
            if DEBUG:
                P.dma("sync", lambda e, row0=row0, ib=ib: e.dma_start(out=dbg["idx"][row0:row0 + 128, :], in_=idxi2[:, ib, :]), sem_dbg, reads=[rg_idxi[ib]])
                P.dma("sync", lambda e, row0=row0, ib=ib: e.dma_start(out=dbg["g"][row0:row0 + 128, :], in_=gsm2[:, ib, :]), sem_dbg, reads=[rg_gsm[ib]])
            yield

        def drain(gen):
            if gen is None:
                return
            for _ in gen:
                pass

        def step(gen):
            if gen is None:
                return None
            try:
                next(gen)
                return gen
            except StopIteration:
                return None

        def chain(*gens):
            for g_ in gens:
                for _ in g_:
                    yield

        def stage_UV(i, other):
            ib = i % 2
            BH, rBH = BH2[:, ib, :], rg_BH[ib]
            def tail(slot, r):
                db = slot % 4
                V(lambda e, slot=slot, db=db, ib=ib: e.tensor_scalar(out=dg[:, db, :], in0=ident_b[:], scalar1=gav[:, slot:slot + 1],
                                                                      scalar2=gsm2[:, ib, slot:slot + 1], op0=ALU.mult, op1=ALU.mult),
                  [rg_gav[slot], rg_gsm[ib], rg["ident_b"]], [rg_dg[db]])
                for hf in range(2):
                    T(lambda e, r=r, db=db, hf=hf, slot=slot: e.matmul(PS[:, 3072 + 512 * hf:3072 + 512 * (hf + 1)], lhsT=dg[:, db, :],
                                                                        rhs=ring_bf[:, r, 1024 + 512 * hf:1024 + 512 * (hf + 1)],
                                                                        start=(slot == 0), stop=(slot == 127)),
                      [rg_dg[db], rg_ring[r]], [rg_ps[6 + hf]])

            LAG = 2
            pend = []
            for slot in range(128):
                r = gather_uv(slot, ib)
                V(lambda e, r=r: e.tensor_tensor(out=ring_bf[:, r, 0:1024], in0=ring_bf[:, r, 0:1024], in1=BH, op=ALU.mult),
                  [rg_ring[r], rBH], [rg_ring[r]])
                A(lambda e, r=r, slot=slot: e.activation(out=ring_bf[:, r, 0:1024], in_=ring_bf[:, r, 0:1024], func=AF.Copy, accum_out=av[:, slot:slot + 1]),
                  [rg_ring[r]], [rg_ring[r], rg_av[slot]])
                A(lambda e, slot=slot: e.activation(out=gav[:, slot:slot + 1], in_=av[:, slot:slot + 1], func=AF.Gelu), [rg_av[slot]], [rg_gav[slot]])
                pend.append((slot, r))
                if len(pend) > LAG:
                    tail(*pend.pop(0))
                if slot % 2 == 1:
                    other = step(other)
            while pend:
                tail(*pend.pop(0))
            drain(other)

        def stage_fin(i):
            xb = i % 2
            BX = BX2[:, xb, :]
            rBX = rg_BX[xb]
            row0 = 128 * (i - 1)
            fin_v = S8B[:, 0:1024]
            V(lambda e: e.tensor_tensor(out=fin_v, in0=PS[:, 3072:4096], in1=gate2, op=ALU.mult), [rg_ps[6], rg_ps[7], rg_gate2], [rg_fin])
            V(lambda e: e.tensor_tensor(out=BX, in0=fin_v, in1=BX, op=ALU.add), [rg_fin, rBX], [rBX])
            P.dma("sync", lambda e, row0=row0: e.dma_start(out=out[row0:row0 + 128, :], in_=BX), sem_o[xb], reads=[rBX])

        if STAGE >= 1:
            drain(stage_A(0))
            last = N_TILES_RUN
            drain(stage_A(1))
            if STAGE >= 2:
                drain(stage_R(1))
            for i in range(1, last + 1):
                nxt = None
                if i < last:
                    nxt = chain(stage_A(i + 1), stage_R(i + 1)) if STAGE >= 2 else stage_A(i + 1)
                if STAGE >= 3:
                    stage_UV(i, nxt)
                    stage_fin(i)
                else:
                    drain(nxt)

        def tail_waits():
            w = [(so, P.dcnt.get(so, 0)) for so in sem_o] + [(sw, P.dcnt.get(sw, 0)) for sw in sem_w] + [(sv, P.dcnt.get(sv, 0)) for sv in sem_cvs]
            if DEBUG:
                w.append((sem_dbg, P.dcnt.get(sem_dbg, 0)))
            return [x for x in w if x[1] > 0]

        P.emit(block, tail_waits)
    return nc


def make_in_maps(x, c, positions, ada_w, ada_b, norm1_g, w_in, q_norm_g, k_norm_g, sinks, conv_w,
                 attn_out_g, conv_out_g, w_out, norm2_g, peer_wq, peer_subkeys, peer_u, peer_v):
    f = lambda a: np.ascontiguousarray(np.asarray(a), dtype=np.float32)
    x = f(x); c = f(c); positions = np.asarray(positions).astype(np.int32)
    S = x.shape[1]
    cpb = N_CORES // x.shape[0]
    assert S // cpb == TOK
    crow = np.zeros((1, 3584), np.float32)
    crow[0, 0:1024] = f(norm1_g)[0]
    crow[0, 1024:2048] = f(norm2_g)[0]
    crow[0, 2048:2560] = f(attn_out_g)[0]
    crow[0, 2560:3072] = np.tile(f(q_norm_g)[0], 8)
    crow[0, 3072:3200] = np.tile(f(k_norm_g)[0], 2)
    crow[0, 3200:3208] = f(sinks)[0]
    crow[0, 3208:3240] = (np.float32(10000.0) ** (-np.arange(0, 64, 2, dtype=np.float32) / np.float32(64))).astype(np.float32)
    convw = np.ascontiguousarray(f(conv_w)[0].T.reshape(4, 128, 3).transpose(1, 0, 2))
    convg = np.ascontiguousarray(f(conv_out_g)[0].reshape(4, 128).T)
    skT = np.ascontiguousarray(f(peer_subkeys)[0].reshape(16, 128, 128).transpose(2, 0, 1))
    ident = np.eye(128, dtype=np.float32)
    tk = np.arange(128)[:, None]
    tq = np.arange(128)[None, :]
    m_cur = (tk <= tq).astype(np.float32)
    m_prev = (tk > tq).astype(np.float32)
    shared = dict(ada_w=f(ada_w)[0], ada_b=f(ada_b), crow=crow, convw=convw, convg=convg, w_in=f(w_in)[0], w_out=f(w_out)[0],
                  wq=f(peer_wq)[0], skT=skT, peer_uv=np.ascontiguousarray(np.concatenate([f(peer_u)[0], f(peer_v)[0]], axis=1)), ident=ident)
    maps = []
    for core in range(N_CORES):
        b, j = core // cpb, core % cpb
        t0 = j * TOK
        xh = np.zeros((NTE * 128, D), np.float32)
        ph = np.zeros((NTE * 128,), np.int32)
        xh[128:] = x[b, t0:t0 + TOK]
        ph[128:] = positions[b, t0:t0 + TOK]
        first = (j == 0)
        if not first:
            xh[:128] = x[b, t0 - 128:t0]
            ph[:128] = positions[b, t0 - 128:t0]
        masks = np.stack([m_cur, m_prev, np.zeros_like(m_prev) if first else m_prev], axis=1)
        m = dict(shared)
        m.update(xh=xh, posi=np.ascontiguousarray(ph.reshape(NTE, 128).T),
                 ccol=np.ascontiguousarray(c[b].reshape(8, 128).T),
                 hflag=np.full((128, 1), 0.0 if first else 1.0, np.float32),
                 masks=np.ascontiguousarray(masks.astype(np.float32)))
        maps.append(m)
    return maps


_NC_CACHE = {}


def kernel(**inputs):
    maps = make_in_maps(**inputs)
    key = (DEBUG, N_TILES_RUN, STAGE, NEXP, BF16_TABLE)
    if key not in _NC_CACHE:
        _NC_CACHE[key] = build_nc()
    nc = _NC_CACHE[key]
    ncores = getattr(kernel, 'ncores', N_CORES)
    if NEXP != 16384:
        for m in maps:
            m['peer_uv'] = m['peer_uv'][:NEXP]
    res = run_bass_kernel_spmd(nc, maps[:ncores], core_ids=list(range(ncores)))
    if ncores != N_CORES:
        kernel.last = res.results
        return None
    B, S = np.asarray(inputs["x"]).shape[:2]
    outp = np.concatenate([r["out"] for r in res.results], axis=0).reshape(B, S, D).astype(np.float32)
    if DEBUG:
        kernel.last = res.results
    return outp
```
